# Optimizing a Trainium2 kernel written in Bass

```python
import jax, jax.numpy as jnp
from jax import lax
import numpy as np

D_MODEL = 1024
BATCH = 32
SEQ = 256
DEPTH = 2
DEC_BATCH = 8
DEC_SEQ = 2048
PAST_LEN = 512

GRID_W = 64
CHUNK = 128
Q_BLOCK = 128
A_WIDTH = D_MODEL // 2
A_GROUPS = 4
A_GC = A_WIDTH // A_GROUPS
HEAD_DIM = 64
N_Q = (D_MODEL // 2) // HEAD_DIM
N_KV = N_Q // 4
GQA = N_Q // N_KV
Q_W = N_Q * HEAD_DIM
KV_W = N_KV * HEAD_DIM
ROPE_THETA = 10000.0
C_HEAD = 64
C_WIDTH = D_MODEL // 2
C_HEADS = C_WIDTH // C_HEAD
DECAY_LORA = 64
AAA_LORA = 64
GATE_LORA = 128
N_DIR = 2
RWKV_FEAT = 3 * C_WIDTH + DECAY_LORA + AAA_LORA
GN_EPS = 64e-5
N_BRANCH = 3
D_FF = 4 * D_MODEL
ALPHA = (2 * DEPTH) ** 0.25
BETA = (8 * DEPTH) ** -0.25
IN_SIZES = (A_WIDTH, A_WIDTH, Q_W, KV_W, KV_W, 3 * C_WIDTH, N_DIR * (DECAY_LORA + AAA_LORA), GATE_LORA, N_BRANCH * D_MODEL)
IN_COLS = A_WIDTH * 2 + Q_W + 2 * KV_W + 3 * C_WIDTH + N_DIR * (DECAY_LORA + AAA_LORA) + GATE_LORA + N_BRANCH * D_MODEL

kernel_name = 'hybrid_diffusion_gmlp_gqa_rwkv7_step'


def _split_cols(x, sizes):
    out, start = [], 0
    for s in sizes:
        out.append(x[..., start:start + s])
        start += s
    return out


def _layer_norm(x, g, b, eps=1e-5):
    xf = x.astype(jnp.float32)
    mu = jnp.mean(xf, -1, keepdims=True)
    var = jnp.mean(jnp.square(xf - mu), -1, keepdims=True)
    return ((xf - mu) * lax.rsqrt(var + eps)).astype(x.dtype) * g + b


def _rms_norm(x, g, eps=1e-6):
    xf = x.astype(jnp.float32)
    return (xf * lax.rsqrt(jnp.mean(xf * xf, -1, keepdims=True) + eps)).astype(x.dtype) * g


def _axial_rope(x):
    L = x.shape[1]
    rows = L // GRID_W
    row = jnp.repeat(jnp.arange(rows, dtype=jnp.float32), GRID_W)
    col = jnp.tile(jnp.arange(GRID_W, dtype=jnp.float32), rows)
    half = HEAD_DIM // 2
    inv_freq = ROPE_THETA ** (-jnp.arange(0, half, 2, dtype=jnp.float32) / half)
    xf = x.astype(jnp.float32)

    def rot(xp, pos):
        ang = pos[:, None] * inv_freq[None, :]
        cos = jnp.cos(ang)[None, :, None, :]
        sin = jnp.sin(ang)[None, :, None, :]
        x1, x2 = xp[..., :half // 2], xp[..., half // 2:]
        return jnp.concatenate([x1 * cos - x2 * sin, x2 * cos + x1 * sin], -1)

    out = jnp.concatenate([rot(xf[..., :half], row), rot(xf[..., half:], col)], -1)
    return out.astype(x.dtype)


def _blocked_attention(q, k, v):
    B, Lq = q.shape[0], q.shape[1]
    nb = Lq // Q_BLOCK
    qb = jnp.moveaxis(q.reshape(B, nb, Q_BLOCK, N_KV, GQA, HEAD_DIM), 1, 0)
    scale = HEAD_DIM ** -0.5

    def one_block(qblk):
        s = jnp.einsum('bqhgd,bkhd->bhgqk', qblk, k).astype(jnp.float32) * scale
        p = jax.nn.softmax(s, axis=-1).astype(v.dtype)
        return jnp.einsum('bhgqk,bkhd->bqhgd', p, v)

    o = lax.map(one_block, qb)
    return jnp.moveaxis(o, 0, 1).reshape(B, Lq, Q_W)


def _chunk_spatial_gate(u, v, ln_g, ln_b, w_s, b_s):
    B, L, _ = u.shape
    vn = _layer_norm(v, ln_g, ln_b).reshape(B, L // CHUNK, CHUNK, A_GROUPS, A_GC)
    s = jnp.einsum('gpq,bnqgc->bnpgc', w_s, vn) + b_s.T[None, None, :, :, None]
    return u * s.reshape(B, L, A_WIDTH)


def _token_shift(f, mu, reverse):
    zero = jnp.zeros_like(f[:, :1])
    nb = jnp.concatenate([f[:, 1:], zero], 1) if reverse else jnp.concatenate([zero, f[:, :-1]], 1)
    return f + mu * (nb - f)


def _wkv_scan(r, w, k, v, a, b, s0, reverse):
    xs = tuple(jnp.moveaxis(t.astype(jnp.float32), 1, 0) for t in (r, w, k, v, a, b))

    def step(S, xt):
        r_t, w_t, k_t, v_t, a_t, b_t = xt
        sa = jnp.einsum('bhvk,bhk->bhv', S, a_t)
        S = S * w_t[:, :, None, :] + sa[..., None] * b_t[:, :, None, :] + v_t[..., None] * k_t[:, :, None, :]
        return S, jnp.einsum('bhvk,bhk->bhv', S, r_t)

    s_fin, ys = lax.scan(step, s0.astype(jnp.float32), xs, reverse=reverse)
    return jnp.moveaxis(ys, 0, 1).astype(r.dtype), s_fin


def _rwkv_mix(rkv, lora, g_down, s0, P, l):
    B, L, _ = rkv.shape
    hs = lambda t: t.reshape(B, L, C_HEADS, C_HEAD)
    k_k = P['rwkv_k_k'][l].reshape(C_HEADS, C_HEAD)
    k_a = P['rwkv_k_a'][l].reshape(C_HEADS, C_HEAD)
    y_sum, bonus_sum, finals = 0.0, 0.0, []
    for d in range(N_DIR):
        f = _token_shift(jnp.concatenate([rkv, lora[:, :, d]], -1), P['rwkv_mu'][l, d], reverse=(d == 1))
        r, k, v, wd, ad = _split_cols(f, (C_WIDTH, C_WIDTH, C_WIDTH, DECAY_LORA, AAA_LORA))
        w_log = (-jax.nn.softplus(-(P['rwkv_w0'][l, d] + jnp.tanh(wd) @ P['rwkv_w2'][l, d])) - 0.5).astype(jnp.float32)
        decay = hs(jnp.exp(-jnp.exp(w_log)))
        a = hs(jax.nn.sigmoid(P['rwkv_a0'][l, d] + ad @ P['rwkv_a2'][l, d]))
        r, k, v = hs(r), hs(k), hs(v)
        kk = k * k_k
        kk = kk / jnp.maximum(jnp.sqrt(jnp.sum(jnp.square(kk.astype(jnp.float32)), -1, keepdims=True)), 1e-12).astype(kk.dtype)
        k_mod = k * (1 + (a - 1) * k_a)
        y, s_fin = _wkv_scan(r, decay, k_mod, v, -kk, kk * a, s0[:, d], reverse=(d == 1))
        y_sum = y_sum + y
        bonus_sum = bonus_sum + jnp.sum(r * k_mod * P['rwkv_r_k'][l], -1, keepdims=True) * v
        finals.append(s_fin)
    yf = y_sum.astype(jnp.float32)
    mu = jnp.mean(yf, -1, keepdims=True)
    var = jnp.mean(jnp.square(yf - mu), -1, keepdims=True)
    gn = ((yf - mu) * lax.rsqrt(var + GN_EPS)).astype(rkv.dtype).reshape(B, L, C_WIDTH)
    gn = gn * P['rwkv_lnx_g'][l] + P['rwkv_lnx_b'][l]
    g = jax.nn.sigmoid(g_down) @ P['rwkv_g2'][l]
    out = (gn + bonus_sum.reshape(B, L, C_WIDTH)) * g
    return out, jnp.stack(finals, axis=1)


def _trunk_layer(x, cvec, P, l, ctx):
    B, L, _ = x.shape
    latent = ctx is not None
    mod = (jax.nn.silu(cvec) @ P['w_ada'][l] + P['b_ada'][l])[:, None, :]
    sh1, sc1, g1, sh2, sc2, g2 = jnp.split(mod, 6, axis=-1)
    h = x * (1 + sc1) + sh1
    proj = h @ P['w_in'][l]
    uA, vA, q, k, v, rkv, lora, g_down, g_log = _split_cols(proj, IN_SIZES)
    oA = _chunk_spatial_gate(uA, vA, P['sgu_ln_g'][l], P['sgu_ln_b'][l], P['sgu_w'][l], P['sgu_b'][l])
    q = _rms_norm(q.reshape(B, L, N_Q, HEAD_DIM), P['q_norm'][l])
    k = _rms_norm(k.reshape(B, L, N_KV, HEAD_DIM), P['k_norm'][l])
    v = v.reshape(B, L, N_KV, HEAD_DIM)
    if latent:
        ctx_k, ctx_v, ctx_s = ctx
        q_r, k_r = _axial_rope(q), _axial_rope(k)
        oB = _blocked_attention(q_r, jnp.concatenate([ctx_k, k_r], 1), jnp.concatenate([ctx_v, v], 1))
        s0 = ctx_s
    else:
        oB = _blocked_attention(q, k, v)
        s0 = jnp.zeros((B, N_DIR, C_HEADS, C_HEAD, C_HEAD), jnp.float32)
    oC, s_fin = _rwkv_mix(rkv, lora.reshape(B, L, N_DIR, DECAY_LORA + AAA_LORA), g_down, s0, P, l)
    branches = jnp.stack([oA, oB, oC], axis=2)
    p = jnp.einsum('bljc,jcd->bljd', branches, P['w_branch'][l])
    gates = jax.nn.sigmoid(g_log.reshape(B, L, N_BRANCH, D_MODEL))
    mixed = jnp.sum(gates * p, axis=2) @ P['w_out'][l]
    x = _layer_norm(ALPHA * x + g1 * mixed, P['ln1_g'][l], P['ln1_b'][l])
    h = x * (1 + sc2) + sh2
    f = jnp.square(jax.nn.relu(h @ P['w_up'][l])) @ P['w_down'][l]
    x = _layer_norm(ALPHA * x + g2 * f, P['ln2_g'][l], P['ln2_b'][l])
    if latent:
        return x
    return x, (k, v, s_fin.astype(x.dtype))


def setup_inputs(seed: int = 0) -> dict:
    key = jax.random.key(seed)
    ks = iter(jax.random.split(key, 48))
    nrm = lambda shape, s=1.0: s * jax.random.normal(next(ks), shape, jnp.float32)
    return {
        'x_prompt': nrm((BATCH, SEQ, D_MODEL)),
        'x_sample': nrm((DEC_BATCH, DEC_SEQ, D_MODEL)),
        'cache_k': nrm((DEC_BATCH, DEPTH, PAST_LEN, N_KV, HEAD_DIM)),
        'cache_v': nrm((DEC_BATCH, DEPTH, PAST_LEN, N_KV, HEAD_DIM)),
        'state_wkv': nrm((DEC_BATCH, DEPTH, N_DIR, C_HEADS, C_HEAD, C_HEAD), 0.5),
        'c': nrm((DEC_BATCH, D_MODEL)),
        'c_ctx': nrm((D_MODEL,)),
        'w_ada': nrm((DEPTH, D_MODEL, 6 * D_MODEL), 0.5 * D_MODEL ** -0.5),
        'b_ada': nrm((DEPTH, 6 * D_MODEL), 0.01),
        'w_in': nrm((DEPTH, D_MODEL, IN_COLS), D_MODEL ** -0.5),
        'sgu_ln_g': 1.0 + nrm((DEPTH, A_WIDTH), 0.02),
        'sgu_ln_b': nrm((DEPTH, A_WIDTH), 0.02),
        'sgu_w': nrm((DEPTH, A_GROUPS, CHUNK, CHUNK), 0.5 * CHUNK ** -0.5),
        'sgu_b': 1.0 + nrm((DEPTH, A_GROUPS, CHUNK), 0.02),
        'q_norm': 1.0 + nrm((DEPTH, HEAD_DIM), 0.02),
        'k_norm': 1.0 + nrm((DEPTH, HEAD_DIM), 0.02),
        'rwkv_mu': jax.random.uniform(next(ks), (DEPTH, N_DIR, RWKV_FEAT), jnp.float32),
        'rwkv_w0': nrm((DEPTH, N_DIR, C_WIDTH), 0.5),
        'rwkv_w2': nrm((DEPTH, N_DIR, DECAY_LORA, C_WIDTH), 0.3 * DECAY_LORA ** -0.5),
        'rwkv_a0': nrm((DEPTH, N_DIR, C_WIDTH), 0.1),
        'rwkv_a2': nrm((DEPTH, N_DIR, AAA_LORA, C_WIDTH), 0.5 * AAA_LORA ** -0.5),
        'rwkv_k_k': 0.85 + nrm((DEPTH, C_WIDTH), 0.02),
        'rwkv_k_a': 1.0 + nrm((DEPTH, C_WIDTH), 0.02),
        'rwkv_r_k': nrm((DEPTH, C_HEADS, C_HEAD), 0.1),
        'rwkv_g2': nrm((DEPTH, GATE_LORA, C_WIDTH), GATE_LORA ** -0.5),
        'rwkv_lnx_g': 1.0 + nrm((DEPTH, C_WIDTH), 0.02),
        'rwkv_lnx_b': nrm((DEPTH, C_WIDTH), 0.02),
        'w_branch': nrm((DEPTH, N_BRANCH, C_WIDTH, D_MODEL), BETA * C_WIDTH ** -0.5),
        'w_out': nrm((DEPTH, D_MODEL, D_MODEL), BETA * D_MODEL ** -0.5),
        'ln1_g': 1.0 + nrm((DEPTH, D_MODEL), 0.02),
        'ln1_b': nrm((DEPTH, D_MODEL), 0.02),
        'w_up': nrm((DEPTH, D_MODEL, D_FF), D_MODEL ** -0.5),
        'w_down': nrm((DEPTH, D_FF, D_MODEL), BETA * D_FF ** -0.5),
        'ln2_g': 1.0 + nrm((DEPTH, D_MODEL), 0.02),
        'ln2_b': nrm((DEPTH, D_MODEL), 0.02),
    }


def reference(x_prompt, x_sample, cache_k, cache_v, state_wkv, c, c_ctx, w_ada, b_ada, w_in, sgu_ln_g, sgu_ln_b, sgu_w, sgu_b, q_norm, k_norm, rwkv_mu, rwkv_w0, rwkv_w2, rwkv_a0, rwkv_a2, rwkv_k_k, rwkv_k_a, rwkv_r_k, rwkv_g2, rwkv_lnx_g, rwkv_lnx_b, w_branch, w_out, ln1_g, ln1_b, w_up, w_down, ln2_g, ln2_b):
    P = dict(w_ada=w_ada, b_ada=b_ada, w_in=w_in, sgu_ln_g=sgu_ln_g, sgu_ln_b=sgu_ln_b, sgu_w=sgu_w, sgu_b=sgu_b,
             q_norm=q_norm, k_norm=k_norm, rwkv_mu=rwkv_mu, rwkv_w0=rwkv_w0, rwkv_w2=rwkv_w2, rwkv_a0=rwkv_a0,
             rwkv_a2=rwkv_a2, rwkv_k_k=rwkv_k_k, rwkv_k_a=rwkv_k_a, rwkv_r_k=rwkv_r_k, rwkv_g2=rwkv_g2,
             rwkv_lnx_g=rwkv_lnx_g, rwkv_lnx_b=rwkv_lnx_b, w_branch=w_branch, w_out=w_out, ln1_g=ln1_g,
             ln1_b=ln1_b, w_up=w_up, w_down=w_down, ln2_g=ln2_g, ln2_b=ln2_b)
    y = x_prompt
    ks, vs, ss = [], [], []
    for l in range(DEPTH):
        y, (k_l, v_l, s_l) = _trunk_layer(y, c_ctx[None, :], P, l, None)
        ks.append(k_l)
        vs.append(v_l)
        ss.append(s_l)
    new_cache_k = jnp.stack(ks, axis=1)
    new_cache_v = jnp.stack(vs, axis=1)
    new_state_wkv = jnp.stack(ss, axis=1)
    z = x_sample
    for l in range(DEPTH):
        z = _trunk_layer(z, c, P, l, (cache_k[:, l], cache_v[:, l], state_wkv[:, l]))
    return (y, z, new_cache_k, new_cache_v, new_state_wkv)
```

```python
import numpy as np
from contextlib import ExitStack
import concourse.bass as bass
import concourse.mybir as mybir
from concourse.bass_utils import run_bass_kernel_spmd

F32 = mybir.dt.float32
F32R = mybir.dt.float32r
BF16 = mybir.dt.bfloat16
AF = mybir.ActivationFunctionType
ALU = mybir.AluOpType
AX = mybir.AxisListType


class KB:
    def __init__(self, n_dma_sems=12):
        self.nc = bass.Bass("TRN2", target_bir_lowering=False)
        nc = self.nc
        self.es = ExitStack()
        self.es_root = self.es
        self.eng = {"pe": nc.tensor, "dve": nc.vector, "act": nc.scalar, "pool": nc.gpsimd, "sp": nc.sync}
        self.sem = {}
        self.cnt = {}
        self.clock = {}
        self.snap = {}
        for e in self.eng:
            self.sem[e] = self.es.enter_context(nc.semaphore("s_" + e))
            self.cnt[e] = 0
            self.clock[e] = {}
            self.snap[e] = {}
        self.dq = {}
        for q in ("sp", "pool", "act"):
            ids = []
            for i in range(n_dma_sems):
                sid = "d_%s_%d" % (q, i)
                self.sem[sid] = self.es.enter_context(nc.semaphore(sid))
                self.cnt[sid] = 0
                self.snap[sid] = {}
                ids.append(sid)
            self.dq[q] = [ids, 0]
        self.EPOCH = 60000
        self.epsem = {e: {0: self.sem[e]} for e in self.eng}
        self.lastw = {}
        self.readers = {}
        self.n_ins = 0
        self.n_wait = 0
        self.uid = 0
        self.ps_row = {}
        self.pending = {}
        self.drain_mode = 0
        self.last_tp = None

    def sb(self, name, shape, dt=F32):
        self.uid += 1
        name = "%s_%d" % (name, self.uid)
        return self.es.enter_context(self.nc.sbuf_tensor("s_" + name, list(shape), dt))

    def ps(self, name, shape, dt=F32):
        return self.es.enter_context(self.nc.psum_tensor("p_" + name, list(shape), dt))

    def dram(self, name, shape, dt=F32, kind="Internal"):
        return self.nc.dram_tensor(name, list(shape), dt, kind=kind)

    def _semv(self, f, c):
        if f not in self.eng:
            return self.sem[f], c
        ep = (c - 1) // self.EPOCH
        if ep not in self.epsem[f]:
            self.epsem[f][ep] = self.es_root.enter_context(self.nc.semaphore("s_%s_e%d" % (f, ep)))
        return self.epsem[f][ep], c - ep * self.EPOCH

    @staticmethod
    def _key(x):
        if isinstance(x, tuple):
            return (x[0].tensor.name, x[1])
        return (x.tensor.name, None)

    @staticmethod
    def _ap(x):
        return x[0] if isinstance(x, tuple) else x

    def _deps(self, eng, reads, writes):
        need = {}

        def add(f, c, same_ok):
            if f == eng and same_ok:
                return
            if c > need.get(f, 0):
                need[f] = c

        for k in reads:
            lw = self.lastw.get(k)
            if lw:
                add(lw[0], lw[1], False)
            if k[0].startswith("p_"):
                for f, c in self.readers.get(k, {}).items():
                    add(f, c, True)
        for k in writes:
            lw = self.lastw.get(k)
            if lw:
                add(lw[0], lw[1], True)
            for f, c in self.readers.get(k, {}).items():
                add(f, c, True)
        clk = self.clock[eng]
        e = self.eng[eng]
        for f, c in need.items():
            if clk.get(f, 0) >= c:
                continue
            if c > self.cnt[f]:
                raise RuntimeError("wait on unmaterialised count %s %d > %d" % (f, c, self.cnt[f]))
            sm, val = self._semv(f, c)
            e.wait_ge(sm, val)
            self.n_wait += 1
            for g, v in self.snap[f][c].items():
                if v > clk.get(g, 0):
                    clk[g] = v
            clk[f] = max(clk.get(f, 0), c)

    def _record(self, who, c, reads, writes):
        for k in reads:
            self.readers.setdefault(k, {})[who] = c
        for k in writes:
            self.lastw[k] = (who, c)
            self.readers[k] = {}

    def I(self, eng, fn, w=(), r=(), inc=True):
        rk = [self._key(x) for x in r]
        wk = [self._key(x) for x in w]
        self._deps(eng, rk, wk)
        ins = fn(self.eng[eng])
        self.n_ins += 1
        if not inc:
            self._record(eng, self.cnt[eng] + 1, rk, wk)
            self.pending[eng] = True
            return ins
        self.pending[eng] = False
        self.cnt[eng] += 1
        c = self.cnt[eng]
        ins.then_inc(self._semv(eng, c)[0], 1)
        s = dict(self.clock[eng])
        s[eng] = c
        self.snap[eng][c] = s
        self._record(eng, c, rk, wk)
        return ins

    def dma(self, out, in_, q="sp", **kw):
        rk = [self._key(in_)]
        wk = [self._key(out)]
        self._deps(q, rk, wk)
        ids, pos = self.dq[q]
        sid = ids[pos % len(ids)]
        self.dq[q][1] = pos + 1
        clk = self.clock[q]
        prev = self.cnt[sid]
        if prev and clk.get(sid, 0) < prev:
            self.eng[q].wait_ge(self.sem[sid], prev)
            for g, v in self.snap[sid][prev].items():
                if v > clk.get(g, 0):
                    clk[g] = v
            clk[sid] = prev
        ins = self.eng[q].dma_start(out=self._ap(out), in_=self._ap(in_), **kw)
        c = prev + 16
        self.cnt[sid] = c
        ins.then_inc(self.sem[sid], 16)
        s = dict(clk)
        s[sid] = c
        self.snap[sid][c] = s
        self._record(sid, c, rk, wk)
        self.n_ins += 1
        return ins

    def finish(self):
        assert not any(self.pending.values()), self.pending
        e = "sp"
        clk = self.clock[e]
        for f in self.sem:
            c = self.cnt[f]
            if c and clk.get(f, 0) < c and f != e:
                sm, val = self._semv(f, c)
                self.eng[e].wait_ge(sm, val)
        self.es.close()

    def mm(self, out, lhsT, rhs, start=True, stop=True, inc=True, **kw):
        a = self._ap
        lt = a(lhsT)
        row = lt.base_partition() if lt.partition_size() < 128 else -1
        okey = self._key(out)
        prev = self.ps_row.get(okey)
        if prev is not None and prev[0] != row and self.clock["pe"].get("pe", 0) < prev[1]:
            if prev[1] > self.cnt["pe"]:
                raise RuntimeError("PE row switch on bank %s needs a materialised count" % (okey,))
            sm, val = self._semv("pe", prev[1])
            self.eng["pe"].wait_ge(sm, val)
            self.clock["pe"]["pe"] = prev[1]
            self.n_wait += 1
        self.ps_row[okey] = (row, self.cnt["pe"] + 1)
        tp = kw.get("tile_position")
        if self.drain_mode and tp is not None and self.cnt["pe"] and self.clock["pe"].get("pe", 0) < self.cnt["pe"]:
            if self.drain_mode == 1 or (self.drain_mode == 2 and tp != self.last_tp):
                sm, val = self._semv("pe", self.cnt["pe"])
                self.eng["pe"].wait_ge(sm, val)
                self.clock["pe"]["pe"] = self.cnt["pe"]
        self.last_tp = tp
        return self.I("pe", lambda e: e.matmul(a(out), a(lhsT), a(rhs), start=start, stop=stop, **kw),
                      w=[out], r=[lhsT, rhs], inc=inc)

    def tr(self, out, in_, ident):
        a = self._ap
        return self.I("pe", lambda e: e.transpose(a(out), a(in_), a(ident)), w=[out], r=[in_, ident])

    def act(self, out, in_, func, bias=None, scale=1.0, accum=None, eng="act"):
        a = self._ap
        r = [in_]
        kw = {}
        if bias is not None:
            if isinstance(bias, (int, float)):
                kw["bias"] = float(bias)
            else:
                kw["bias"] = a(bias)
                r.append(bias)
        if isinstance(scale, (int, float)):
            kw["scale"] = float(scale)
        else:
            kw["scale"] = a(scale)
            r.append(scale)
        w = [out]
        if accum is not None:
            kw["accum_out"] = a(accum)
            w.append(accum)
        return self.I(eng, lambda e: e.activation(a(out), a(in_), func, **kw), w=w, r=r)

    def tt(self, out, x, y, op, eng="dve"):
        a = self._ap
        return self.I(eng, lambda e: e.tensor_tensor(a(out), a(x), a(y), op), w=[out], r=[x, y])

    def ts(self, out, x, s1, op0, s2=None, op1=None, eng="dve", accum=None):
        a = self._ap
        r = [x]
        v1 = s1
        if not isinstance(s1, (int, float)):
            r.append(s1)
            v1 = a(s1)
        v2 = s2
        if s2 is not None and not isinstance(s2, (int, float)):
            r.append(s2)
            v2 = a(s2)
        kw = {}
        w = [out]
        if op1 is not None:
            kw["op1"] = op1
        if accum is not None:
            kw["accum_out"] = a(accum)
            w.append(accum)
        return self.I(eng, lambda e: e.tensor_scalar(a(out), a(x), v1, v2, op0, **kw), w=w, r=r)

    def stt(self, out, x, s, y, op0, op1, eng="dve"):
        a = self._ap
        r = [x, y]
        v = s
        if not isinstance(s, (int, float)):
            r.append(s)
            v = a(s)
        return self.I(eng, lambda e: e.scalar_tensor_tensor(a(out), a(x), v, a(y), op0, op1), w=[out], r=r)

    def cp(self, out, in_, eng="dve"):
        a = self._ap
        if eng == "act":
            return self.I(eng, lambda e: e.copy(a(out), a(in_)), w=[out], r=[in_])
        return self.I(eng, lambda e: e.tensor_copy(a(out), a(in_)), w=[out], r=[in_])

    def memset(self, out, val, eng="pool"):
        a = self._ap
        return self.I(eng, lambda e: e.memset(a(out), val), w=[out])

    def recip(self, out, in_):
        a = self._ap
        return self.I("dve", lambda e: e.reciprocal(a(out), a(in_)), w=[out], r=[in_])

    def rsqrt(self, out, in_, eps):
        self.act(out, in_, AF.Ln, bias=eps)
        self.act(out, out, AF.Exp, scale=-0.5)

    def barrier(self):
        assert not any(self.pending.values()), self.pending
        for e in self.eng:
            clk = self.clock[e]
            for f in self.sem:
                c = self.cnt[f]
                if f == e or not c or clk.get(f, 0) >= c:
                    continue
                sm, val = self._semv(f, c)
                self.eng[e].wait_ge(sm, val)
                clk[f] = c
        full = {f: self.cnt[f] for f in self.sem if self.cnt[f]}
        for e in self.eng:
            for f, c in full.items():
                if f != e:
                    self.clock[e][f] = max(self.clock[e].get(f, 0), c)

    def scope(self):
        kb = self

        class _S:
            def __enter__(s):
                s.old = kb.es
                kb.es = ExitStack()
                return s

            def __exit__(s, *a):
                kb.barrier()
                kb.es.close()
                kb.es = s.old
                return False

        return _S()

T = 3072
NG = 6
D = 1024
NPP = 160
O_BADA, O_LN1G, O_LN1B, O_LN2G, O_LN2B, O_QN, O_KN = 0, 48, 56, 64, 72, 80, 81
O_MU, O_W0, O_A0, O_KK, O_KA, O_RK, O_LXG, O_LXB, O_OMKA, O_OMU = 82, 108, 116, 124, 128, 132, 136, 140, 144, 148
NPP = 176
ALPHA = 4.0 ** 0.25
C_UA, C_VA, C_Q, C_K, C_V, C_R, C_GL = 0, 512, 1024, 1536, 1664, 1792, 3712


class StopBuild(Exception):
    pass


def build(stop_after=None, dbg=()):
    kb = KB()
    kb.stop_after = stop_after
    kb.drain_mode = 1 if "drain1" in dbg else (2 if "drain2" in dbg else 0)
    try:
        _build(kb, stop_after, dbg)
    except StopBuild:
        pass
    kb.finish()
    return kb


def chk(kb, name, l=0):
    if kb.stop_after == (name, l):
        raise StopBuild()


def _build(kb, stop_after, dbg):
    nc = kb.nc
    IN = lambda n, s, dt=F32: kb.dram(n, s, dt, kind="ExternalInput")
    OUT = lambda n, s, dt=F32: kb.dram(n, s, dt, kind="ExternalOutput")
    x_in = IN("x", [T, D])
    ck_in = IN("ck", [2, 512, 128])
    cvv_in = IN("cvv", [2, 512, 128])
    st_in = IN("st0", [2, 2, 128, 256])
    cvec_in = IN("cvec", [128, 8, 2])
    pp_in = IN("pp", [2, 128, NPP])
    w_ada = IN("w_ada", [2, D, 6144])
    w_in = IN("w_in", [2, D, 6784])
    w_br = IN("w_branch", [2, 3, 512, D])
    w_out = IN("w_out", [2, D, D])
    w_up = IN("w_up", [2, D, 4096])
    w_dn = IN("w_down", [2, 4096, D])
    wsT_in = IN("wsT", [2, 128, 4, 128])
    sgb_in = IN("sgb", [2, 1, 512])
    lnA_in = IN("lnA", [2, 2, 128, 512])
    wa2_in = IN("wa2", [2, 2, 128, 512])
    g2_in = IN("g2", [2, 128, 512])
    cst_in = IN("cst", [128, 9, 512])
    rope_in = IN("rope", [64, 2, 2048])
    y_out = OUT("y", [T, D])
    nk_out = OUT("nk", [2, 1024, 128])
    nv_out = OUT("nv", [2, 1024, 128])
    ns_out = OUT("ns", [4, 2, 2, 128, 256])
    XT = [kb.dram("XT%d" % i, [128, 8, T]) for i in range(2)]
    RKV = kb.dram("RKV", [128, 15, T])
    BRA = kb.dram("BRA", [128, 4, T], BF16)
    BRB = kb.dram("BRB", [64, 8, T], BF16)
    BRC = kb.dram("BRC", [128, 4, T], BF16)
    BONS = kb.dram("BONS", [128, 4, T])
    XM = kb.dram("XM", [128, 8, T])
    YS = kb.dram("YS", [128, 4, T])
    dbg_out = {}
    if "dump" in dbg:
        dbg_out["d1"] = OUT("dbg1", [128, 32, 256])
        dbg_out["d2"] = OUT("dbg2", [128, 14, 256])
        dbg_out["d3"] = OUT("dbg3", [128, 6, 1024])

    P = [kb.ps("P%d" % i, [128, 512]) for i in range(8)]
    cst = kb.sb("cst", [128, 9, 512])
    kb.dma(cst[:], cst_in.ap()[:, :, :])
    ident = cst[:, 0, 0:128]
    BDm = cst[:, 0, 128:256]
    BD64 = cst[:, 0, 256:384]
    ones64 = cst[0:64, 0, 384:448]
    maskF4, maskB4, maskFT8, maskBT8, ident8, rst = (cst[:, i, :] for i in range(1, 7))
    rm_f = cst[0:64, 7, 0:64]
    ones_row = cst[:, 7, 64:192]
    onesD = cst[:, 8, 0:128]
    rm_b = kb.sb("rm_b", [64, 64], BF16)
    kb.cp(rm_b[:], rm_f, eng="dve")
    ident_b = kb.sb("ident_b", [128, 128], BF16)
    kb.cp(ident_b[:], ident, eng="dve")
    pp = [kb.sb("pp%d" % l, [128, NPP]) for l in range(2)]
    for l in range(2):
        kb.dma(pp[l][:], pp_in.ap()[l])
    modT = [kb.sb("modT%d" % l, [128, 48, 2]) for l in range(2)]

    with kb.scope():
        cv = kb.sb("cv", [128, 8, 2])
        sl = kb.sb("sl", [128, 8, 2])
        kb.dma(cv[:], cvec_in.ap()[:, :, :])
        kb.act(sl[:], cv[:], AF.Silu)
        wts = [kb.sb("wadat%d" % i, [128, 8, 512]) for i in range(2)]
        for l in range(2):
            wv = w_ada.ap()[l].rearrange("(kc p) n -> p kc n", p=128)
            for nb in range(12):
                wt = wts[nb % 2]
                kb.dma(wt[:], wv[:, :, nb * 512:(nb + 1) * 512])
                ps = P[nb % 2]
                for j in range(4):
                    for kc in range(8):
                        kb.mm(ps[:, 2 * j:2 * j + 2], wt[:, kc, j * 128:(j + 1) * 128], sl[:, kc, :],
                              start=(kc == 0), stop=(kc == 7), inc=(kc == 7))
                for j in range(4):
                    c = nb * 4 + j
                    kb.ts(modT[l][:, c, :], ps[:, 2 * j:2 * j + 2], pp[l][:, O_BADA + c:O_BADA + c + 1], ALU.add)
            for c0 in (8, 32):
                kb.ts(modT[l][:, c0:c0 + 8, :], modT[l][:, c0:c0 + 8, :], 1.0, ALU.add, eng="pool")
            kb.ts(pp[l][:, O_OMU:O_OMU + 26], pp[l][:, O_MU:O_MU + 26], -1.0, ALU.mult, 1.0, ALU.add, eng="pool")
            kb.ts(pp[l][:, O_OMKA:O_OMKA + 4], pp[l][:, O_KA:O_KA + 4], -1.0, ALU.mult, 1.0, ALU.add, eng="pool")

    chk(kb, "mod")
    with kb.scope():
        xts = [kb.sb("xt_t%d" % i, [128, 8, 512]) for i in range(2)]
        xins = [kb.sb("xin%d" % i, [128, 1024]) for i in range(2)]
        k = 0
        for g in range(NG):
            xt = xts[g % 2]
            for tt in range(4):
                xin = xins[tt % 2]
                kb.dma(xin[:], x_in.ap()[(g * 4 + tt) * 128:(g * 4 + tt + 1) * 128, :])
                for half in range(2):
                    ps = P[k % 4]
                    k += 1
                    for c in range(4):
                        kb.tr(ps[:, c * 128:(c + 1) * 128], xin[:, (half * 4 + c) * 128:(half * 4 + c + 1) * 128], ident)
                    kb.cp(xt[:, half * 4:(half + 1) * 4, tt * 128:(tt + 1) * 128],
                          ps[:].rearrange("p (c t) -> p c t", c=4), eng=("dve" if half else "act"))
            kb.dma(XT[0].ap()[:, :, g * 512:(g + 1) * 512], xt[:])

    chk(kb, "t0")
    for l in range(2):
        layer(kb, l, locals())
        chk(kb, "layer", l)

    if stop_after is None:
        with kb.scope():
            xts = [kb.sb("oxt%d" % i, [128, 8, 512]) for i in range(2)]
            yos = [kb.sb("yo%d" % i, [128, 1024]) for i in range(2)]
            k = 0
            for g in range(NG):
                xt = xts[g % 2]
                kb.dma(xt[:], XT[0].ap()[:, :, g * 512:(g + 1) * 512])
                for tt in range(4):
                    yo = yos[tt % 2]
                    for half in range(2):
                        ps = P[k % 4]
                        k += 1
                        for c in range(4):
                            kb.tr(ps[:, c * 128:(c + 1) * 128], xt[:, half * 4 + c, tt * 128:(tt + 1) * 128], ident)
                        kb.cp(yo[:, half * 512:(half + 1) * 512], ps[:], eng=("dve" if half else "act"))
                    kb.dma(y_out.ap()[(g * 4 + tt) * 128:(g * 4 + tt + 1) * 128, :], yo[:])


def mod_cols(modT_l, base, g):
    v = 0 if g < 2 else 1
    return [modT_l[:, base + c, v:v + 1] for c in range(8)]


def layer(kb, l, E):
    P = E["P"]; pp = E["pp"][l]; modT = E["modT"][l]; cst = E["cst"]
    XTi = E["XT"][l % 2]; XTo = E["XT"][(l + 1) % 2]
    w_in = E["w_in"]; RKV = E["RKV"]; BRA = E["BRA"]; BRB = E["BRB"]; BRC = E["BRC"]
    ident = E["ident"]; ident_b = E["ident_b"]
    winv = w_in.ap()[l].rearrange("(kc p) n -> p kc n", p=128)
    pk = [0]

    def PS(lo=0, hi=8):
        i = lo + pk[0] % (hi - lo)
        pk[0] += 1
        return P[i]

    ek = [0]

    def EV():
        ek[0] += 1
        return "dve" if ek[0] % 2 else "act"

    with kb.scope():
        HT = kb.sb("HT", [128, 8, T], BF16)
        QT = kb.sb("QT", [64, 8, T], BF16)
        KT = kb.sb("KT", [64, 2, T + 512], BF16)
        VS = kb.sb("VS", [128, 28, 2, 65], BF16)
        kb.memset(VS[:], 1.0)
        with kb.scope():
            xls = [kb.sb("xl%d" % i, [128, 8, 512]) for i in range(2)]
            for g in range(NG):
                xl = xls[g % 2]
                kb.dma(xl[:], XTi.ap()[:, :, g * 512:(g + 1) * 512])
                sc = mod_cols(modT, 8, g); sh = mod_cols(modT, 0, g)
                for c in range(8):
                    kb.ts(HT[:, c, g * 512:(g + 1) * 512], xl[:, c, :], sc[c], ALU.mult, sh[c], ALU.add,
                          eng=("dve" if c % 2 else "pool"))
        chk(kb, "ht", l)
        with kb.scope():
            UA = kb.sb("UA", [128, 4, T], BF16)
            W = kb.sb("Wua", [128, 8, 512], BF16)
            kb.dma(W[:], winv[:, :, C_UA:C_UA + 512], q="pool")
            for g in range(NG):
                for j in range(4):
                    ps = PS()
                    for kc in range(8):
                        kb.mm(ps[:], W[:, kc, j * 128:(j + 1) * 128], HT[:, kc, g * 512:(g + 1) * 512],
                              start=(kc == 0), stop=(kc == 7), inc=(kc == 7))
                    kb.cp(UA[:, j, g * 512:(g + 1) * 512], ps[:], eng=EV())
            chk(kb, "ua", l)
            W2 = kb.sb("Wva", [128, 8, 512], BF16)
            kb.dma(W2[:], winv[:, :, C_VA:C_VA + 512], q="pool")
            wsT = kb.sb("wsT", [128, 4, 128], BF16)
            kb.dma(wsT[:], E["wsT_in"].ap()[l], q="pool")
            sgb = kb.sb("sgb", [1, 512])
            kb.dma(sgb[:], E["sgb_in"].ap()[l])
            lnG = kb.sb("lnG", [128, 512]); lnB = kb.sb("lnB", [128, 512])
            kb.dma(lnG[:], E["lnA_in"].ap()[l, 0]); kb.dma(lnB[:], E["lnA_in"].ap()[l, 1])
            ones_row = E["ones_row"]
            sqs = [kb.sb("a_sq%d" % i, [128, 512]) for i in range(2)]
            v0s = [kb.sb("a_v0%d" % i, [128, 512]) for i in range(2)]
            vns = [kb.sb("a_vn%d" % i, [128, 512], BF16) for i in range(2)]
            sts = [kb.sb("a_st%d" % i, [128, 8]) for i in range(2)]
            for tt in range(24):
                ps = PS(0, 4)
                for kc in range(8):
                    kb.mm(ps[:], HT[:, kc, tt * 128:(tt + 1) * 128], W2[:, kc, :], start=(kc == 0), stop=(kc == 7), inc=(kc == 7))
                sq = sqs[tt % 2]; v0 = v0s[tt % 2]; vn = vns[tt % 2]; st = sts[tt % 2]
                cut = [int(x[3:]) for x in E["dbg"] if x.startswith("cut")]
                cut = cut[0] if cut else 99
                kb.act(v0[:], ps[:], AF.Copy)
                kb.act(sq[:], v0[:], AF.Square)
                if cut < 2: continue
                kb.I("dve", lambda e: e.reduce_sum(st[:, 0:1], v0[:], AX.X), w=[st[:]], r=[v0[:]])
                kb.I("dve", lambda e: e.reduce_sum(st[:, 1:2], sq[:], AX.X), w=[st[:]], r=[sq[:], st[:]])
                if cut < 3: continue
                kb.ts(st[:, 2:3], st[:, 0:1], 1.0 / 512, ALU.mult)
                kb.tt(st[:, 3:4], st[:, 2:3], st[:, 2:3], ALU.mult)
                kb.stt(st[:, 4:5], st[:, 1:2], 1.0 / 512, st[:, 3:4], ALU.mult, ALU.subtract)
                if cut < 4: continue
                kb.rsqrt(st[:, 5:6], st[:, 4:5], 1e-5)
                kb.stt(st[:, 6:7], st[:, 2:3], -1.0, st[:, 5:6], ALU.mult, ALU.mult)
                if cut < 5: continue
                kb.act(v0[:], v0[:], AF.Identity, bias=st[:, 6:7], scale=st[:, 5:6])
                if cut < 6: continue
                kb.tt(v0[:], v0[:], lnG[:], ALU.mult, eng="pool")
                kb.tt(vn[:], v0[:], lnB[:], ALU.add, eng="pool")
                if cut < 7: continue
                ps2 = PS(4, 8)
                for j in range(4):
                    nob = "nobias" in E["dbg"]
                    kb.mm(ps2[:, j * 128:(j + 1) * 128], vn[:, j * 128:(j + 1) * 128], wsT[:, j, :], start=True, stop=nob)
                    if not nob:
                        kb.mm(ps2[:, j * 128:(j + 1) * 128], ones_row[0:1, :], sgb[0:1, j * 128:(j + 1) * 128], start=False, stop=True)
                if cut < 8: continue
                kb.tt(UA[:, :, tt * 128:(tt + 1) * 128], UA[:, :, tt * 128:(tt + 1) * 128],
                      ps2[:].rearrange("p (j t) -> p j t", j=4), ALU.mult)
            if "nobra" not in E["dbg"]:
                kb.dma(BRA.ap()[:, :, :], UA[:])
        chk(kb, "va", l)
        with kb.scope():
            Wq = kb.sb("Wq", [128, 8, 512], BF16)
            kb.dma(Wq[:], winv[:, :, C_Q:C_Q + 512], q="pool")
            Wkv = kb.sb("Wkv", [128, 8, 256], BF16)
            kb.dma(Wkv[:], winv[:, :, C_K:C_K + 256], q="pool")
            rope = kb.sb("rope", [64, 2, 2048])
            kb.dma(rope[:], E["rope_in"].ap()[:, :, :])
            sqs = [kb.sb("q_sq%d" % i, [64, 512]) for i in range(2)]
            rss = [kb.sb("q_rs%d" % i, [64, 512]) for i in range(2)]
            qns = [kb.sb("q_qn%d" % i, [64, 512]) for i in range(2)]
            qbs = [kb.sb("q_qb%d" % i, [64, 512], BF16) for i in range(2)]
            t1s = [kb.sb("q_t1%d" % i, [64, 512]) for i in range(2)]
            t2s = [kb.sb("q_t2%d" % i, [64, 512]) for i in range(2)]
            qrs = [kb.sb("q_qr%d" % i, [64, 512]) for i in range(2)]
            kos = [kb.sb("q_ko%d" % i, [128, 4, 128]) for i in range(2)]
            vos = [kb.sb("q_vo%d" % i, [128, 128]) for i in range(2)]
            ones64 = E["ones64"]; rm_b = E["rm_b"]
            it = 0
            for g in range(NG):
                latent = g >= 2
                tok = slice(g * 512, (g + 1) * 512)
                for hh in range(10):
                    isk = hh >= 8
                    h = hh - 8 if isk else hh
                    Wt = Wkv if isk else Wq
                    gcol = pp[0:64, (O_KN if isk else O_QN):(O_KN if isk else O_QN) + 1]
                    i2 = it % 2
                    it += 1
                    sq, rs, qn, qb, t1, t2 = sqs[i2], rss[i2], qns[i2], qbs[i2], t1s[i2], t2s[i2]
                    ps = PS(0, 3)
                    for kc in range(8):
                        kb.mm(ps[0:64, :], Wt[:, kc, h * 64:(h + 1) * 64], HT[:, kc, tok], start=(kc == 0), stop=(kc == 7), inc=(kc == 7))
                    qr = qrs[i2]
                    kb.act(qr[:], ps[0:64, :], AF.Copy)
                    kb.act(sq[:], ps[0:64, :], AF.Square)
                    ps2 = PS(3, 6)
                    kb.mm(ps2[0:64, :], ones64, sq[:])
                    kb.rsqrt(rs[:], ps2[0:64, :], 1e-6)
                    dst = (KT[:, h, tok] if isk else QT[:, h, tok])
                    if (not latent) and (not isk):
                        kb.stt(dst, qr[:], gcol, rs[:], ALU.mult, ALU.mult)
                        continue
                    kb.stt(qn[:], qr[:], gcol, rs[:], ALU.mult, ALU.mult)
                    if not latent:
                        kb.cp(dst, qn[:], eng="pool")
                        ko = kos[g % 2]
                        ps3 = PS(6, 8)
                        for tt in range(4):
                            kb.tr(ps3[:, tt * 64:(tt + 1) * 64], qn[:, tt * 128:(tt + 1) * 128], ident[0:64, 0:64])
                        kb.cp(ko[:, :, h * 64:(h + 1) * 64], ps3[:, 0:256].rearrange("p (t d) -> p t d", t=4), eng="act")
                        if h == 1:
                            kb.dma(E["nk_out"].ap()[l, g * 512:(g + 1) * 512, :].rearrange("(t p) c -> p t c", p=128), ko[:])
                        continue
                    pos = slice((g - 2) * 512, (g - 1) * 512)
                    kb.cp(qb[:], qn[:], eng="pool")
                    ps3 = PS(6, 8)
                    kb.mm(ps3[0:64, :], rm_b[:], qb[:])
                    kb.tt(t1[:], qn[:], rope[:, 0, pos], ALU.mult, eng="pool")
                    kb.tt(t2[:], ps3[0:64, :], rope[:, 1, pos], ALU.mult)
                    kb.tt(dst, t1[:], t2[:], ALU.add, eng="pool")
                for t4 in range(4):
                    tt = g * 4 + t4
                    ps = PS(0, 3)
                    for kc in range(8):
                        kb.mm(ps[:, 0:128], HT[:, kc, tt * 128:(tt + 1) * 128], Wkv[:, kc, 128:256], start=(kc == 0), stop=(kc == 7), inc=(kc == 7))
                    vo = vos[t4 % 2]
                    kb.cp(vo[:], ps[:, 0:128], eng="act")
                    kb.cp(VS[:, tt, :, 0:64], vo[:].rearrange("p (k d) -> p k d", k=2), eng="pool")
                    if not latent:
                        kb.dma(E["nv_out"].ap()[l, tt * 128:(tt + 1) * 128, :], vo[:])
            ckt = kb.sb("ckt", [128, 4, 128]); cvt = kb.sb("cvt", [128, 4, 128])
            kb.dma(ckt[:], E["ck_in"].ap()[l].rearrange("(t p) c -> p t c", p=128))
            kb.dma(cvt[:], E["cvv_in"].ap()[l].rearrange("(t p) c -> p t c", p=128))
            for t4 in range(4):
                for kv in range(2):
                    ps = PS(0, 3)
                    kb.tr(ps[0:64, 0:128], ckt[:, t4, kv * 64:(kv + 1) * 64], ident)
                    kb.cp(KT[:, kv, T + t4 * 128:T + (t4 + 1) * 128], ps[0:64, 0:128], eng=EV())
                kb.cp(VS[:, 24 + t4, :, 0:64], cvt[:, t4, :].rearrange("p (k d) -> p k d", k=2), eng="pool")
        chk(kb, "q", l)
        with kb.scope():
            Ws = [kb.sb("Wr%d" % i, [128, 8, 512], BF16) for i in range(2)]
            stg = [kb.sb("rstg%d" % i, [128, 512]) for i in range(3)]
            k = 0
            for wb in range(4):
                ncol = 512 if wb < 3 else 384
                W = Ws[wb % 2]
                kb.dma(W[:, :, 0:ncol], winv[:, :, C_R + wb * 512:C_R + wb * 512 + ncol], q="pool")
                for bb in range(ncol // 128):
                    b = wb * 4 + bb
                    for g in range(NG):
                        ps = PS()
                        for kc in range(8):
                            kb.mm(ps[:], W[:, kc, bb * 128:(bb + 1) * 128], HT[:, kc, g * 512:(g + 1) * 512],
                                  start=(kc == 0), stop=(kc == 7), inc=(kc == 7))
                        s = stg[k % 3]
                        k += 1
                        kb.cp(s[:], ps[:], eng=EV())
                        kb.dma(RKV.ap()[:, b, g * 512:(g + 1) * 512], s[:])
        chk(kb, "rkv", l)
        with kb.scope():
            pts = [kb.sb("pt%d" % i, [128, 512], BF16) for i in range(3)]
            ons = [kb.sb("on%d" % i, [128, 512]) for i in range(2)]
            rcs = [kb.sb("rc%d" % i, [128, 512]) for i in range(2)]
            obs = [kb.sb("ob%d" % i, [64, 512], BF16) for i in range(2)]
            seqs = [(s * 256, 256, [(s * 256 + i * 128, s * 2 + i) for i in range(2)]) for s in range(4)]
            seqs.append((1024, 2048, [(T + i * 128, 24 + i) for i in range(4)] + [(1024 + i * 128, 8 + i) for i in range(16)]))
            n = 0
            for (t0, L, keys) in seqs:
                for kv in range(2):
                    for qt in range(L // 128):
                        q0 = t0 + qt * 128
                        po = P[6 + n % 2]
                        nk_ = len(keys)
                        pss = [None] * nk_

                        def score(i):
                            ps = PS(0, 4)
                            kb.mm(ps[:], KT[:, kv, keys[i][0]:keys[i][0] + 128], QT[:, 4 * kv:4 * kv + 4, q0:q0 + 128])
                            pss[i] = ps

                        score(0)
                        for i, (kc0, vt) in enumerate(keys):
                            if i + 1 < nk_:
                                score(i + 1)
                            pt = pts[i % 3]
                            kb.act(pt[:], pss[i][:], AF.Exp, scale=0.125)
                            kb.mm(po[0:65, :], VS[:, vt, kv, :], pt[:], start=(i == 0), stop=(i == nk_ - 1))
                        on = ons[n % 2]; rc = rcs[n % 2]; ob = obs[n % 2]
                        kb.cp(on[0:65, :], po[0:65, :], eng="act")
                        kb.recip(rc[64:65, :], on[64:65, :])
                        pb = P[4 + n % 2]
                        kb.mm(pb[0:64, :], E["ones_row"][64:65, 0:64], rc[64:65, :], tile_position=(64, 0))
                        kb.tt(ob[:], on[0:64, :], pb[0:64, :], ALU.mult)
                        kb.dma(BRB.ap()[:, 4 * kv:4 * kv + 4, q0:q0 + 128], ob[:].rearrange("p (g t) -> p g t", g=4))
                        n += 1
    chk(kb, "att", l)
    if "noC" not in E["dbg"]:
        rwkv(kb, l, E)
    chk(kb, "rw", l)
    merge_ffn(kb, l, E)


def mmg(kb, terms):
    order = sorted(range(len(terms)), key=lambda i: (terms[i][4][0], i))
    first, last = {}, {}
    for n, i in enumerate(order):
        r = terms[i][0]
        first.setdefault(r, n)
        last[r] = n
    for n, i in enumerate(order):
        r, out, lhsT, rhs, tp = terms[i]
        kb.mm(out, lhsT, rhs, start=(first[r] == n), stop=(last[r] == n), tile_position=tp)


def rwkv(kb, l, E):
    P = E["P"]; pp = E["pp"][l]; RKV = E["RKV"]; BRC = E["BRC"]; BONS = E["BONS"]; YS = E["YS"]
    ident = E["ident"]; BDm = E["BDm"]; BD64 = E["BD64"]
    NP_ = 256
    fk = [0]

    def FPS():
        fk[0] += 1
        return P[6 + fk[0] % 2]

    def bc4(off, n=NP_):
        return pp[:, off:off + 4].unsqueeze(2).to_broadcast([128, 4, n])

    with kb.scope():
        A = lambda n, s, dt=F32: kb.sb(n, s, dt)
        XR = A("XR", [128, 14, NP_ + 2])
        R = A("rR", [128, 4, NP_]); K = A("rK", [128, 4, NP_]); V = A("rV", [128, 4, NP_]); LO = A("rLO", [128, NP_])
        TW = A("rTW", [128, NP_]); LW = A("rLW", [128, 4, NP_]); AA = A("rAA", [128, 4, NP_])
        KK = A("rKK", [128, 4, NP_])
        CL = A("rCL", [128, 4, NP_]); KM = A("rKM", [128, 4, NP_]); BB = A("rBB", [128, 4, NP_])
        Bh = A("rBh", [128, 4, NP_]); Kh = A("rKh", [128, 4, NP_])
        ARs = [A("rAR%d" % i, [128, 4, 4, 128]) for i in range(2)]
        Bts = [A("rBt%d" % i, [128, 4, NP_]) for i in range(2)]
        Kts = [A("rKt%d" % i, [128, 4, NP_]) for i in range(2)]
        BhTs = [A("rBhT%d" % i, [128, 2, 512]) for i in range(2)]
        KhTs = [A("rKhT%d" % i, [128, 2, 512]) for i in range(2)]
        VTs = [A("rVT%d" % i, [128, 2, 512]) for i in range(2)]
        BONs = [A("rBON%d" % i, [128, 4, NP_]) for i in range(2)]
        GAMs = [A("rGAM%d" % i, [128, 4, 4]) for i in range(2)]
        YPs = [A("rYP%d" % i, [128, 4, NP_]) for i in range(2)]
        F1 = A("fT1", [128, 4, NP_]); F2 = A("fT2", [128, 4, NP_]); FE = A("fEE", [128, 4, NP_])
        WA = [A("rWA%d" % d, [128, 512]) for d in range(2)]
        for d in range(2):
            kb.dma(WA[d][:], E["wa2_in"].ap()[l, d])
        G2 = A("rG2", [128, 512], BF16)
        kb.dma(G2[:], E["g2_in"].ap()[l], q="pool")
        NM = A("cNM", [128, 8, 128]); KMt = A("cKMt", [128, 8, 128])
        Nb = [A("cN%d" % i, [128, 8, 64]) for i in range(2)]
        NTb = [A("cNT%d" % i, [128, 8, 64]) for i in range(2)]
        TTb = [A("cTT%d" % i, [128, 8, 64]) for i in range(2)]
        RHS = A("cRHS", [128, 512]); U = A("cU", [128, 512]); RHS0 = A("cRHS0", [128, 512])
        ST = [A("cST%d" % i, [128, 4, 64]) for i in range(2)]
        SGD = A("rSGD", [128, NP_], BF16); GD = A("rGD", [128, NP_]); OC = A("rOC", [128, 4, NP_], BF16)
        v64 = lambda ap: ap.rearrange("p a (c t) -> p a c t", t=64)

        items = []
        for (t0, L, latent, sidx) in [(s * 256, 256, False, s) for s in range(4)] + [(1024, 2048, True, None)]:
            npc = L // NP_
            for d in range(2):
                order = list(range(npc)) if d == 0 else list(range(npc - 1, -1, -1))
                for j, pc in enumerate(order):
                    items.append(dict(t0=t0, latent=latent, sidx=sidx, d=d, pc=pc, npc=npc,
                                      first=(j == 0), last=(j == npc - 1)))
        sic = [0]

        def front(it, b):
            d, pc, npc = it["d"], it["pc"], it["npc"]
            p0 = it["t0"] + pc * NP_
            AR, Bt, Kt, BhT, KhT, VT, BON, GAM, YP = ARs[b], Bts[b], Kts[b], BhTs[b], KhTs[b], VTs[b], BONs[b], GAMs[b], YPs[b]
            lo_h = 1 if pc == 0 else 0
            hi_h = NP_ + 1 if pc == npc - 1 else NP_ + 2
            if pc == 0:
                kb.memset(XR[:, :, 0:1], 0.0)
            if pc == npc - 1:
                kb.memset(XR[:, :, NP_ + 1:NP_ + 2], 0.0)
            kb.dma(XR[:, :, lo_h:hi_h], RKV.ap()[:, 0:14, p0 - 1 + lo_h:p0 - 1 + hi_h])
            yield
            sh = 0 if d == 0 else 2
            dsts = [R[:, i, :] for i in range(4)] + [K[:, i, :] for i in range(4)] + [V[:, i, :] for i in range(4)] + [LO[:]]
            for blk in range(13):
                sbk = blk if blk < 12 else 12 + d
                mu = pp[:, O_MU + d * 13 + blk:O_MU + d * 13 + blk + 1]
                omu = pp[:, O_OMU + d * 13 + blk:O_OMU + d * 13 + blk + 1]
                tmp = F1[:, blk % 4, :]
                kb.ts(tmp, XR[:, sbk, sh:sh + NP_], mu, ALU.mult, eng="pool")
                kb.stt(dsts[blk], XR[:, sbk, 1:NP_ + 1], omu, tmp, ALU.mult, ALU.add)
                if blk % 3 == 2:
                    yield
            kb.act(TW[0:64, :], LO[0:64, :], AF.Tanh)
            for pr in range(4):
                ps = FPS()
                kb.mm(ps[:, 0:NP_], WA[d][0:64, pr * 128:(pr + 1) * 128], TW[0:64, :])
                kb.act(LW[:, pr, :], ps[:, 0:NP_], AF.Sigmoid, bias=pp[:, O_W0 + d * 4 + pr:O_W0 + d * 4 + pr + 1])
                ps = FPS()
                kb.mm(ps[:, 0:NP_], WA[d][64:128, pr * 128:(pr + 1) * 128], LO[64:128, :], tile_position=(64, 0))
                kb.act(AA[:, pr, :], ps[:, 0:NP_], AF.Sigmoid, bias=pp[:, O_A0 + d * 4 + pr:O_A0 + d * 4 + pr + 1])
                yield
            kb.ts(LW[:], LW[:], -0.6065306597126334, ALU.mult, eng="pool")
            for pr in range(4):
                kb.I("dve", lambda e, pr=pr: e.tensor_tensor_scan(CL[:, pr, :], E["rst"][:, 0:NP_], LW[:, pr, :], 0.0, ALU.mult, ALU.add),
                     w=[CL[:]], r=[LW[:], E["cst"][:]])
            yield
            if d == 0:
                tot = v64(CL[:])[:, :, :, 63:64]
            else:
                totf = v64(CL[:])[:, :, :, 63:64].to_broadcast([128, 4, 4, 64])
                kb.tt(v64(F2[:]), totf, v64(CL[:]), ALU.subtract)
                kb.tt(CL[:], F2[:], LW[:], ALU.add)
                tot = v64(CL[:])[:, :, :, 0:1]
            kb.act(GAM[:].unsqueeze(3), tot, AF.Exp)
            yield
            kb.tt(KK[:], K[:], bc4(O_KK), ALU.mult, eng="pool")
            kb.tt(F1[:], KK[:], KK[:], ALU.mult, eng="pool")
            for hf in range(2):
                ps = FPS()
                kb.mm(ps[:], BDm, F1[:, 2 * hf:2 * hf + 2, :])
                kb.rsqrt(F2[:, 2 * hf:2 * hf + 2, :], ps[:].rearrange("p (a t) -> p a t", a=2), 1e-24)
            yield
            kb.tt(KK[:], KK[:], F2[:], ALU.mult)
            kb.tt(F1[:], AA[:], bc4(O_KA), ALU.mult, eng="pool")
            kb.tt(F1[:], F1[:], bc4(O_OMKA), ALU.add, eng="pool")
            kb.tt(KM[:], K[:], F1[:], ALU.mult)
            kb.tt(BB[:], KK[:], AA[:], ALU.mult, eng="pool")
            yield
            kb.tt(F1[:], R[:], KM[:], ALU.mult, eng="pool")
            kb.tt(F1[:], F1[:], bc4(O_RK), ALU.mult, eng="pool")
            for hf in range(2):
                ps = FPS()
                kb.mm(ps[:], BDm, F1[:, 2 * hf:2 * hf + 2, :])
                kb.tt(BON[:, 2 * hf:2 * hf + 2, :], ps[:].rearrange("p (a t) -> p a t", a=2), V[:, 2 * hf:2 * hf + 2, :], ALU.mult)
            if d == 0:
                kb.dma(BONS.ap()[:, :, p0:p0 + NP_], BON[:])
            else:
                kb.dma(F2[:], BONS.ap()[:, :, p0:p0 + NP_])
                kb.tt(BON[:], BON[:], F2[:], ALU.add)
            yield
            kb.tt(F1[:], CL[:], LW[:], ALU.subtract, eng="pool")
            kb.act(FE[:], F1[:], AF.Exp)
            kb.stt(AR[:, :, :, 0:64], v64(KK[:]), -1.0, v64(FE[:]), ALU.mult, ALU.mult)
            yield
            kb.act(FE[:], CL[:], AF.Exp)
            kb.tt(AR[:, :, :, 64:128], v64(R[:]), v64(FE[:]), ALU.mult)
            yield
            kb.act(FE[:], CL[:], AF.Exp, scale=-1.0)
            kb.tt(Bt[:], BB[:], FE[:], ALU.mult)
            kb.tt(Kt[:], KM[:], FE[:], ALU.mult, eng="pool")
            yield
            kb.tt(v64(F1[:]), tot.to_broadcast([128, 4, 4, 64]), v64(CL[:]), ALU.subtract)
            kb.act(FE[:], F1[:], AF.Exp)
            kb.tt(Bh[:], BB[:], FE[:], ALU.mult)
            kb.tt(Kh[:], KM[:], FE[:], ALU.mult, eng="pool")
            yield
            for (src, dst) in ((Bh, BhT), (Kh, KhT), (V, VT)):
                for tb in range(2):
                    ps = FPS()
                    for pr in range(4):
                        kb.tr(ps[:, pr * 128:(pr + 1) * 128], src[:, pr, tb * 128:(tb + 1) * 128], ident)
                    kb.cp(dst[:, tb, :], ps[:], eng=("act" if tb else "dve"))
                    yield

        def pairs(it, b, step):
            d, pc = it["d"], it["pc"]
            AR, Bt, Kt, BhT, KhT, VT, GAM, YP = ARs[b], Bts[b], Kts[b], BhTs[b], KhTs[b], VTs[b], GAMs[b], YPs[b]
            if d == 1:
                p0_ = it["t0"] + pc * NP_
                kb.dma(YP[:], YS.ap()[:, :, p0_:p0_ + NP_])
            if it["first"]:
                sic[0] = 0
                if it["latent"]:
                    kb.dma(ST[0][:], E["st_in"].ap()[l, d].rearrange("p (a v) -> p a v", a=4))
                else:
                    kb.memset(ST[0][:], 0.0)
            mask4 = E["maskF4"] if d == 0 else E["maskB4"]
            maskT8 = E["maskFT8"] if d == 0 else E["maskBT8"]
            for tb in (range(2) if d == 0 else range(1, -1, -1)):
                PA = [P[0], P[1]]; PB = [P[2], P[3]]; PC = [P[4], P[5]]
                for ci in range(2):
                    c = 2 * tb + ci; cb = ci * 64; cs = slice(c * 64, (c + 1) * 64)
                    for h in range(8):
                        par = h % 2; hb = par * 64; pr = h // 2
                        lastm = (h >= 6)
                        kb.mm(PA[par][cb:cb + 64, pr * 128:(pr + 1) * 128], Bt[hb:hb + 64, pr, cs],
                              AR[hb:hb + 64, pr, c, :], tile_position=(hb, cb), inc=lastm)
                        kb.mm(PB[par][cb:cb + 64, pr * 128:(pr + 1) * 128], Kt[hb:hb + 64, pr, cs],
                              AR[hb:hb + 64, pr, c, :], tile_position=(hb, cb), inc=lastm)
                        kb.mm(PC[par][cb:cb + 64, pr * 64:(pr + 1) * 64], AR[hb:hb + 64, pr, c, 0:64],
                              Bt[hb:hb + 64, pr, cs], tile_position=(hb, cb), inc=lastm)
                m4 = mask4.rearrange("p (h t) -> p h t", h=4)
                for par in range(2):
                    kb.tt(NM[:, par:8:2, :], PA[par][:].rearrange("p (h t) -> p h t", h=4), m4, ALU.mult)
                    kb.tt(KMt[:, par:8:2, :], PB[par][:].rearrange("p (h t) -> p h t", h=4), m4, ALU.mult)
                    kb.tt(NTb[0][:, par:8:2, :], PC[par][:, 0:256].rearrange("p (h t) -> p h t", h=4),
                          maskT8[:, 0:256].rearrange("p (h t) -> p h t", h=4), ALU.mult)
                kb.cp(Nb[0][:], NM[:, :, 0:64], eng="pool")
                kb.tt(TTb[0][:], Nb[0][:], E["ident8"].rearrange("p (h t) -> p h t", h=8), ALU.add, eng="pool")
                step()
                cur = 0
                for lev in range(5):
                    nx = 1 - cur
                    Nc, NTc, TTc = Nb[cur], NTb[cur], TTb[cur]
                    PNT, PN, PTt = P[0], P[1], P[2]
                    for ci in range(2):
                        cb = ci * 64
                        for h in range(8):
                            hs = slice(h * 64, (h + 1) * 64)
                            kb.mm(PNT[cb:cb + 64, hs], Nc[cb:cb + 64, h, :], NTc[cb:cb + 64, h, :], tile_position=(cb, cb), inc=(h == 7))
                            if lev < 4:
                                kb.mm(PN[cb:cb + 64, hs], NTc[cb:cb + 64, h, :], Nc[cb:cb + 64, h, :], tile_position=(cb, cb), inc=(h == 7))
                    kb.cp(NTb[nx][:], PNT[:].rearrange("p (h t) -> p h t", h=8), eng="act")
                    if lev < 4:
                        kb.cp(Nb[nx][:], PN[:].rearrange("p (h t) -> p h t", h=8), eng="dve")
                    for ci in range(2):
                        cb = ci * 64
                        for h in range(8):
                            hs = slice(h * 64, (h + 1) * 64)
                            kb.mm(PTt[cb:cb + 64, hs], NTb[nx][cb:cb + 64, h, :], TTc[cb:cb + 64, h, :], tile_position=(cb, cb), inc=(h == 7))
                    kb.tt(TTb[nx][:], TTc[:], PTt[:].rearrange("p (h t) -> p h t", h=8), ALU.add)
                    cur = nx
                    step()
                TT = TTb[cur]
                PX = P[5]
                for ci in range(2):
                    cb = ci * 64
                    for h in range(8):
                        hs = slice(h * 64, (h + 1) * 64)
                        kb.mm(PX[cb:cb + 64, hs], KMt[cb:cb + 64, h, 0:64], VT[cb:cb + 64, tb, hs], tile_position=(cb, cb), inc=(h == 7))
                kb.cp(RHS0[:], PX[:], eng="act")
                step()
                for ci in ((0, 1) if d == 0 else (1, 0)):
                    c = 2 * tb + ci; cb = ci * 64
                    Sc = ST[sic[0] % 2]; Sn = ST[(sic[0] + 1) % 2]
                    sic[0] += 1
                    PR, PU, PYa, PYb, PSn = P[3], P[4], P[5], P[0], P[1]
                    for h in (0, 2, 4, 6, 1, 3, 5, 7):
                        hb = (h % 2) * 64; pr = h // 2; hs = slice(h * 64, (h + 1) * 64)
                        kb.mm(PR[cb:cb + 64, hs], AR[hb:hb + 64, pr, c, 0:64], Sc[hb:hb + 64, pr, :], tile_position=(hb, cb), inc=(h >= 6))
                    kb.tt(RHS[cb:cb + 64, :], PR[cb:cb + 64, :], RHS0[cb:cb + 64, :], ALU.add)
                    step()
                    for h in range(8):
                        hs = slice(h * 64, (h + 1) * 64)
                        kb.mm(PU[cb:cb + 64, hs], TT[cb:cb + 64, h, :], RHS[cb:cb + 64, hs], tile_position=(cb, cb), inc=(h == 7))
                    kb.cp(U[cb:cb + 64, :], PU[cb:cb + 64, :], eng="act")
                    step()
                    for h in (0, 2, 4, 6, 1, 3, 5, 7):
                        hb = (h % 2) * 64; pr = h // 2
                        ys = slice(pr * 64, (pr + 1) * 64)
                        kb.mm(PYa[hb:hb + 64, ys], Sc[hb:hb + 64, pr, :], AR[hb:hb + 64, pr, c, 64:128], tile_position=(hb, hb), inc=(h >= 6))
                    for h in range(8):
                        hb = (h % 2) * 64; pr = h // 2; hs = slice(h * 64, (h + 1) * 64)
                        ys = slice(pr * 64, (pr + 1) * 64)
                        kb.mm(PSn[hb:hb + 64, ys], BhT[cb:cb + 64, tb, hs], U[cb:cb + 64, hs],
                              start=True, stop=False, tile_position=(cb, hb), inc=False)
                        kb.mm(PSn[hb:hb + 64, ys], KhT[cb:cb + 64, tb, hs], VT[cb:cb + 64, tb, hs],
                              start=False, stop=True, tile_position=(cb, hb), inc=(h >= 6))
                    kb.tt(Sn[:], Sc[:], GAM[:, :, c:c + 1].to_broadcast([128, 4, 64]), ALU.mult, eng="pool")
                    kb.tt(Sn[:], Sn[:], PSn[:, 0:256].rearrange("p (a t) -> p a t", a=4), ALU.add)
                    for h in range(8):
                        hb = (h % 2) * 64; pr = h // 2; hs = slice(h * 64, (h + 1) * 64)
                        ys = slice(pr * 64, (pr + 1) * 64)
                        kb.mm(PYb[hb:hb + 64, ys], U[cb:cb + 64, hs], NM[cb:cb + 64, h, 64:128],
                              start=True, stop=False, tile_position=(cb, hb), inc=False)
                        kb.mm(PYb[hb:hb + 64, ys], VT[cb:cb + 64, tb, hs], KMt[cb:cb + 64, h, 64:128],
                              start=False, stop=True, tile_position=(cb, hb), inc=(h >= 6))
                    ytk = slice(c * 64, (c + 1) * 64)
                    pya = PYa[:, 0:256].rearrange("p (a t) -> p a t", a=4)
                    pyb = PYb[:, 0:256].rearrange("p (a t) -> p a t", a=4)
                    if d == 0:
                        kb.cp(YP[:, :, ytk], pya, eng="act")
                    else:
                        kb.tt(YP[:, :, ytk], YP[:, :, ytk], pya, ALU.add)
                    kb.tt(YP[:, :, ytk], YP[:, :, ytk], pyb, ALU.add)
                    step()
            if it["last"] and not it["latent"]:
                kb.dma(E["ns_out"].ap()[it["sidx"], l, d].rearrange("p (a v) -> p a v", a=4), ST[sic[0] % 2][:])

        def post(it, b):
            d, pc = it["d"], it["pc"]
            p0 = it["t0"] + pc * NP_
            YP, BON = YPs[b], BONs[b]
            if d == 0:
                kb.dma(YS.ap()[:, :, p0:p0 + NP_], YP[:])
                return
            for hf in range(2):
                ps = FPS()
                kb.mm(ps[:], BD64, YP[:, 2 * hf:2 * hf + 2, :])
                kb.tt(F1[:, 2 * hf:2 * hf + 2, :], YP[:, 2 * hf:2 * hf + 2, :], ps[:].rearrange("p (a t) -> p a t", a=2), ALU.subtract)
                yield
            kb.tt(F2[:], F1[:], F1[:], ALU.mult, eng="pool")
            yield
            for hf in range(2):
                ps = FPS()
                kb.mm(ps[:], BD64, F2[:, 2 * hf:2 * hf + 2, :])
                kb.rsqrt(FE[:, 2 * hf:2 * hf + 2, :], ps[:].rearrange("p (a t) -> p a t", a=2), 64e-5)
                yield
            kb.tt(F1[:], F1[:], FE[:], ALU.mult)
            kb.tt(F1[:], F1[:], bc4(O_LXG), ALU.mult, eng="pool")
            yield
            kb.tt(F1[:], F1[:], bc4(O_LXB), ALU.add, eng="pool")
            kb.tt(F1[:], F1[:], BON[:], ALU.add)
            kb.dma(GD[:], RKV.ap()[:, 14, p0:p0 + NP_])
            kb.act(SGD[:], GD[:], AF.Sigmoid)
            yield
            for pr in range(4):
                ps = FPS()
                kb.mm(ps[:, 0:NP_], G2[:, pr * 128:(pr + 1) * 128], SGD[:])
                kb.tt(OC[:, pr, :], F1[:, pr, :], ps[:, 0:NP_], ALU.mult)
                yield
            kb.dma(BRC.ap()[:, :, p0:p0 + NP_], OC[:])

        g0 = front(items[0], 0)
        for _ in g0:
            pass
        pend = [None]
        for i, it in enumerate(items):
            b = i % 2
            nxt = front(items[i + 1], 1 - b) if i + 1 < len(items) else None

            def step(nxt=nxt):
                if pend[0] is not None:
                    try:
                        next(pend[0])
                        return
                    except StopIteration:
                        pend[0] = None
                if nxt is not None:
                    next(nxt, None)

            pairs(it, b, step)
            if pend[0] is not None:
                for _ in pend[0]:
                    pass
                pend[0] = None
            if nxt is not None:
                for _ in nxt:
                    pass
            pg_ = post(it, b)
            if it["d"] == 0:
                for _ in pg_:
                    pass
            else:
                pend[0] = pg_
        if pend[0] is not None:
            for _ in pend[0]:
                pass


def merge_ffn(kb, l, E):
    P = E["P"]; pp = E["pp"][l]; modT = E["modT"][l]
    XTi = E["XT"][l % 2]; XTo = E["XT"][(l + 1) % 2]
    XM = E["XM"]
    BRA = E["BRA"]; BRB = E["BRB"]; BRC = E["BRC"]; onesD = E["onesD"]
    winv = E["w_in"].ap()[l].rearrange("(kc p) n -> p kc n", p=128)
    pk = [0]

    def PS(lo=0, hi=8):
        i = lo + pk[0] % (hi - lo)
        pk[0] += 1
        return P[i]

    def lnorm(Y, sqs, sc3, og, ob, n):
        pm = PS(0, 2); pq = PS(2, 4)
        for c in range(8):
            kb.mm(pm[:, 0:n], onesD, Y[:, c, :], start=(c == 0), stop=(c == 7), inc=(c == 7))
        for c in range(8):
            sq = sqs[c % 2]
            kb.act(sq[:, 0:n], Y[:, c, :], AF.Square)
            kb.mm(pq[:, 0:n], onesD, sq[:, 0:n], start=(c == 0), stop=(c == 7))
        mean = sc3[:, 0, 0:n]; rstd = sc3[:, 1, 0:n]; tmp = sc3[:, 2, 0:n]
        kb.cp(mean, pm[:, 0:n], eng="act")
        kb.tt(tmp, mean, mean, ALU.mult, eng="pool")
        kb.tt(tmp, pq[:, 0:n], tmp, ALU.subtract)
        kb.rsqrt(rstd, tmp, 1e-5)
        for c in range(8):
            e = "dve" if c % 2 else "pool"
            kb.tt(Y[:, c, :], Y[:, c, :], mean, ALU.subtract, eng=e)
            kb.tt(Y[:, c, :], Y[:, c, :], rstd, ALU.mult, eng=e)
            kb.ts(Y[:, c, :], Y[:, c, :], pp[:, og + c:og + c + 1], ALU.mult, pp[:, ob + c:ob + c + 1], ALU.add, eng=e)

    noC = "noC" in E["dbg"]
    with kb.scope():
        WG = kb.sb("mWG", [128, 8, 3072], BF16)
        WB = kb.sb("mWB", [128, 8, D], BF16)
        wbrv = E["w_br"].ap()[l]
        WBB = kb.sb("mWBB", [64, 8, D], BF16)
        WO = kb.sb("mWO", [128, 8, D], BF16)
        wov_ = E["w_out"].ap()[l].rearrange("(kc p) n -> p kc n", p=128)
        for hf_ in range(2):
            hsl = slice(hf_ * 512, (hf_ + 1) * 512)
            for j in range(3):
                i = j * 2 + hf_
                kb.dma((WG[:, :, i * 512:(i + 1) * 512], i), winv[:, :, C_GL + i * 512:C_GL + (i + 1) * 512], q="pool")
            kb.dma((WB[:, 0:4, hsl], hf_), wbrv[0].rearrange("(cc p) d -> p cc d", p=128)[:, :, hsl], q="pool")
            kb.dma((WB[:, 4:8, hsl], 2 + hf_), wbrv[2].rearrange("(cc p) d -> p cc d", p=128)[:, :, hsl], q="pool")
            kb.dma((WBB[:, :, hsl], hf_), wbrv[1].rearrange("(h p) d -> p h d", p=64)[:, :, hsl], q="pool")
        for hf_ in range(2):
            hsl = slice(hf_ * 512, (hf_ + 1) * 512)
            kb.dma((WO[:, :, hsl], hf_), wov_[:, :, hsl], q="pool")
        XLs = [kb.sb("mXL%d" % i, [128, 8, 512]) for i in range(2)]
        HT = kb.sb("mHT", [128, 8, 512], BF16)
        OA = kb.sb("mOA", [128, 4, 512], BF16); OB = kb.sb("mOB", [64, 8, 512], BF16); OCt = kb.sb("mOC", [128, 4, 512], BF16)
        MI = kb.sb("mMI", [128, 8, 512], BF16)
        sqs = [kb.sb("msq%d" % i, [128, 512]) for i in range(2)]
        sc3 = kb.sb("msc3", [128, 3, 512])
        gts = [kb.sb("mgt%d" % i, [128, 512]) for i in range(2)]
        acc = kb.sb("macc", [128, 512])

        def prep(g):
            XL = XLs[g % 2]
            tok = slice(g * 512, (g + 1) * 512)
            kb.dma(XL[:], XTi.ap()[:, :, tok])
            kb.dma(OA[:], BRA.ap()[:, :, tok]); kb.dma(OB[:], BRB.ap()[:, :, tok])
            if not noC:
                kb.dma(OCt[:], BRC.ap()[:, :, tok])
            sc = mod_cols(modT, 8, g); sh = mod_cols(modT, 0, g)
            for c in range(8):
                kb.ts(HT[:, c, :], XL[:, c, :], sc[c], ALU.mult, sh[c], ALU.add, eng=("dve" if c % 2 else "pool"))

        def mix(g):
            for dc in range(8):
                dcs = slice(dc * 128, (dc + 1) * 128)
                for j in range(3):
                    if j == 2 and noC:
                        continue
                    pg = PS(0, 3); pb = PS(3, 6)
                    for kc in range(8):
                        kb.mm(pg[:], (WG[:, kc, j * D + dc * 128:j * D + (dc + 1) * 128], j * 2 + dc // 4), HT[:, kc, :], start=(kc == 0), stop=(kc == 7), inc=(kc == 7))
                    if j != 1:
                        src = OA if j == 0 else OCt
                        for cc in range(4):
                            kb.mm(pb[:], (WB[:, (j // 2) * 4 + cc, dcs], (j // 2) * 2 + dc // 4), src[:, cc, :], start=(cc == 0), stop=(cc == 3), inc=(cc == 3))
                    else:
                        for h in range(8):
                            kb.mm(pb[:], (WBB[:, h, dcs], dc // 4), OB[:, h, :], start=(h == 0), stop=(h == 7), inc=(h == 7))
                    gt = gts[j % 2]
                    kb.act(gt[:], pg[:], AF.Sigmoid)
                    if j == 0:
                        kb.tt(acc[:], gt[:], pb[:], ALU.mult)
                    else:
                        kb.tt(gt[:], gt[:], pb[:], ALU.mult)
                        kb.tt((MI[:, dc, :] if (j == 2 or (noC and j == 1)) else acc[:]), acc[:], gt[:], ALU.add, eng="pool")

        def wout_ln(g):
            XL = XLs[g % 2]
            g1 = mod_cols(modT, 16, g)
            for dc in range(8):
                ps = PS(6, 8)
                for kc in range(8):
                    kb.mm(ps[:], (WO[:, kc, dc * 128:(dc + 1) * 128], dc // 4), MI[:, kc, :], start=(kc == 0), stop=(kc == 7), inc=(kc == 7))
                kb.ts(XL[:, dc, :], XL[:, dc, :], ALPHA, ALU.mult, eng="pool")
                kb.stt(XL[:, dc, :], ps[:], g1[dc], XL[:, dc, :], ALU.mult, ALU.add)
            lnorm(XL, sqs, sc3, O_LN1G, O_LN1B, 512)
            kb.dma(XM.ap()[:, :, g * 512:(g + 1) * 512], XL[:])

        prep(0)
        for g in range(NG):
            mix(g)
            if g + 1 < NG:
                prep(g + 1)
            wout_ln(g)
    chk(kb, "mrg", l)
    with kb.scope():
        WU = kb.sb("fWU", [128, 8, 4096], BF16)
        upv = E["w_up"].ap()[l].rearrange("(kc p) n -> p kc n", p=128)
        WD = kb.sb("fWD", [128, 32, D], BF16)
        dnv = E["w_dn"].ap()[l].rearrange("(fc p) n -> p fc n", p=128)
        for i in range(8):
            kb.dma((WU[:, :, i * 512:(i + 1) * 512], i), upv[:, :, i * 512:(i + 1) * 512], q="pool")
        for i in range(8):
            kb.dma((WD[:, i * 4:(i + 1) * 4, :], i), dnv[:, i * 4:(i + 1) * 4, :], q="pool")
        NF = 256
        NGF = T // NF
        X1s = [kb.sb("fX1%d" % i, [128, 8, NF]) for i in range(2)]
        H2 = kb.sb("fH2", [128, 8, NF], BF16)
        ACTt = kb.sb("fACT", [128, 32, NF], BF16)
        sqs = [kb.sb("fsq%d" % i, [128, 512]) for i in range(2)]
        sc3 = kb.sb("fsc3", [128, 3, 512])
        rl = [kb.sb("frl%d" % i, [128, NF]) for i in range(2)]

        def prep_up(gg):
            g = (gg * NF) // 512
            X1 = X1s[gg % 2]
            kb.dma(X1[:], XM.ap()[:, :, gg * NF:(gg + 1) * NF])
            sc2 = mod_cols(modT, 32, g); sh2 = mod_cols(modT, 24, g)
            for c in range(8):
                kb.ts(H2[:, c, :], X1[:, c, :], sc2[c], ALU.mult, sh2[c], ALU.add, eng=("dve" if c % 2 else "pool"))
            for fc in range(32):
                ps = PS(0, 4)
                for kc in range(8):
                    kb.mm(ps[:, 0:NF], (WU[:, kc, fc * 128:(fc + 1) * 128], fc // 4), H2[:, kc, :], start=(kc == 0), stop=(kc == 7), inc=(kc == 7))
                r_ = rl[fc % 2]
                kb.act(r_[:], ps[:, 0:NF], AF.Relu)
                kb.tt(ACTt[:, fc, :], r_[:], r_[:], ALU.mult, eng=("dve" if fc % 2 else "pool"))

        def down(gg):
            g = (gg * NF) // 512
            X1 = X1s[gg % 2]
            g2 = mod_cols(modT, 40, g)
            for dc in range(8):
                ps = PS(4, 8)
                for fc in range(32):
                    kb.mm(ps[:, 0:NF], (WD[:, fc, dc * 128:(dc + 1) * 128], fc // 4), ACTt[:, fc, :], start=(fc == 0), stop=(fc == 31), inc=(fc == 31))
                kb.ts(X1[:, dc, :], X1[:, dc, :], ALPHA, ALU.mult, eng="pool")
                kb.stt(X1[:, dc, :], ps[:, 0:NF], g2[dc], X1[:, dc, :], ALU.mult, ALU.add)

        def ln_out(gg):
            X1 = X1s[gg % 2]
            lnorm(X1, sqs, sc3, O_LN2G, O_LN2B, NF)
            kb.dma(XTo.ap()[:, :, gg * NF:(gg + 1) * NF], X1[:])

        prep_up(0)
        for gg in range(NGF):
            down(gg)
            if gg + 1 < NGF:
                prep_up(gg + 1)
            ln_out(gg)


def _consts():
    c = np.zeros((128, 9, 512), np.float32)
    r = np.arange(128)
    c[:, 0, 0:128] = np.eye(128)
    bd = (r[:, None] // 64 == r[None, :] // 64).astype(np.float32)
    c[:, 0, 128:256] = bd
    c[:, 0, 256:384] = bd / 64.0
    c[0:64, 0, 384:448] = 1.0 / 64.0
    s = (r % 64)[:, None]
    col = np.arange(128)[None, :]
    mf = np.where(col < 64, s < col, s <= col - 64).astype(np.float32)
    mb = np.where(col < 64, s > col, s >= col - 64).astype(np.float32)
    c[:, 1, :] = np.tile(mf, (1, 4))
    c[:, 2, :] = np.tile(mb, (1, 4))
    t = (r % 64)[:, None]
    sc = np.arange(64)[None, :]
    c[:, 3, :] = np.tile((sc < t).astype(np.float32), (1, 8))
    c[:, 4, :] = np.tile((sc > t).astype(np.float32), (1, 8))
    c[:, 5, :] = np.tile((sc == t).astype(np.float32), (1, 8))
    c[:, 6, :] = (np.arange(512) % 64 != 0).astype(np.float32)[None, :]
    rm = np.zeros((64, 64), np.float32)
    for d in range(64):
        i = d % 32
        if i < 16:
            rm[d + 16, d] = -1.0
        else:
            rm[d - 16, d] = 1.0
    c[0:64, 7, 0:64] = rm
    c[:, 7, 64:192] = 1.0
    c[:, 8, 0:128] = 1.0 / 1024.0
    tt = np.arange(2048)
    row = (tt // 64).astype(np.float32)
    colp = (tt % 64).astype(np.float32)
    inv = (10000.0 ** (-np.arange(0, 32, 2, dtype=np.float32) / 32.0)).astype(np.float32)
    rope = np.zeros((64, 2, 2048), np.float32)
    for d in range(64):
        pos = row if d < 32 else colp
        ang = (pos * inv[(d % 32) % 16]).astype(np.float32)
        rope[d, 0] = np.cos(ang)
        rope[d, 1] = np.sin(ang)
    return c, rope


def _colT(v, n):
    return np.ascontiguousarray(np.asarray(v, np.float32).reshape(n, 128).T)


_NC_CACHE = {}


def kernel(**inp):
    f = lambda k: np.asarray(inp[k], np.float32)
    n = inp.get("_ncores", 8)
    cst, rope = _consts()
    pp = np.zeros((2, 128, NPP), np.float32)
    for l in range(2):
        pp[l, :, O_BADA:O_BADA + 48] = _colT(f("b_ada")[l], 48)
        pp[l, :, O_LN1G:O_LN1G + 8] = _colT(f("ln1_g")[l], 8)
        pp[l, :, O_LN1B:O_LN1B + 8] = _colT(f("ln1_b")[l], 8)
        pp[l, :, O_LN2G:O_LN2G + 8] = _colT(f("ln2_g")[l], 8)
        pp[l, :, O_LN2B:O_LN2B + 8] = _colT(f("ln2_b")[l], 8)
        pp[l, 0:64, O_QN] = f("q_norm")[l]
        pp[l, 0:64, O_KN] = f("k_norm")[l]
        for d in range(2):
            pp[l, :, O_MU + d * 13:O_MU + (d + 1) * 13] = _colT(f("rwkv_mu")[l, d], 13)
            pp[l, :, O_W0 + d * 4:O_W0 + (d + 1) * 4] = _colT(f("rwkv_w0")[l, d], 4)
            pp[l, :, O_A0 + d * 4:O_A0 + (d + 1) * 4] = _colT(f("rwkv_a0")[l, d], 4)
        pp[l, :, O_KK:O_KK + 4] = _colT(f("rwkv_k_k")[l], 4)
        pp[l, :, O_KA:O_KA + 4] = _colT(f("rwkv_k_a")[l], 4)
        pp[l, :, O_RK:O_RK + 4] = _colT(f("rwkv_r_k")[l].reshape(512), 4)
        pp[l, :, O_LXG:O_LXG + 4] = _colT(f("rwkv_lnx_g")[l], 4)
        pp[l, :, O_LXB:O_LXB + 4] = _colT(f("rwkv_lnx_b")[l], 4)
    wsT = np.ascontiguousarray(f("sgu_w").transpose(0, 3, 1, 2))
    sgb = np.ascontiguousarray(f("sgu_b").reshape(2, 1, 512))
    lnA = np.ascontiguousarray(np.broadcast_to(
        np.stack([f("sgu_ln_g"), f("sgu_ln_b")], 1)[:, :, None, :], (2, 2, 128, 512)))
    wa2 = np.ascontiguousarray(np.concatenate([f("rwkv_w2"), f("rwkv_a2")], axis=2))
    shared = dict(pp=pp, w_ada=f("w_ada"), w_in=f("w_in"), w_branch=f("w_branch"), w_out=f("w_out"),
                  w_up=f("w_up"), w_down=f("w_down"), wsT=wsT, sgb=sgb, lnA=lnA, wa2=wa2, g2=f("rwkv_g2"),
                  cst=cst, rope=rope)
    xp, xs = f("x_prompt"), f("x_sample")
    ck, cvv, st, c, cctx = f("cache_k"), f("cache_v"), f("state_wkv"), f("c"), f("c_ctx")
    in_maps = []
    for i in range(n):
        m = dict(shared)
        m["x"] = np.ascontiguousarray(np.concatenate([xp[4 * i:4 * i + 4].reshape(1024, D), xs[i]], 0))
        m["ck"] = np.ascontiguousarray(ck[i].reshape(2, 512, 128))
        m["cvv"] = np.ascontiguousarray(cvv[i].reshape(2, 512, 128))
        s_ = st[i].reshape(2, 2, 4, 2, 64, 64)
        m["st0"] = np.ascontiguousarray(s_.transpose(0, 1, 3, 5, 2, 4).reshape(2, 2, 128, 256))
        cvec = np.stack([cctx, c[i]], -1)
        m["cvec"] = np.ascontiguousarray(cvec.reshape(8, 128, 2).transpose(1, 0, 2))
        in_maps.append(m)
    if inp.get("_prep_only"):
        return in_maps
    if "nc" not in _NC_CACHE:
        _NC_CACHE["nc"] = build()
    kb = _NC_CACHE["nc"]
    res = run_bass_kernel_spmd(kb.nc, in_maps, core_ids=list(range(n)))
    yp = np.zeros((32, 256, D), np.float32)
    ys = np.zeros((8, 2048, D), np.float32)
    nk = np.zeros((32, 2, 256, 2, 64), np.float32)
    nv = np.zeros((32, 2, 256, 2, 64), np.float32)
    ns = np.zeros((32, 2, 2, 8, 64, 64), np.float32)
    for i in range(n):
        r = res.results[i]
        yp[4 * i:4 * i + 4] = r["y"][:1024].reshape(4, 256, D)
        ys[i] = r["y"][1024:]
        nk[4 * i:4 * i + 4] = r["nk"].reshape(2, 4, 256, 2, 64).transpose(1, 0, 2, 3, 4)
        nv[4 * i:4 * i + 4] = r["nv"].reshape(2, 4, 256, 2, 64).transpose(1, 0, 2, 3, 4)
        s_ = r["ns"].reshape(4, 2, 2, 2, 64, 4, 64)
        ns[4 * i:4 * i + 4] = s_.transpose(0, 1, 2, 5, 3, 6, 4).reshape(4, 2, 2, 8, 64, 64)
    return (yp, ys, nk, nv, ns)
```

```python
import numpy as np
from contextlib import ExitStack
import concourse.bass as bass
import concourse.mybir as mybir
from concourse.bass_utils import run_bass_kernel_spmd

F32 = mybir.dt.float32
F32R = mybir.dt.float32r
BF16 = mybir.dt.bfloat16
AF = mybir.ActivationFunctionType
ALU = mybir.AluOpType
AX = mybir.AxisListType


class KB:
    def __init__(self, n_dma_sems=12):
        self.nc = bass.Bass("TRN2", target_bir_lowering=False)
        nc = self.nc
        self.es = ExitStack()
        self.es_root = self.es
        self.eng = {"pe": nc.tensor, "dve": nc.vector, "act": nc.scalar, "pool": nc.gpsimd, "sp": nc.sync}
        self.sem = {}
        self.cnt = {}
        self.clock = {}
        self.snap = {}
        for e in self.eng:
            self.sem[e] = self.es.enter_context(nc.semaphore("s_" + e))
            self.cnt[e] = 0
            self.clock[e] = {}
            self.snap[e] = {}
        self.dq = {}
        for q in ("sp", "pool", "act"):
            ids = []
            for i in range(n_dma_sems):
                sid = "d_%s_%d" % (q, i)
                self.sem[sid] = self.es.enter_context(nc.semaphore(sid))
                self.cnt[sid] = 0
                self.snap[sid] = {}
                ids.append(sid)
            self.dq[q] = [ids, 0]
        self.EPOCH = 60000
        self.epsem = {e: {0: self.sem[e]} for e in self.eng}
        self.lastw = {}
        self.readers = {}
        self.n_ins = 0
        self.n_wait = 0
        self.uid = 0
        self.ps_row = {}
        self.pending = {}
        self.drain_mode = 0
        self.last_tp = None

    def sb(self, name, shape, dt=F32):
        self.uid += 1
        name = "%s_%d" % (name, self.uid)
        return self.es.enter_context(self.nc.sbuf_tensor("s_" + name, list(shape), dt))

    def ps(self, name, shape, dt=F32):
        return self.es.enter_context(self.nc.psum_tensor("p_" + name, list(shape), dt))

    def dram(self, name, shape, dt=F32, kind="Internal"):
        return self.nc.dram_tensor(name, list(shape), dt, kind=kind)

    def _semv(self, f, c):
        if f not in self.eng:
            return self.sem[f], c
        ep = (c - 1) // self.EPOCH
        if ep not in self.epsem[f]:
            self.epsem[f][ep] = self.es_root.enter_context(self.nc.semaphore("s_%s_e%d" % (f, ep)))
        return self.epsem[f][ep], c - ep * self.EPOCH

    @staticmethod
    def _key(x):
        if isinstance(x, tuple):
            return (x[0].tensor.name, x[1])
        return (x.tensor.name, None)

    @staticmethod
    def _ap(x):
        return x[0] if isinstance(x, tuple) else x

    def _deps(self, eng, reads, writes):
        need = {}

        def add(f, c, same_ok):
            if f == eng and same_ok:
                return
            if c > need.get(f, 0):
                need[f] = c

        for k in reads:
            lw = self.lastw.get(k)
            if lw:
                add(lw[0], lw[1], False)
            if k[0].startswith("p_"):
                for f, c in self.readers.get(k, {}).items():
                    add(f, c, True)
        for k in writes:
            lw = self.lastw.get(k)
            if lw:
                add(lw[0], lw[1], True)
            for f, c in self.readers.get(k, {}).items():
                add(f, c, True)
        clk = self.clock[eng]
        e = self.eng[eng]
        for f, c in need.items():
            if clk.get(f, 0) >= c:
                continue
            if c > self.cnt[f]:
                raise RuntimeError("wait on unmaterialised count %s %d > %d" % (f, c, self.cnt[f]))
            sm, val = self._semv(f, c)
            e.wait_ge(sm, val)
            self.n_wait += 1
            for g, v in self.snap[f][c].items():
                if v > clk.get(g, 0):
                    clk[g] = v
            clk[f] = max(clk.get(f, 0), c)

    def _record(self, who, c, reads, writes):
        for k in reads:
            self.readers.setdefault(k, {})[who] = c
        for k in writes:
            self.lastw[k] = (who, c)
            self.readers[k] = {}

    def I(self, eng, fn, w=(), r=(), inc=True):
        rk = [self._key(x) for x in r]
        wk = [self._key(x) for x in w]
        self._deps(eng, rk, wk)
        ins = fn(self.eng[eng])
        self.n_ins += 1
        if not inc:
            self._record(eng, self.cnt[eng] + 1, rk, wk)
            self.pending[eng] = True
            return ins
        self.pending[eng] = False
        self.cnt[eng] += 1
        c = self.cnt[eng]
        ins.then_inc(self._semv(eng, c)[0], 1)
        s = dict(self.clock[eng])
        s[eng] = c
        self.snap[eng][c] = s
        self._record(eng, c, rk, wk)
        return ins

    def dma(self, out, in_, q="sp", **kw):
        rk = [self._key(in_)]
        wk = [self._key(out)]
        self._deps(q, rk, wk)
        ids, pos = self.dq[q]
        sid = ids[pos % len(ids)]
        self.dq[q][1] = pos + 1
        clk = self.clock[q]
        prev = self.cnt[sid]
        if prev and clk.get(sid, 0) < prev:
            self.eng[q].wait_ge(self.sem[sid], prev)
            for g, v in self.snap[sid][prev].items():
                if v > clk.get(g, 0):
                    clk[g] = v
            clk[sid] = prev
        ins = self.eng[q].dma_start(out=self._ap(out), in_=self._ap(in_), **kw)
        c = prev + 16
        self.cnt[sid] = c
        ins.then_inc(self.sem[sid], 16)
        s = dict(clk)
        s[sid] = c
        self.snap[sid][c] = s
        self._record(sid, c, rk, wk)
        self.n_ins += 1
        return ins

    def finish(self):
        assert not any(self.pending.values()), self.pending
        e = "sp"
        clk = self.clock[e]
        for f in self.sem:
            c = self.cnt[f]
            if c and clk.get(f, 0) < c and f != e:
                sm, val = self._semv(f, c)
                self.eng[e].wait_ge(sm, val)
        self.es.close()

    def mm(self, out, lhsT, rhs, start=True, stop=True, inc=True, **kw):
        a = self._ap
        lt = a(lhsT)
        row = lt.base_partition() if lt.partition_size() < 128 else -1
        okey = self._key(out)
        prev = self.ps_row.get(okey)
        if prev is not None and prev[0] != row and self.clock["pe"].get("pe", 0) < prev[1]:
            if prev[1] > self.cnt["pe"]:
                raise RuntimeError("PE row switch on bank %s needs a materialised count" % (okey,))
            sm, val = self._semv("pe", prev[1])
            self.eng["pe"].wait_ge(sm, val)
            self.clock["pe"]["pe"] = prev[1]
            self.n_wait += 1
        self.ps_row[okey] = (row, self.cnt["pe"] + 1)
        tp = kw.get("tile_position")
        if self.drain_mode and tp is not None and self.cnt["pe"] and self.clock["pe"].get("pe", 0) < self.cnt["pe"]:
            if self.drain_mode == 1 or (self.drain_mode == 2 and tp != self.last_tp):
                sm, val = self._semv("pe", self.cnt["pe"])
                self.eng["pe"].wait_ge(sm, val)
                self.clock["pe"]["pe"] = self.cnt["pe"]
        self.last_tp = tp
        return self.I("pe", lambda e: e.matmul(a(out), a(lhsT), a(rhs), start=start, stop=stop, **kw),
                      w=[out], r=[lhsT, rhs], inc=inc)

    def tr(self, out, in_, ident):
        a = self._ap
        return self.I("pe", lambda e: e.transpose(a(out), a(in_), a(ident)), w=[out], r=[in_, ident])

    def act(self, out, in_, func, bias=None, scale=1.0, accum=None, eng="act"):
        a = self._ap
        r = [in_]
        kw = {}
        if bias is not None:
            if isinstance(bias, (int, float)):
                kw["bias"] = float(bias)
            else:
                kw["bias"] = a(bias)
                r.append(bias)
        if isinstance(scale, (int, float)):
            kw["scale"] = float(scale)
        else:
            kw["scale"] = a(scale)
            r.append(scale)
        w = [out]
        if accum is not None:
            kw["accum_out"] = a(accum)
            w.append(accum)
        return self.I(eng, lambda e: e.activation(a(out), a(in_), func, **kw), w=w, r=r)

    def tt(self, out, x, y, op, eng="dve"):
        a = self._ap
        return self.I(eng, lambda e: e.tensor_tensor(a(out), a(x), a(y), op), w=[out], r=[x, y])

    def ts(self, out, x, s1, op0, s2=None, op1=None, eng="dve", accum=None):
        a = self._ap
        r = [x]
        v1 = s1
        if not isinstance(s1, (int, float)):
            r.append(s1)
            v1 = a(s1)
        v2 = s2
        if s2 is not None and not isinstance(s2, (int, float)):
            r.append(s2)
            v2 = a(s2)
        kw = {}
        w = [out]
        if op1 is not None:
            kw["op1"] = op1
        if accum is not None:
            kw["accum_out"] = a(accum)
            w.append(accum)
        return self.I(eng, lambda e: e.tensor_scalar(a(out), a(x), v1, v2, op0, **kw), w=w, r=r)

    def stt(self, out, x, s, y, op0, op1, eng="dve"):
        a = self._ap
        r = [x, y]
        v = s
        if not isinstance(s, (int, float)):
            r.append(s)
            v = a(s)
        return self.I(eng, lambda e: e.scalar_tensor_tensor(a(out), a(x), v, a(y), op0, op1), w=[out], r=r)

    def cp(self, out, in_, eng="dve"):
        a = self._ap
        if eng == "act":
            return self.I(eng, lambda e: e.copy(a(out), a(in_)), w=[out], r=[in_])
        return self.I(eng, lambda e: e.tensor_copy(a(out), a(in_)), w=[out], r=[in_])

    def memset(self, out, val, eng="pool"):
        a = self._ap
        return self.I(eng, lambda e: e.memset(a(out), val), w=[out])

    def recip(self, out, in_):
        a = self._ap
        return self.I("dve", lambda e: e.reciprocal(a(out), a(in_)), w=[out], r=[in_])

    def rsqrt(self, out, in_, eps):
        self.act(out, in_, AF.Ln, bias=eps)
        self.act(out, out, AF.Exp, scale=-0.5)

    def barrier(self):
        assert not any(self.pending.values()), self.pending
        for e in self.eng:
            clk = self.clock[e]
            for f in self.sem:
                c = self.cnt[f]
                if f == e or not c or clk.get(f, 0) >= c:
                    continue
                sm, val = self._semv(f, c)
                self.eng[e].wait_ge(sm, val)
                clk[f] = c
        full = {f: self.cnt[f] for f in self.sem if self.cnt[f]}
        for e in self.eng:
            for f, c in full.items():
                if f != e:
                    self.clock[e][f] = max(self.clock[e].get(f, 0), c)

    def scope(self):
        kb = self

        class _S:
            def __enter__(s):
                s.old = kb.es
                kb.es = ExitStack()
                return s

            def __exit__(s, *a):
                kb.barrier()
                kb.es.close()
                kb.es = s.old
                return False

        return _S()

T = 3072
NG = 6
D = 1024
NPP = 160
O_BADA, O_LN1G, O_LN1B, O_LN2G, O_LN2B, O_QN, O_KN = 0, 48, 56, 64, 72, 80, 81
O_MU, O_W0, O_A0, O_KK, O_KA, O_RK, O_LXG, O_LXB, O_OMKA, O_OMU = 82, 108, 116, 124, 128, 132, 136, 140, 144, 148
NPP = 176
ALPHA = 4.0 ** 0.25
C_UA, C_VA, C_Q, C_K, C_V, C_R, C_GL = 0, 512, 1024, 1536, 1664, 1792, 3712


class StopBuild(Exception):
    pass


def build(stop_after=None, dbg=()):
    kb = KB()
    kb.stop_after = stop_after
    kb.drain_mode = 1 if "drain1" in dbg else (2 if "drain2" in dbg else 0)
    try:
        _build(kb, stop_after, dbg)
    except StopBuild:
        pass
    kb.finish()
    return kb


def chk(kb, name, l=0):
    if kb.stop_after == (name, l):
        raise StopBuild()


def _build(kb, stop_after, dbg):
    nc = kb.nc
    IN = lambda n, s, dt=F32: kb.dram(n, s, dt, kind="ExternalInput")
    OUT = lambda n, s, dt=F32: kb.dram(n, s, dt, kind="ExternalOutput")
    x_in = IN("x", [T, D])
    ck_in = IN("ck", [2, 512, 128])
    cvv_in = IN("cvv", [2, 512, 128])
    st_in = IN("st0", [2, 2, 128, 256])
    cvec_in = IN("cvec", [128, 8, 2])
    pp_in = IN("pp", [2, 128, NPP])
    w_ada = IN("w_ada", [2, D, 6144])
    w_in = IN("w_in", [2, D, 6784])
    w_br = IN("w_branch", [2, 3, 512, D])
    w_out = IN("w_out", [2, D, D])
    w_up = IN("w_up", [2, D, 4096])
    w_dn = IN("w_down", [2, 4096, D])
    wsT_in = IN("wsT", [2, 128, 4, 128])
    sgb_in = IN("sgb", [2, 1, 512])
    lnA_in = IN("lnA", [2, 2, 128, 512])
    wa2_in = IN("wa2", [2, 2, 128, 512])
    g2_in = IN("g2", [2, 128, 512])
    cst_in = IN("cst", [128, 9, 512])
    rope_in = IN("rope", [64, 2, 2048])
    y_out = OUT("y", [T, D])
    nk_out = OUT("nk", [2, 1024, 128])
    nv_out = OUT("nv", [2, 1024, 128])
    ns_out = OUT("ns", [4, 2, 2, 128, 256])
    XT = [kb.dram("XT%d" % i, [128, 8, T]) for i in range(2)]
    RKV = kb.dram("RKV", [128, 15, T])
    BRA = kb.dram("BRA", [128, 4, T], BF16)
    BRB = kb.dram("BRB", [64, 8, T], BF16)
    BRC = kb.dram("BRC", [128, 4, T], BF16)
    BONS = kb.dram("BONS", [128, 4, T])
    XM = kb.dram("XM", [128, 8, T])
    YS = kb.dram("YS", [128, 4, T])
    dbg_out = {}
    if "dump" in dbg:
        dbg_out["d1"] = OUT("dbg1", [128, 32, 256])
        dbg_out["d2"] = OUT("dbg2", [128, 14, 256])
        dbg_out["d3"] = OUT("dbg3", [128, 6, 1024])

    P = [kb.ps("P%d" % i, [128, 512]) for i in range(8)]
    cst = kb.sb("cst", [128, 9, 512])
    kb.dma(cst[:], cst_in.ap()[:, :, :])
    ident = cst[:, 0, 0:128]
    BDm = cst[:, 0, 128:256]
    BD64 = cst[:, 0, 256:384]
    ones64 = cst[0:64, 0, 384:448]
    maskF4, maskB4, maskFT8, maskBT8, ident8, rst = (cst[:, i, :] for i in range(1, 7))
    rm_f = cst[0:64, 7, 0:64]
    ones_row = cst[:, 7, 64:192]
    onesD = cst[:, 8, 0:128]
    rm_b = kb.sb("rm_b", [64, 64], BF16)
    kb.cp(rm_b[:], rm_f, eng="dve")
    ident_b = kb.sb("ident_b", [128, 128], BF16)
    kb.cp(ident_b[:], ident, eng="dve")
    pp = [kb.sb("pp%d" % l, [128, NPP]) for l in range(2)]
    for l in range(2):
        kb.dma(pp[l][:], pp_in.ap()[l])
    modT = [kb.sb("modT%d" % l, [128, 48, 2]) for l in range(2)]

    with kb.scope():
        cv = kb.sb("cv", [128, 8, 2])
        sl = kb.sb("sl", [128, 8, 2])
        kb.dma(cv[:], cvec_in.ap()[:, :, :])
        kb.act(sl[:], cv[:], AF.Silu)
        wts = [kb.sb("wadat%d" % i, [128, 8, 512]) for i in range(2)]
        for l in range(2):
            wv = w_ada.ap()[l].rearrange("(kc p) n -> p kc n", p=128)
            for nb in range(12):
                wt = wts[nb % 2]
                kb.dma(wt[:], wv[:, :, nb * 512:(nb + 1) * 512])
                ps = P[nb % 2]
                for j in range(4):
                    for kc in range(8):
                        kb.mm(ps[:, 2 * j:2 * j + 2], wt[:, kc, j * 128:(j + 1) * 128], sl[:, kc, :],
                              start=(kc == 0), stop=(kc == 7), inc=(kc == 7))
                for j in range(4):
                    c = nb * 4 + j
                    kb.ts(modT[l][:, c, :], ps[:, 2 * j:2 * j + 2], pp[l][:, O_BADA + c:O_BADA + c + 1], ALU.add)
            for c0 in (8, 32):
                kb.ts(modT[l][:, c0:c0 + 8, :], modT[l][:, c0:c0 + 8, :], 1.0, ALU.add, eng="pool")
            kb.ts(pp[l][:, O_OMU:O_OMU + 26], pp[l][:, O_MU:O_MU + 26], -1.0, ALU.mult, 1.0, ALU.add, eng="pool")
            kb.ts(pp[l][:, O_OMKA:O_OMKA + 4], pp[l][:, O_KA:O_KA + 4], -1.0, ALU.mult, 1.0, ALU.add, eng="pool")

    chk(kb, "mod")
    with kb.scope():
        xts = [kb.sb("xt_t%d" % i, [128, 8, 512]) for i in range(2)]
        xins = [kb.sb("xin%d" % i, [128, 1024]) for i in range(2)]
        k = 0
        for g in range(NG):
            xt = xts[g % 2]
            for tt in range(4):
                xin = xins[tt % 2]
                kb.dma(xin[:], x_in.ap()[(g * 4 + tt) * 128:(g * 4 + tt + 1) * 128, :])
                for half in range(2):
                    ps = P[k % 4]
                    k += 1
                    for c in range(4):
                        kb.tr(ps[:, c * 128:(c + 1) * 128], xin[:, (half * 4 + c) * 128:(half * 4 + c + 1) * 128], ident)
                    kb.cp(xt[:, half * 4:(half + 1) * 4, tt * 128:(tt + 1) * 128],
                          ps[:].rearrange("p (c t) -> p c t", c=4), eng=("dve" if half else "act"))
            kb.dma(XT[0].ap()[:, :, g * 512:(g + 1) * 512], xt[:])

    chk(kb, "t0")
    for l in range(2):
        layer(kb, l, locals())
        chk(kb, "layer", l)

    if stop_after is None:
        with kb.scope():
            xts = [kb.sb("oxt%d" % i, [128, 8, 512]) for i in range(2)]
            yos = [kb.sb("yo%d" % i, [128, 1024]) for i in range(2)]
            k = 0
            for g in range(NG):
                xt = xts[g % 2]
                kb.dma(xt[:], XT[0].ap()[:, :, g * 512:(g + 1) * 512])
                for tt in range(4):
                    yo = yos[tt % 2]
                    for half in range(2):
                        ps = P[k % 4]
                        k += 1
                        for c in range(4):
                            kb.tr(ps[:, c * 128:(c + 1) * 128], xt[:, half * 4 + c, tt * 128:(tt + 1) * 128], ident)
                        kb.cp(yo[:, half * 512:(half + 1) * 512], ps[:], eng=("dve" if half else "act"))
                    kb.dma(y_out.ap()[(g * 4 + tt) * 128:(g * 4 + tt + 1) * 128, :], yo[:])


def mod_cols(modT_l, base, g):
    v = 0 if g < 2 else 1
    return [modT_l[:, base + c, v:v + 1] for c in range(8)]


def layer(kb, l, E):
    P = E["P"]; pp = E["pp"][l]; modT = E["modT"][l]; cst = E["cst"]
    XTi = E["XT"][l % 2]; XTo = E["XT"][(l + 1) % 2]
    w_in = E["w_in"]; RKV = E["RKV"]; BRA = E["BRA"]; BRB = E["BRB"]; BRC = E["BRC"]
    ident = E["ident"]; ident_b = E["ident_b"]
    winv = w_in.ap()[l].rearrange("(kc p) n -> p kc n", p=128)
    pk = [0]

    def PS(lo=0, hi=8):
        i = lo + pk[0] % (hi - lo)
        pk[0] += 1
        return P[i]

    ek = [0]

    def EV():
        ek[0] += 1
        return "dve" if ek[0] % 2 else "act"

    with kb.scope():
        HT = kb.sb("HT", [128, 8, T], BF16)
        QT = kb.sb("QT", [64, 8, T], BF16)
        KT = kb.sb("KT", [64, 2, T + 512], BF16)
        VS = kb.sb("VS", [128, 28, 2, 65], BF16)
        kb.memset(VS[:], 1.0)
        with kb.scope():
            xls = [kb.sb("xl%d" % i, [128, 8, 512]) for i in range(2)]
            for g in range(NG):
                xl = xls[g % 2]
                kb.dma(xl[:], XTi.ap()[:, :, g * 512:(g + 1) * 512])
                sc = mod_cols(modT, 8, g); sh = mod_cols(modT, 0, g)
                for c in range(8):
                    kb.ts(HT[:, c, g * 512:(g + 1) * 512], xl[:, c, :], sc[c], ALU.mult, sh[c], ALU.add,
                          eng=("dve" if c % 2 else "pool"))
        chk(kb, "ht", l)
        with kb.scope():
            UA = kb.sb("UA", [128, 4, T], BF16)
            W = kb.sb("Wua", [128, 8, 512], BF16)
            kb.dma(W[:], winv[:, :, C_UA:C_UA + 512], q="pool")
            for g in range(NG):
                for j in range(4):
                    ps = PS()
                    for kc in range(8):
                        kb.mm(ps[:], W[:, kc, j * 128:(j + 1) * 128], HT[:, kc, g * 512:(g + 1) * 512],
                              start=(kc == 0), stop=(kc == 7), inc=(kc == 7))
                    kb.cp(UA[:, j, g * 512:(g + 1) * 512], ps[:], eng=EV())
            chk(kb, "ua", l)
            W2 = kb.sb("Wva", [128, 8, 512], BF16)
            kb.dma(W2[:], winv[:, :, C_VA:C_VA + 512], q="pool")
            wsT = kb.sb("wsT", [128, 4, 128], BF16)
            kb.dma(wsT[:], E["wsT_in"].ap()[l], q="pool")
            sgb = kb.sb("sgb", [1, 512])
            kb.dma(sgb[:], E["sgb_in"].ap()[l])
            lnG = kb.sb("lnG", [128, 512]); lnB = kb.sb("lnB", [128, 512])
            kb.dma(lnG[:], E["lnA_in"].ap()[l, 0]); kb.dma(lnB[:], E["lnA_in"].ap()[l, 1])
            ones_row = E["ones_row"]
            sqs = [kb.sb("a_sq%d" % i, [128, 512]) for i in range(2)]
            v0s = [kb.sb("a_v0%d" % i, [128, 512]) for i in range(2)]
            vns = [kb.sb("a_vn%d" % i, [128, 512], BF16) for i in range(2)]
            sts = [kb.sb("a_st%d" % i, [128, 8]) for i in range(2)]
            for tt in range(24):
                ps = PS(0, 4)
                for kc in range(8):
                    kb.mm(ps[:], HT[:, kc, tt * 128:(tt + 1) * 128], W2[:, kc, :], start=(kc == 0), stop=(kc == 7), inc=(kc == 7))
                sq = sqs[tt % 2]; v0 = v0s[tt % 2]; vn = vns[tt % 2]; st = sts[tt % 2]
                cut = [int(x[3:]) for x in E["dbg"] if x.startswith("cut")]
                cut = cut[0] if cut else 99
                kb.act(v0[:], ps[:], AF.Copy)
                kb.act(sq[:], v0[:], AF.Square)
                if cut < 2: continue
                kb.I("dve", lambda e: e.reduce_sum(st[:, 0:1], v0[:], AX.X), w=[st[:]], r=[v0[:]])
                kb.I("dve", lambda e: e.reduce_sum(st[:, 1:2], sq[:], AX.X), w=[st[:]], r=[sq[:], st[:]])
                if cut < 3: continue
                kb.ts(st[:, 2:3], st[:, 0:1], 1.0 / 512, ALU.mult)
                kb.tt(st[:, 3:4], st[:, 2:3], st[:, 2:3], ALU.mult)
                kb.stt(st[:, 4:5], st[:, 1:2], 1.0 / 512, st[:, 3:4], ALU.mult, ALU.subtract)
                if cut < 4: continue
                kb.rsqrt(st[:, 5:6], st[:, 4:5], 1e-5)
                kb.stt(st[:, 6:7], st[:, 2:3], -1.0, st[:, 5:6], ALU.mult, ALU.mult)
                if cut < 5: continue
                kb.act(v0[:], v0[:], AF.Identity, bias=st[:, 6:7], scale=st[:, 5:6])
                if cut < 6: continue
                kb.tt(v0[:], v0[:], lnG[:], ALU.mult, eng="pool")
                kb.tt(vn[:], v0[:], lnB[:], ALU.add, eng="pool")
                if cut < 7: continue
                ps2 = PS(4, 8)
                for j in range(4):
                    nob = "nobias" in E["dbg"]
                    kb.mm(ps2[:, j * 128:(j + 1) * 128], vn[:, j * 128:(j + 1) * 128], wsT[:, j, :], start=True, stop=nob)
                    if not nob:
                        kb.mm(ps2[:, j * 128:(j + 1) * 128], ones_row[0:1, :], sgb[0:1, j * 128:(j + 1) * 128], start=False, stop=True)
                if cut < 8: continue
                kb.tt(UA[:, :, tt * 128:(tt + 1) * 128], UA[:, :, tt * 128:(tt + 1) * 128],
                      ps2[:].rearrange("p (j t) -> p j t", j=4), ALU.mult)
            if "nobra" not in E["dbg"]:
                kb.dma(BRA.ap()[:, :, :], UA[:])
        chk(kb, "va", l)
        with kb.scope():
            Wq = kb.sb("Wq", [128, 8, 512], BF16)
            kb.dma(Wq[:], winv[:, :, C_Q:C_Q + 512], q="pool")
            Wkv = kb.sb("Wkv", [128, 8, 256], BF16)
            kb.dma(Wkv[:], winv[:, :, C_K:C_K + 256], q="pool")
            rope = kb.sb("rope", [64, 2, 2048])
            kb.dma(rope[:], E["rope_in"].ap()[:, :, :])
            sqs = [kb.sb("q_sq%d" % i, [64, 512]) for i in range(2)]
            rss = [kb.sb("q_rs%d" % i, [64, 512]) for i in range(2)]
            qns = [kb.sb("q_qn%d" % i, [64, 512]) for i in range(2)]
            qbs = [kb.sb("q_qb%d" % i, [64, 512], BF16) for i in range(2)]
            t1s = [kb.sb("q_t1%d" % i, [64, 512]) for i in range(2)]
            t2s = [kb.sb("q_t2%d" % i, [64, 512]) for i in range(2)]
            qrs = [kb.sb("q_qr%d" % i, [64, 512]) for i in range(2)]
            kos = [kb.sb("q_ko%d" % i, [128, 4, 128]) for i in range(2)]
            vos = [kb.sb("q_vo%d" % i, [128, 128]) for i in range(2)]
            ones64 = E["ones64"]; rm_b = E["rm_b"]
            it = 0
            for g in range(NG):
                latent = g >= 2
                tok = slice(g * 512, (g + 1) * 512)
                for hh in range(10):
                    isk = hh >= 8
                    h = hh - 8 if isk else hh
                    Wt = Wkv if isk else Wq
                    gcol = pp[0:64, (O_KN if isk else O_QN):(O_KN if isk else O_QN) + 1]
                    i2 = it % 2
                    it += 1
                    sq, rs, qn, qb, t1, t2 = sqs[i2], rss[i2], qns[i2], qbs[i2], t1s[i2], t2s[i2]
                    ps = PS(0, 3)
                    for kc in range(8):
                        kb.mm(ps[0:64, :], Wt[:, kc, h * 64:(h + 1) * 64], HT[:, kc, tok], start=(kc == 0), stop=(kc == 7), inc=(kc == 7))
                    qr = qrs[i2]
                    kb.act(qr[:], ps[0:64, :], AF.Copy)
                    kb.act(sq[:], ps[0:64, :], AF.Square)
                    ps2 = PS(3, 6)
                    kb.mm(ps2[0:64, :], ones64, sq[:])
                    kb.rsqrt(rs[:], ps2[0:64, :], 1e-6)
                    dst = (KT[:, h, tok] if isk else QT[:, h, tok])
                    if (not latent) and (not isk):
                        kb.stt(dst, qr[:], gcol, rs[:], ALU.mult, ALU.mult)
                        continue
                    kb.stt(qn[:], qr[:], gcol, rs[:], ALU.mult, ALU.mult)
                    if not latent:
                        kb.cp(dst, qn[:], eng="pool")
                        ko = kos[g % 2]
                        ps3 = PS(6, 8)
                        for tt in range(4):
                            kb.tr(ps3[:, tt * 64:(tt + 1) * 64], qn[:, tt * 128:(tt + 1) * 128], ident[0:64, 0:64])
                        kb.cp(ko[:, :, h * 64:(h + 1) * 64], ps3[:, 0:256].rearrange("p (t d) -> p t d", t=4), eng="act")
                        if h == 1:
                            kb.dma(E["nk_out"].ap()[l, g * 512:(g + 1) * 512, :].rearrange("(t p) c -> p t c", p=128), ko[:])
                        continue
                    pos = slice((g - 2) * 512, (g - 1) * 512)
                    kb.cp(qb[:], qn[:], eng="pool")
                    ps3 = PS(6, 8)
                    kb.mm(ps3[0:64, :], rm_b[:], qb[:])
                    kb.tt(t1[:], qn[:], rope[:, 0, pos], ALU.mult, eng="pool")
                    kb.tt(t2[:], ps3[0:64, :], rope[:, 1, pos], ALU.mult)
                    kb.tt(dst, t1[:], t2[:], ALU.add, eng="pool")
                for t4 in range(4):
                    tt = g * 4 + t4
                    ps = PS(0, 3)
                    for kc in range(8):
                        kb.mm(ps[:, 0:128], HT[:, kc, tt * 128:(tt + 1) * 128], Wkv[:, kc, 128:256], start=(kc == 0), stop=(kc == 7), inc=(kc == 7))
                    vo = vos[t4 % 2]
                    kb.cp(vo[:], ps[:, 0:128], eng="act")
                    kb.cp(VS[:, tt, :, 0:64], vo[:].rearrange("p (k d) -> p k d", k=2), eng="pool")
                    if not latent:
                        kb.dma(E["nv_out"].ap()[l, tt * 128:(tt + 1) * 128, :], vo[:])
            ckt = kb.sb("ckt", [128, 4, 128]); cvt = kb.sb("cvt", [128, 4, 128])
            kb.dma(ckt[:], E["ck_in"].ap()[l].rearrange("(t p) c -> p t c", p=128))
            kb.dma(cvt[:], E["cvv_in"].ap()[l].rearrange("(t p) c -> p t c", p=128))
            for t4 in range(4):
                for kv in range(2):
                    ps = PS(0, 3)
                    kb.tr(ps[0:64, 0:128], ckt[:, t4, kv * 64:(kv + 1) * 64], ident)
                    kb.cp(KT[:, kv, T + t4 * 128:T + (t4 + 1) * 128], ps[0:64, 0:128], eng=EV())
                kb.cp(VS[:, 24 + t4, :, 0:64], cvt[:, t4, :].rearrange("p (k d) -> p k d", k=2), eng="pool")
        chk(kb, "q", l)
        with kb.scope():
            Ws = [kb.sb("Wr%d" % i, [128, 8, 512], BF16) for i in range(2)]
            stg = [kb.sb("rstg%d" % i, [128, 512]) for i in range(3)]
            k = 0
            for wb in range(4):
                ncol = 512 if wb < 3 else 384
                W = Ws[wb % 2]
                kb.dma(W[:, :, 0:ncol], winv[:, :, C_R + wb * 512:C_R + wb * 512 + ncol], q="pool")
                for bb in range(ncol // 128):
                    b = wb * 4 + bb
                    for g in range(NG):
                        ps = PS()
                        for kc in range(8):
                            kb.mm(ps[:], W[:, kc, bb * 128:(bb + 1) * 128], HT[:, kc, g * 512:(g + 1) * 512],
                                  start=(kc == 0), stop=(kc == 7), inc=(kc == 7))
                        s = stg[k % 3]
                        k += 1
                        kb.cp(s[:], ps[:], eng=EV())
                        kb.dma((RKV.ap()[:, b, g * 512:(g + 1) * 512], (b, g)), s[:])
        chk(kb, "rkv", l)
        with kb.scope():
            pts = [kb.sb("pt%d" % i, [128, 512], BF16) for i in range(3)]
            ons = [kb.sb("on%d" % i, [128, 512]) for i in range(2)]
            rcs = [kb.sb("rc%d" % i, [128, 512]) for i in range(2)]
            obs = [kb.sb("ob%d" % i, [64, 512], BF16) for i in range(2)]
            seqs = [(s * 256, 256, [(s * 256 + i * 128, s * 2 + i) for i in range(2)]) for s in range(4)]
            seqs.append((1024, 2048, [(T + i * 128, 24 + i) for i in range(4)] + [(1024 + i * 128, 8 + i) for i in range(16)]))
            n = 0
            for (t0, L, keys) in seqs:
                for kv in range(2):
                    for qt in range(L // 128):
                        q0 = t0 + qt * 128
                        po = P[6 + n % 2]
                        nk_ = len(keys)
                        pss = [None] * nk_

                        def score(i):
                            ps = PS(0, 4)
                            kb.mm(ps[:], KT[:, kv, keys[i][0]:keys[i][0] + 128], QT[:, 4 * kv:4 * kv + 4, q0:q0 + 128])
                            pss[i] = ps

                        score(0)
                        for i, (kc0, vt) in enumerate(keys):
                            if i + 1 < nk_:
                                score(i + 1)
                            pt = pts[i % 3]
                            kb.act(pt[:], pss[i][:], AF.Exp, scale=0.125)
                            kb.mm(po[0:65, :], VS[:, vt, kv, :], pt[:], start=(i == 0), stop=(i == nk_ - 1))
                        on = ons[n % 2]; rc = rcs[n % 2]; ob = obs[n % 2]
                        kb.cp(on[0:65, :], po[0:65, :], eng="act")
                        kb.recip(rc[64:65, :], on[64:65, :])
                        pb = P[4 + n % 2]
                        kb.mm(pb[0:64, :], E["ones_row"][64:65, 0:64], rc[64:65, :], tile_position=(64, 0))
                        kb.tt(ob[:], on[0:64, :], pb[0:64, :], ALU.mult)
                        kb.dma(BRB.ap()[:, 4 * kv:4 * kv + 4, q0:q0 + 128], ob[:].rearrange("p (g t) -> p g t", g=4))
                        n += 1
    chk(kb, "att", l)
    if "noC" not in E["dbg"]:
        rwkv(kb, l, E)
    chk(kb, "rw", l)
    merge_ffn(kb, l, E)


def mmg(kb, terms):
    order = sorted(range(len(terms)), key=lambda i: (terms[i][4][0], i))
    first, last = {}, {}
    for n, i in enumerate(order):
        r = terms[i][0]
        first.setdefault(r, n)
        last[r] = n
    for n, i in enumerate(order):
        r, out, lhsT, rhs, tp = terms[i]
        kb.mm(out, lhsT, rhs, start=(first[r] == n), stop=(last[r] == n), tile_position=tp)


def rwkv(kb, l, E):
    P = E["P"]; pp = E["pp"][l]; RKV = E["RKV"]; BRC = E["BRC"]; BONS = E["BONS"]; YS = E["YS"]
    ident = E["ident"]; BDm = E["BDm"]; BD64 = E["BD64"]
    NP_ = 256
    fk = [0]

    def FPS():
        fk[0] += 1
        return P[6 + fk[0] % 2]

    def bc4(off, n=NP_):
        return pp[:, off:off + 4].unsqueeze(2).to_broadcast([128, 4, n])

    with kb.scope():
        A = lambda n, s, dt=F32: kb.sb(n, s, dt)
        XR = A("XR", [128, 14, NP_ + 2])
        R = A("rR", [128, 4, NP_]); K = A("rK", [128, 4, NP_]); V = A("rV", [128, 4, NP_]); LO = A("rLO", [128, NP_])
        TW = A("rTW", [128, NP_]); LW = A("rLW", [128, 4, NP_]); AA = A("rAA", [128, 4, NP_])
        KK = A("rKK", [128, 4, NP_])
        CL = A("rCL", [128, 4, NP_]); KM = A("rKM", [128, 4, NP_]); BB = A("rBB", [128, 4, NP_])
        Bh = A("rBh", [128, 4, NP_]); Kh = A("rKh", [128, 4, NP_])
        ARs = [A("rAR%d" % i, [128, 4, 4, 128]) for i in range(2)]
        Bts = [A("rBt%d" % i, [128, 4, NP_]) for i in range(2)]
        Kts = [A("rKt%d" % i, [128, 4, NP_]) for i in range(2)]
        BhTs = [A("rBhT%d" % i, [128, 2, 512]) for i in range(2)]
        KhTs = [A("rKhT%d" % i, [128, 2, 512]) for i in range(2)]
        VTs = [A("rVT%d" % i, [128, 2, 512]) for i in range(2)]
        BONs = [A("rBON%d" % i, [128, 4, NP_]) for i in range(2)]
        GAMs = [A("rGAM%d" % i, [128, 4, 4]) for i in range(2)]
        YPs = [A("rYP%d" % i, [128, 4, NP_]) for i in range(2)]
        F1 = A("fT1", [128, 4, NP_]); F2 = A("fT2", [128, 4, NP_]); FE = A("fEE", [128, 4, NP_])
        WA = [A("rWA%d" % d, [128, 512]) for d in range(2)]
        for d in range(2):
            kb.dma(WA[d][:], E["wa2_in"].ap()[l, d])
        G2 = A("rG2", [128, 512], BF16)
        kb.dma(G2[:], E["g2_in"].ap()[l], q="pool")
        NM = A("cNM", [128, 8, 128]); KMt = A("cKMt", [128, 8, 128])
        Nb = [A("cN%d" % i, [128, 8, 64]) for i in range(2)]
        NTb = [A("cNT%d" % i, [128, 8, 64]) for i in range(2)]
        TTb = [A("cTT%d" % i, [128, 8, 64]) for i in range(2)]
        RHS = A("cRHS", [128, 512]); U = A("cU", [128, 512]); RHS0 = A("cRHS0", [128, 512])
        ST = [A("cST%d" % i, [128, 4, 64]) for i in range(2)]
        SGD = A("rSGD", [128, NP_], BF16); GD = A("rGD", [128, NP_]); OC = A("rOC", [128, 4, NP_], BF16)
        v64 = lambda ap: ap.rearrange("p a (c t) -> p a c t", t=64)

        items = []
        for (t0, L, latent, sidx) in [(s * 256, 256, False, s) for s in range(4)] + [(1024, 2048, True, None)]:
            npc = L // NP_
            for d in range(2):
                order = list(range(npc)) if d == 0 else list(range(npc - 1, -1, -1))
                for j, pc in enumerate(order):
                    items.append(dict(t0=t0, latent=latent, sidx=sidx, d=d, pc=pc, npc=npc,
                                      first=(j == 0), last=(j == npc - 1)))
        sic = [0]

        def front(it, b):
            d, pc, npc = it["d"], it["pc"], it["npc"]
            p0 = it["t0"] + pc * NP_
            AR, Bt, Kt, BhT, KhT, VT, BON, GAM, YP = ARs[b], Bts[b], Kts[b], BhTs[b], KhTs[b], VTs[b], BONs[b], GAMs[b], YPs[b]
            lo_h = 1 if pc == 0 else 0
            hi_h = NP_ + 1 if pc == npc - 1 else NP_ + 2
            if pc == 0:
                kb.memset(XR[:, :, 0:1], 0.0)
            if pc == npc - 1:
                kb.memset(XR[:, :, NP_ + 1:NP_ + 2], 0.0)
            kb.dma(XR[:, :, lo_h:hi_h], RKV.ap()[:, 0:14, p0 - 1 + lo_h:p0 - 1 + hi_h])
            yield
            sh = 0 if d == 0 else 2
            dsts = [R[:, i, :] for i in range(4)] + [K[:, i, :] for i in range(4)] + [V[:, i, :] for i in range(4)] + [LO[:]]
            for blk in range(13):
                sbk = blk if blk < 12 else 12 + d
                mu = pp[:, O_MU + d * 13 + blk:O_MU + d * 13 + blk + 1]
                omu = pp[:, O_OMU + d * 13 + blk:O_OMU + d * 13 + blk + 1]
                tmp = F1[:, blk % 4, :]
                kb.ts(tmp, XR[:, sbk, sh:sh + NP_], mu, ALU.mult, eng="pool")
                kb.stt(dsts[blk], XR[:, sbk, 1:NP_ + 1], omu, tmp, ALU.mult, ALU.add)
                if blk % 3 == 2:
                    yield
            kb.act(TW[0:64, :], LO[0:64, :], AF.Tanh)
            for pr in range(4):
                ps = FPS()
                kb.mm(ps[:, 0:NP_], WA[d][0:64, pr * 128:(pr + 1) * 128], TW[0:64, :])
                kb.act(LW[:, pr, :], ps[:, 0:NP_], AF.Sigmoid, bias=pp[:, O_W0 + d * 4 + pr:O_W0 + d * 4 + pr + 1])
                ps = FPS()
                kb.mm(ps[:, 0:NP_], WA[d][64:128, pr * 128:(pr + 1) * 128], LO[64:128, :], tile_position=(64, 0))
                kb.act(AA[:, pr, :], ps[:, 0:NP_], AF.Sigmoid, bias=pp[:, O_A0 + d * 4 + pr:O_A0 + d * 4 + pr + 1])
                yield
            kb.ts(LW[:], LW[:], -0.6065306597126334, ALU.mult, eng="pool")
            for pr in range(4):
                kb.I("dve", lambda e, pr=pr: e.tensor_tensor_scan(CL[:, pr, :], E["rst"][:, 0:NP_], LW[:, pr, :], 0.0, ALU.mult, ALU.add),
                     w=[CL[:]], r=[LW[:], E["cst"][:]])
            yield
            if d == 0:
                tot = v64(CL[:])[:, :, :, 63:64]
            else:
                totf = v64(CL[:])[:, :, :, 63:64].to_broadcast([128, 4, 4, 64])
                kb.tt(v64(F2[:]), totf, v64(CL[:]), ALU.subtract)
                kb.tt(CL[:], F2[:], LW[:], ALU.add)
                tot = v64(CL[:])[:, :, :, 0:1]
            kb.act(GAM[:].unsqueeze(3), tot, AF.Exp)
            yield
            kb.tt(KK[:], K[:], bc4(O_KK), ALU.mult, eng="pool")
            kb.tt(F1[:], KK[:], KK[:], ALU.mult, eng="pool")
            for hf in range(2):
                ps = FPS()
                kb.mm(ps[:], BDm, F1[:, 2 * hf:2 * hf + 2, :])
                kb.rsqrt(F2[:, 2 * hf:2 * hf + 2, :], ps[:].rearrange("p (a t) -> p a t", a=2), 1e-24)
            yield
            kb.tt(KK[:], KK[:], F2[:], ALU.mult)
            kb.tt(F1[:], AA[:], bc4(O_KA), ALU.mult, eng="pool")
            kb.tt(F1[:], F1[:], bc4(O_OMKA), ALU.add, eng="pool")
            kb.tt(KM[:], K[:], F1[:], ALU.mult)
            kb.tt(BB[:], KK[:], AA[:], ALU.mult, eng="pool")
            yield
            kb.tt(F1[:], R[:], KM[:], ALU.mult, eng="pool")
            kb.tt(F1[:], F1[:], bc4(O_RK), ALU.mult, eng="pool")
            for hf in range(2):
                ps = FPS()
                kb.mm(ps[:], BDm, F1[:, 2 * hf:2 * hf + 2, :])
                kb.tt(BON[:, 2 * hf:2 * hf + 2, :], ps[:].rearrange("p (a t) -> p a t", a=2), V[:, 2 * hf:2 * hf + 2, :], ALU.mult)
            if d == 0:
                kb.dma(BONS.ap()[:, :, p0:p0 + NP_], BON[:])
            else:
                kb.dma(F2[:], BONS.ap()[:, :, p0:p0 + NP_])
                kb.tt(BON[:], BON[:], F2[:], ALU.add)
            yield
            kb.tt(F1[:], CL[:], LW[:], ALU.subtract, eng="pool")
            kb.act(FE[:], F1[:], AF.Exp)
            kb.stt(AR[:, :, :, 0:64], v64(KK[:]), -1.0, v64(FE[:]), ALU.mult, ALU.mult)
            yield
            kb.act(FE[:], CL[:], AF.Exp)
            kb.tt(AR[:, :, :, 64:128], v64(R[:]), v64(FE[:]), ALU.mult)
            yield
            kb.act(FE[:], CL[:], AF.Exp, scale=-1.0)
            kb.tt(Bt[:], BB[:], FE[:], ALU.mult)
            kb.tt(Kt[:], KM[:], FE[:], ALU.mult, eng="pool")
            yield
            kb.tt(v64(F1[:]), tot.to_broadcast([128, 4, 4, 64]), v64(CL[:]), ALU.subtract)
            kb.act(FE[:], F1[:], AF.Exp)
            kb.tt(Bh[:], BB[:], FE[:], ALU.mult)
            kb.tt(Kh[:], KM[:], FE[:], ALU.mult, eng="pool")
            yield
            for (src, dst) in ((Bh, BhT), (Kh, KhT), (V, VT)):
                for tb in range(2):
                    ps = FPS()
                    for pr in range(4):
                        kb.tr(ps[:, pr * 128:(pr + 1) * 128], src[:, pr, tb * 128:(tb + 1) * 128], ident)
                    kb.cp(dst[:, tb, :], ps[:], eng=("act" if tb else "dve"))
                    yield

        def pairs(it, b, step):
            d, pc = it["d"], it["pc"]
            AR, Bt, Kt, BhT, KhT, VT, GAM, YP = ARs[b], Bts[b], Kts[b], BhTs[b], KhTs[b], VTs[b], GAMs[b], YPs[b]
            if d == 1:
                p0_ = it["t0"] + pc * NP_
                kb.dma(YP[:], YS.ap()[:, :, p0_:p0_ + NP_])
            if it["first"]:
                sic[0] = 0
                if it["latent"]:
                    kb.dma(ST[0][:], E["st_in"].ap()[l, d].rearrange("p (a v) -> p a v", a=4))
                else:
                    kb.memset(ST[0][:], 0.0)
            mask4 = E["maskF4"] if d == 0 else E["maskB4"]
            maskT8 = E["maskFT8"] if d == 0 else E["maskBT8"]
            for tb in (range(2) if d == 0 else range(1, -1, -1)):
                PA = [P[0], P[1]]; PB = [P[2], P[3]]; PC = [P[4], P[5]]
                for ci in range(2):
                    c = 2 * tb + ci; cb = ci * 64; cs = slice(c * 64, (c + 1) * 64)
                    for h in range(8):
                        par = h % 2; hb = par * 64; pr = h // 2
                        lastm = (h >= 6)
                        kb.mm(PA[par][cb:cb + 64, pr * 128:(pr + 1) * 128], Bt[hb:hb + 64, pr, cs],
                              AR[hb:hb + 64, pr, c, :], tile_position=(hb, cb), inc=lastm)
                        kb.mm(PB[par][cb:cb + 64, pr * 128:(pr + 1) * 128], Kt[hb:hb + 64, pr, cs],
                              AR[hb:hb + 64, pr, c, :], tile_position=(hb, cb), inc=lastm)
                        kb.mm(PC[par][cb:cb + 64, pr * 64:(pr + 1) * 64], AR[hb:hb + 64, pr, c, 0:64],
                              Bt[hb:hb + 64, pr, cs], tile_position=(hb, cb), inc=lastm)
                m4 = mask4.rearrange("p (h t) -> p h t", h=4)
                for par in range(2):
                    kb.tt(NM[:, par:8:2, :], PA[par][:].rearrange("p (h t) -> p h t", h=4), m4, ALU.mult)
                    kb.tt(KMt[:, par:8:2, :], PB[par][:].rearrange("p (h t) -> p h t", h=4), m4, ALU.mult)
                    kb.tt(NTb[0][:, par:8:2, :], PC[par][:, 0:256].rearrange("p (h t) -> p h t", h=4),
                          maskT8[:, 0:256].rearrange("p (h t) -> p h t", h=4), ALU.mult)
                kb.cp(Nb[0][:], NM[:, :, 0:64], eng="pool")
                kb.tt(TTb[0][:], Nb[0][:], E["ident8"].rearrange("p (h t) -> p h t", h=8), ALU.add, eng="pool")
                step()
                cur = 0
                for lev in range(5):
                    nx = 1 - cur
                    Nc, NTc, TTc = Nb[cur], NTb[cur], TTb[cur]
                    PNT, PN, PTt = P[0], P[1], P[2]
                    for ci in range(2):
                        cb = ci * 64
                        for h in range(8):
                            hs = slice(h * 64, (h + 1) * 64)
                            kb.mm(PNT[cb:cb + 64, hs], Nc[cb:cb + 64, h, :], NTc[cb:cb + 64, h, :], tile_position=(cb, cb), inc=(h == 7))
                            if lev < 4:
                                kb.mm(PN[cb:cb + 64, hs], NTc[cb:cb + 64, h, :], Nc[cb:cb + 64, h, :], tile_position=(cb, cb), inc=(h == 7))
                    kb.cp(NTb[nx][:], PNT[:].rearrange("p (h t) -> p h t", h=8), eng="act")
                    if lev < 4:
                        kb.cp(Nb[nx][:], PN[:].rearrange("p (h t) -> p h t", h=8), eng="dve")
                    for ci in range(2):
                        cb = ci * 64
                        for h in range(8):
                            hs = slice(h * 64, (h + 1) * 64)
                            kb.mm(PTt[cb:cb + 64, hs], NTb[nx][cb:cb + 64, h, :], TTc[cb:cb + 64, h, :], tile_position=(cb, cb), inc=(h == 7))
                    kb.tt(TTb[nx][:], TTc[:], PTt[:].rearrange("p (h t) -> p h t", h=8), ALU.add)
                    cur = nx
                    step()
                TT = TTb[cur]
                PX = P[5]
                for ci in range(2):
                    cb = ci * 64
                    for h in range(8):
                        hs = slice(h * 64, (h + 1) * 64)
                        kb.mm(PX[cb:cb + 64, hs], KMt[cb:cb + 64, h, 0:64], VT[cb:cb + 64, tb, hs], tile_position=(cb, cb), inc=(h == 7))
                kb.cp(RHS0[:], PX[:], eng="act")
                step()
                for ci in ((0, 1) if d == 0 else (1, 0)):
                    c = 2 * tb + ci; cb = ci * 64
                    Sc = ST[sic[0] % 2]; Sn = ST[(sic[0] + 1) % 2]
                    sic[0] += 1
                    PR, PU, PYa, PYb, PSn = P[3], P[4], P[5], P[0], P[1]
                    for h in (0, 2, 4, 6, 1, 3, 5, 7):
                        hb = (h % 2) * 64; pr = h // 2; hs = slice(h * 64, (h + 1) * 64)
                        kb.mm(PR[cb:cb + 64, hs], AR[hb:hb + 64, pr, c, 0:64], Sc[hb:hb + 64, pr, :], tile_position=(hb, cb), inc=(h >= 6))
                    kb.tt(RHS[cb:cb + 64, :], PR[cb:cb + 64, :], RHS0[cb:cb + 64, :], ALU.add)
                    step()
                    for h in range(8):
                        hs = slice(h * 64, (h + 1) * 64)
                        kb.mm(PU[cb:cb + 64, hs], TT[cb:cb + 64, h, :], RHS[cb:cb + 64, hs], tile_position=(cb, cb), inc=(h == 7))
                    kb.cp(U[cb:cb + 64, :], PU[cb:cb + 64, :], eng="act")
                    step()
                    for h in (0, 2, 4, 6, 1, 3, 5, 7):
                        hb = (h % 2) * 64; pr = h // 2
                        ys = slice(pr * 64, (pr + 1) * 64)
                        kb.mm(PYa[hb:hb + 64, ys], Sc[hb:hb + 64, pr, :], AR[hb:hb + 64, pr, c, 64:128], tile_position=(hb, hb), inc=(h >= 6))
                    for h in range(8):
                        hb = (h % 2) * 64; pr = h // 2; hs = slice(h * 64, (h + 1) * 64)
                        ys = slice(pr * 64, (pr + 1) * 64)
                        kb.mm(PSn[hb:hb + 64, ys], BhT[cb:cb + 64, tb, hs], U[cb:cb + 64, hs],
                              start=True, stop=False, tile_position=(cb, hb), inc=False)
                        kb.mm(PSn[hb:hb + 64, ys], KhT[cb:cb + 64, tb, hs], VT[cb:cb + 64, tb, hs],
                              start=False, stop=True, tile_position=(cb, hb), inc=(h >= 6))
                    kb.tt(Sn[:], Sc[:], GAM[:, :, c:c + 1].to_broadcast([128, 4, 64]), ALU.mult, eng="pool")
                    kb.tt(Sn[:], Sn[:], PSn[:, 0:256].rearrange("p (a t) -> p a t", a=4), ALU.add)
                    for h in range(8):
                        hb = (h % 2) * 64; pr = h // 2; hs = slice(h * 64, (h + 1) * 64)
                        ys = slice(pr * 64, (pr + 1) * 64)
                        kb.mm(PYb[hb:hb + 64, ys], U[cb:cb + 64, hs], NM[cb:cb + 64, h, 64:128],
                              start=True, stop=False, tile_position=(cb, hb), inc=False)
                        kb.mm(PYb[hb:hb + 64, ys], VT[cb:cb + 64, tb, hs], KMt[cb:cb + 64, h, 64:128],
                              start=False, stop=True, tile_position=(cb, hb), inc=(h >= 6))
                    ytk = slice(c * 64, (c + 1) * 64)
                    pya = PYa[:, 0:256].rearrange("p (a t) -> p a t", a=4)
                    pyb = PYb[:, 0:256].rearrange("p (a t) -> p a t", a=4)
                    if d == 0:
                        kb.cp(YP[:, :, ytk], pya, eng="act")
                    else:
                        kb.tt(YP[:, :, ytk], YP[:, :, ytk], pya, ALU.add)
                    kb.tt(YP[:, :, ytk], YP[:, :, ytk], pyb, ALU.add)
                    step()
            if it["last"] and not it["latent"]:
                kb.dma(E["ns_out"].ap()[it["sidx"], l, d].rearrange("p (a v) -> p a v", a=4), ST[sic[0] % 2][:])

        def post(it, b):
            d, pc = it["d"], it["pc"]
            p0 = it["t0"] + pc * NP_
            YP, BON = YPs[b], BONs[b]
            if d == 0:
                kb.dma(YS.ap()[:, :, p0:p0 + NP_], YP[:])
                return
            for hf in range(2):
                ps = FPS()
                kb.mm(ps[:], BD64, YP[:, 2 * hf:2 * hf + 2, :])
                kb.tt(F1[:, 2 * hf:2 * hf + 2, :], YP[:, 2 * hf:2 * hf + 2, :], ps[:].rearrange("p (a t) -> p a t", a=2), ALU.subtract)
            kb.tt(F2[:], F1[:], F1[:], ALU.mult, eng="pool")
            for hf in range(2):
                ps = FPS()
                kb.mm(ps[:], BD64, F2[:, 2 * hf:2 * hf + 2, :])
                kb.rsqrt(FE[:, 2 * hf:2 * hf + 2, :], ps[:].rearrange("p (a t) -> p a t", a=2), 64e-5)
            kb.tt(F1[:], F1[:], FE[:], ALU.mult)
            kb.tt(F1[:], F1[:], bc4(O_LXG), ALU.mult, eng="pool")
            kb.tt(F1[:], F1[:], bc4(O_LXB), ALU.add, eng="pool")
            kb.tt(F1[:], F1[:], BON[:], ALU.add)
            kb.dma(GD[:], RKV.ap()[:, 14, p0:p0 + NP_])
            kb.act(SGD[:], GD[:], AF.Sigmoid)
            for pr in range(4):
                ps = FPS()
                kb.mm(ps[:, 0:NP_], G2[:, pr * 128:(pr + 1) * 128], SGD[:])
                kb.tt(OC[:, pr, :], F1[:, pr, :], ps[:, 0:NP_], ALU.mult)
            kb.dma(BRC.ap()[:, :, p0:p0 + NP_], OC[:])

        g0 = front(items[0], 0)
        for _ in g0:
            pass
        for i, it in enumerate(items):
            b = i % 2
            nxt = front(items[i + 1], 1 - b) if i + 1 < len(items) else None

            def step(nxt=nxt):
                if nxt is not None:
                    next(nxt, None)

            pairs(it, b, step)
            if nxt is not None:
                for _ in nxt:
                    pass
            post(it, b)


def merge_ffn(kb, l, E):
    P = E["P"]; pp = E["pp"][l]; modT = E["modT"][l]
    XTi = E["XT"][l % 2]; XTo = E["XT"][(l + 1) % 2]
    XM = E["XM"]
    BRA = E["BRA"]; BRB = E["BRB"]; BRC = E["BRC"]; onesD = E["onesD"]
    winv = E["w_in"].ap()[l].rearrange("(kc p) n -> p kc n", p=128)
    pk = [0]

    def PS(lo=0, hi=8):
        i = lo + pk[0] % (hi - lo)
        pk[0] += 1
        return P[i]

    def lnorm(Y, sqs, sc3, og, ob, n):
        pm = PS(0, 2); pq = PS(2, 4)
        for c in range(8):
            kb.mm(pm[:, 0:n], onesD, Y[:, c, :], start=(c == 0), stop=(c == 7), inc=(c == 7))
        for c in range(8):
            sq = sqs[c % 2]
            kb.act(sq[:, 0:n], Y[:, c, :], AF.Square)
            kb.mm(pq[:, 0:n], onesD, sq[:, 0:n], start=(c == 0), stop=(c == 7))
        mean = sc3[:, 0, 0:n]; rstd = sc3[:, 1, 0:n]; tmp = sc3[:, 2, 0:n]
        kb.cp(mean, pm[:, 0:n], eng="act")
        kb.tt(tmp, mean, mean, ALU.mult, eng="pool")
        kb.tt(tmp, pq[:, 0:n], tmp, ALU.subtract)
        kb.rsqrt(rstd, tmp, 1e-5)
        for c in range(8):
            e = "dve" if c % 2 else "pool"
            kb.tt(Y[:, c, :], Y[:, c, :], mean, ALU.subtract, eng=e)
            kb.tt(Y[:, c, :], Y[:, c, :], rstd, ALU.mult, eng=e)
            kb.ts(Y[:, c, :], Y[:, c, :], pp[:, og + c:og + c + 1], ALU.mult, pp[:, ob + c:ob + c + 1], ALU.add, eng=e)

    noC = "noC" in E["dbg"]
    with kb.scope():
        WG = kb.sb("mWG", [128, 8, 3072], BF16)
        WB = kb.sb("mWB", [128, 8, D], BF16)
        wbrv = E["w_br"].ap()[l]
        WBB = kb.sb("mWBB", [64, 8, D], BF16)
        WO = kb.sb("mWO", [128, 8, D], BF16)
        wov_ = E["w_out"].ap()[l].rearrange("(kc p) n -> p kc n", p=128)
        for hf_ in range(2):
            hsl = slice(hf_ * 512, (hf_ + 1) * 512)
            for j in range(3):
                i = j * 2 + hf_
                kb.dma((WG[:, :, i * 512:(i + 1) * 512], i), winv[:, :, C_GL + i * 512:C_GL + (i + 1) * 512], q="pool")
            kb.dma((WB[:, 0:4, hsl], hf_), wbrv[0].rearrange("(cc p) d -> p cc d", p=128)[:, :, hsl], q="pool")
            kb.dma((WB[:, 4:8, hsl], 2 + hf_), wbrv[2].rearrange("(cc p) d -> p cc d", p=128)[:, :, hsl], q="pool")
            kb.dma((WBB[:, :, hsl], hf_), wbrv[1].rearrange("(h p) d -> p h d", p=64)[:, :, hsl], q="pool")
        for hf_ in range(2):
            hsl = slice(hf_ * 512, (hf_ + 1) * 512)
            kb.dma((WO[:, :, hsl], hf_), wov_[:, :, hsl], q="pool")
        XLs = [kb.sb("mXL%d" % i, [128, 8, 512]) for i in range(2)]
        HT = kb.sb("mHT", [128, 8, 512], BF16)
        OA = kb.sb("mOA", [128, 4, 512], BF16); OB = kb.sb("mOB", [64, 8, 512], BF16); OCt = kb.sb("mOC", [128, 4, 512], BF16)
        MI = kb.sb("mMI", [128, 8, 512], BF16)
        sqs = [kb.sb("msq%d" % i, [128, 512]) for i in range(2)]
        sc3 = kb.sb("msc3", [128, 3, 512])
        gts = [kb.sb("mgt%d" % i, [128, 512]) for i in range(2)]
        acc = kb.sb("macc", [128, 512])

        def prep(g):
            XL = XLs[g % 2]
            tok = slice(g * 512, (g + 1) * 512)
            kb.dma(XL[:], XTi.ap()[:, :, tok])
            kb.dma(OA[:], BRA.ap()[:, :, tok]); kb.dma(OB[:], BRB.ap()[:, :, tok])
            if not noC:
                kb.dma(OCt[:], BRC.ap()[:, :, tok])
            sc = mod_cols(modT, 8, g); sh = mod_cols(modT, 0, g)
            for c in range(8):
                kb.ts(HT[:, c, :], XL[:, c, :], sc[c], ALU.mult, sh[c], ALU.add, eng=("dve" if c % 2 else "pool"))

        def mix(g):
            for dc in range(8):
                dcs = slice(dc * 128, (dc + 1) * 128)
                for j in range(3):
                    if j == 2 and noC:
                        continue
                    pg = PS(0, 3); pb = PS(3, 6)
                    for kc in range(8):
                        kb.mm(pg[:], (WG[:, kc, j * D + dc * 128:j * D + (dc + 1) * 128], j * 2 + dc // 4), HT[:, kc, :], start=(kc == 0), stop=(kc == 7), inc=(kc == 7))
                    if j != 1:
                        src = OA if j == 0 else OCt
                        for cc in range(4):
                            kb.mm(pb[:], (WB[:, (j // 2) * 4 + cc, dcs], (j // 2) * 2 + dc // 4), src[:, cc, :], start=(cc == 0), stop=(cc == 3), inc=(cc == 3))
                    else:
                        for h in range(8):
                            kb.mm(pb[:], (WBB[:, h, dcs], dc // 4), OB[:, h, :], start=(h == 0), stop=(h == 7), inc=(h == 7))
                    gt = gts[j % 2]
                    kb.act(gt[:], pg[:], AF.Sigmoid)
                    if j == 0:
                        kb.tt(acc[:], gt[:], pb[:], ALU.mult)
                    else:
                        kb.tt(gt[:], gt[:], pb[:], ALU.mult)
                        kb.tt((MI[:, dc, :] if (j == 2 or (noC and j == 1)) else acc[:]), acc[:], gt[:], ALU.add, eng="pool")

        def wout_ln(g):
            XL = XLs[g % 2]
            g1 = mod_cols(modT, 16, g)
            for dc in range(8):
                ps = PS(6, 8)
                for kc in range(8):
                    kb.mm(ps[:], (WO[:, kc, dc * 128:(dc + 1) * 128], dc // 4), MI[:, kc, :], start=(kc == 0), stop=(kc == 7), inc=(kc == 7))
                kb.ts(XL[:, dc, :], XL[:, dc, :], ALPHA, ALU.mult, eng="pool")
                kb.stt(XL[:, dc, :], ps[:], g1[dc], XL[:, dc, :], ALU.mult, ALU.add)
            lnorm(XL, sqs, sc3, O_LN1G, O_LN1B, 512)
            kb.dma(XM.ap()[:, :, g * 512:(g + 1) * 512], XL[:])

        prep(0)
        for g in range(NG):
            mix(g)
            if g + 1 < NG:
                prep(g + 1)
            wout_ln(g)
    chk(kb, "mrg", l)
    with kb.scope():
        WU = kb.sb("fWU", [128, 8, 4096], BF16)
        upv = E["w_up"].ap()[l].rearrange("(kc p) n -> p kc n", p=128)
        WD = kb.sb("fWD", [128, 32, D], BF16)
        dnv = E["w_dn"].ap()[l].rearrange("(fc p) n -> p fc n", p=128)
        for i in range(8):
            kb.dma((WU[:, :, i * 512:(i + 1) * 512], i), upv[:, :, i * 512:(i + 1) * 512], q="pool")
        for i in range(8):
            kb.dma((WD[:, i * 4:(i + 1) * 4, :], i), dnv[:, i * 4:(i + 1) * 4, :], q="pool")
        NF = 256
        NGF = T // NF
        X1s = [kb.sb("fX1%d" % i, [128, 8, NF]) for i in range(2)]
        H2 = kb.sb("fH2", [128, 8, NF], BF16)
        ACTt = kb.sb("fACT", [128, 32, NF], BF16)
        sqs = [kb.sb("fsq%d" % i, [128, 512]) for i in range(2)]
        sc3 = kb.sb("fsc3", [128, 3, 512])
        rl = [kb.sb("frl%d" % i, [128, NF]) for i in range(2)]

        def prep_up(gg):
            g = (gg * NF) // 512
            X1 = X1s[gg % 2]
            kb.dma(X1[:], XM.ap()[:, :, gg * NF:(gg + 1) * NF])
            sc2 = mod_cols(modT, 32, g); sh2 = mod_cols(modT, 24, g)
            for c in range(8):
                kb.ts(H2[:, c, :], X1[:, c, :], sc2[c], ALU.mult, sh2[c], ALU.add, eng=("dve" if c % 2 else "pool"))
            for fc in range(32):
                ps = PS(0, 4)
                for kc in range(8):
                    kb.mm(ps[:, 0:NF], (WU[:, kc, fc * 128:(fc + 1) * 128], fc // 4), H2[:, kc, :], start=(kc == 0), stop=(kc == 7), inc=(kc == 7))
                r_ = rl[fc % 2]
                kb.act(r_[:], ps[:, 0:NF], AF.Relu)
                kb.tt(ACTt[:, fc, :], r_[:], r_[:], ALU.mult, eng=("dve" if fc % 2 else "pool"))

        def down(gg):
            g = (gg * NF) // 512
            X1 = X1s[gg % 2]
            g2 = mod_cols(modT, 40, g)
            for dc in range(8):
                ps = PS(4, 8)
                for fc in range(32):
                    kb.mm(ps[:, 0:NF], (WD[:, fc, dc * 128:(dc + 1) * 128], fc // 4), ACTt[:, fc, :], start=(fc == 0), stop=(fc == 31), inc=(fc == 31))
                kb.ts(X1[:, dc, :], X1[:, dc, :], ALPHA, ALU.mult, eng="pool")
                kb.stt(X1[:, dc, :], ps[:, 0:NF], g2[dc], X1[:, dc, :], ALU.mult, ALU.add)

        def ln_out(gg):
            X1 = X1s[gg % 2]
            lnorm(X1, sqs, sc3, O_LN2G, O_LN2B, NF)
            kb.dma(XTo.ap()[:, :, gg * NF:(gg + 1) * NF], X1[:])

        prep_up(0)
        for gg in range(NGF):
            down(gg)
            if gg + 1 < NGF:
                prep_up(gg + 1)
            ln_out(gg)


def _consts():
    c = np.zeros((128, 9, 512), np.float32)
    r = np.arange(128)
    c[:, 0, 0:128] = np.eye(128)
    bd = (r[:, None] // 64 == r[None, :] // 64).astype(np.float32)
    c[:, 0, 128:256] = bd
    c[:, 0, 256:384] = bd / 64.0
    c[0:64, 0, 384:448] = 1.0 / 64.0
    s = (r % 64)[:, None]
    col = np.arange(128)[None, :]
    mf = np.where(col < 64, s < col, s <= col - 64).astype(np.float32)
    mb = np.where(col < 64, s > col, s >= col - 64).astype(np.float32)
    c[:, 1, :] = np.tile(mf, (1, 4))
    c[:, 2, :] = np.tile(mb, (1, 4))
    t = (r % 64)[:, None]
    sc = np.arange(64)[None, :]
    c[:, 3, :] = np.tile((sc < t).astype(np.float32), (1, 8))
    c[:, 4, :] = np.tile((sc > t).astype(np.float32), (1, 8))
    c[:, 5, :] = np.tile((sc == t).astype(np.float32), (1, 8))
    c[:, 6, :] = (np.arange(512) % 64 != 0).astype(np.float32)[None, :]
    rm = np.zeros((64, 64), np.float32)
    for d in range(64):
        i = d % 32
        if i < 16:
            rm[d + 16, d] = -1.0
        else:
            rm[d - 16, d] = 1.0
    c[0:64, 7, 0:64] = rm
    c[:, 7, 64:192] = 1.0
    c[:, 8, 0:128] = 1.0 / 1024.0
    tt = np.arange(2048)
    row = (tt // 64).astype(np.float32)
    colp = (tt % 64).astype(np.float32)
    inv = (10000.0 ** (-np.arange(0, 32, 2, dtype=np.float32) / 32.0)).astype(np.float32)
    rope = np.zeros((64, 2, 2048), np.float32)
    for d in range(64):
        pos = row if d < 32 else colp
        ang = (pos * inv[(d % 32) % 16]).astype(np.float32)
        rope[d, 0] = np.cos(ang)
        rope[d, 1] = np.sin(ang)
    return c, rope


def _colT(v, n):
    return np.ascontiguousarray(np.asarray(v, np.float32).reshape(n, 128).T)


_NC_CACHE = {}


def kernel(**inp):
    f = lambda k: np.asarray(inp[k], np.float32)
    n = inp.get("_ncores", 8)
    cst, rope = _consts()
    pp = np.zeros((2, 128, NPP), np.float32)
    for l in range(2):
        pp[l, :, O_BADA:O_BADA + 48] = _colT(f("b_ada")[l], 48)
        pp[l, :, O_LN1G:O_LN1G + 8] = _colT(f("ln1_g")[l], 8)
        pp[l, :, O_LN1B:O_LN1B + 8] = _colT(f("ln1_b")[l], 8)
        pp[l, :, O_LN2G:O_LN2G + 8] = _colT(f("ln2_g")[l], 8)
        pp[l, :, O_LN2B:O_LN2B + 8] = _colT(f("ln2_b")[l], 8)
        pp[l, 0:64, O_QN] = f("q_norm")[l]
        pp[l, 0:64, O_KN] = f("k_norm")[l]
        for d in range(2):
            pp[l, :, O_MU + d * 13:O_MU + (d + 1) * 13] = _colT(f("rwkv_mu")[l, d], 13)
            pp[l, :, O_W0 + d * 4:O_W0 + (d + 1) * 4] = _colT(f("rwkv_w0")[l, d], 4)
            pp[l, :, O_A0 + d * 4:O_A0 + (d + 1) * 4] = _colT(f("rwkv_a0")[l, d], 4)
        pp[l, :, O_KK:O_KK + 4] = _colT(f("rwkv_k_k")[l], 4)
        pp[l, :, O_KA:O_KA + 4] = _colT(f("rwkv_k_a")[l], 4)
        pp[l, :, O_RK:O_RK + 4] = _colT(f("rwkv_r_k")[l].reshape(512), 4)
        pp[l, :, O_LXG:O_LXG + 4] = _colT(f("rwkv_lnx_g")[l], 4)
        pp[l, :, O_LXB:O_LXB + 4] = _colT(f("rwkv_lnx_b")[l], 4)
    wsT = np.ascontiguousarray(f("sgu_w").transpose(0, 3, 1, 2))
    sgb = np.ascontiguousarray(f("sgu_b").reshape(2, 1, 512))
    lnA = np.ascontiguousarray(np.broadcast_to(
        np.stack([f("sgu_ln_g"), f("sgu_ln_b")], 1)[:, :, None, :], (2, 2, 128, 512)))
    wa2 = np.ascontiguousarray(np.concatenate([f("rwkv_w2"), f("rwkv_a2")], axis=2))
    shared = dict(pp=pp, w_ada=f("w_ada"), w_in=f("w_in"), w_branch=f("w_branch"), w_out=f("w_out"),
                  w_up=f("w_up"), w_down=f("w_down"), wsT=wsT, sgb=sgb, lnA=lnA, wa2=wa2, g2=f("rwkv_g2"),
                  cst=cst, rope=rope)
    xp, xs = f("x_prompt"), f("x_sample")
    ck, cvv, st, c, cctx = f("cache_k"), f("cache_v"), f("state_wkv"), f("c"), f("c_ctx")
    in_maps = []
    for i in range(n):
        m = dict(shared)
        m["x"] = np.ascontiguousarray(np.concatenate([xp[4 * i:4 * i + 4].reshape(1024, D), xs[i]], 0))
        m["ck"] = np.ascontiguousarray(ck[i].reshape(2, 512, 128))
        m["cvv"] = np.ascontiguousarray(cvv[i].reshape(2, 512, 128))
        s_ = st[i].reshape(2, 2, 4, 2, 64, 64)
        m["st0"] = np.ascontiguousarray(s_.transpose(0, 1, 3, 5, 2, 4).reshape(2, 2, 128, 256))
        cvec = np.stack([cctx, c[i]], -1)
        m["cvec"] = np.ascontiguousarray(cvec.reshape(8, 128, 2).transpose(1, 0, 2))
        in_maps.append(m)
    if inp.get("_prep_only"):
        return in_maps
    if "nc" not in _NC_CACHE:
        _NC_CACHE["nc"] = build()
    kb = _NC_CACHE["nc"]
    res = run_bass_kernel_spmd(kb.nc, in_maps, core_ids=list(range(n)))
    yp = np.zeros((32, 256, D), np.float32)
    ys = np.zeros((8, 2048, D), np.float32)
    nk = np.zeros((32, 2, 256, 2, 64), np.float32)
    nv = np.zeros((32, 2, 256, 2, 64), np.float32)
    ns = np.zeros((32, 2, 2, 8, 64, 64), np.float32)
    for i in range(n):
        r = res.results[i]
        yp[4 * i:4 * i + 4] = r["y"][:1024].reshape(4, 256, D)
        ys[i] = r["y"][1024:]
        nk[4 * i:4 * i + 4] = r["nk"].reshape(2, 4, 256, 2, 64).transpose(1, 0, 2, 3, 4)
        nv[4 * i:4 * i + 4] = r["nv"].reshape(2, 4, 256, 2, 64).transpose(1, 0, 2, 3, 4)
        s_ = r["ns"].reshape(4, 2, 2, 2, 64, 4, 64)
        ns[4 * i:4 * i + 4] = s_.transpose(0, 1, 2, 5, 3, 6, 4).reshape(4, 2, 2, 8, 64, 64)
    return (yp, ys, nk, nv, ns)
```

```python
import numpy as np
from contextlib import ExitStack
import concourse.bass as bass
import concourse.mybir as mybir
from concourse.bass_utils import run_bass_kernel_spmd

F32 = mybir.dt.float32
F32R = mybir.dt.float32r
BF16 = mybir.dt.bfloat16
AF = mybir.ActivationFunctionType
ALU = mybir.AluOpType
AX = mybir.AxisListType


class KB:
    def __init__(self, n_dma_sems=12):
        self.nc = bass.Bass("TRN2", target_bir_lowering=False)
        nc = self.nc
        self.es = ExitStack()
        self.es_root = self.es
        self.eng = {"pe": nc.tensor, "dve": nc.vector, "act": nc.scalar, "pool": nc.gpsimd, "sp": nc.sync}
        self.sem = {}
        self.cnt = {}
        self.clock = {}
        self.snap = {}
        for e in self.eng:
            self.sem[e] = self.es.enter_context(nc.semaphore("s_" + e))
            self.cnt[e] = 0
            self.clock[e] = {}
            self.snap[e] = {}
        self.dq = {}
        for q in ("sp", "pool", "act"):
            ids = []
            for i in range(n_dma_sems):
                sid = "d_%s_%d" % (q, i)
                self.sem[sid] = self.es.enter_context(nc.semaphore(sid))
                self.cnt[sid] = 0
                self.snap[sid] = {}
                ids.append(sid)
            self.dq[q] = [ids, 0]
        self.EPOCH = 60000
        self.epsem = {e: {0: self.sem[e]} for e in self.eng}
        self.lastw = {}
        self.readers = {}
        self.n_ins = 0
        self.n_wait = 0
        self.uid = 0
        self.ps_row = {}
        self.pending = {}
        self.drain_mode = 0
        self.last_tp = None

    def sb(self, name, shape, dt=F32):
        self.uid += 1
        name = "%s_%d" % (name, self.uid)
        return self.es.enter_context(self.nc.sbuf_tensor("s_" + name, list(shape), dt))

    def ps(self, name, shape, dt=F32):
        return self.es.enter_context(self.nc.psum_tensor("p_" + name, list(shape), dt))

    def dram(self, name, shape, dt=F32, kind="Internal"):
        return self.nc.dram_tensor(name, list(shape), dt, kind=kind)

    def _semv(self, f, c):
        if f not in self.eng:
            return self.sem[f], c
        ep = (c - 1) // self.EPOCH
        if ep not in self.epsem[f]:
            self.epsem[f][ep] = self.es_root.enter_context(self.nc.semaphore("s_%s_e%d" % (f, ep)))
        return self.epsem[f][ep], c - ep * self.EPOCH

    @staticmethod
    def _key(x):
        if isinstance(x, tuple):
            return (x[0].tensor.name, x[1])
        return (x.tensor.name, None)

    @staticmethod
    def _ap(x):
        return x[0] if isinstance(x, tuple) else x

    def _deps(self, eng, reads, writes):
        need = {}

        def add(f, c, same_ok):
            if f == eng and same_ok:
                return
            if c > need.get(f, 0):
                need[f] = c

        for k in reads:
            lw = self.lastw.get(k)
            if lw:
                add(lw[0], lw[1], False)
            if k[0].startswith("p_"):
                for f, c in self.readers.get(k, {}).items():
                    add(f, c, True)
        for k in writes:
            lw = self.lastw.get(k)
            if lw:
                add(lw[0], lw[1], True)
            for f, c in self.readers.get(k, {}).items():
                add(f, c, True)
        clk = self.clock[eng]
        e = self.eng[eng]
        for f, c in need.items():
            if clk.get(f, 0) >= c:
                continue
            if c > self.cnt[f]:
                raise RuntimeError("wait on unmaterialised count %s %d > %d" % (f, c, self.cnt[f]))
            sm, val = self._semv(f, c)
            e.wait_ge(sm, val)
            self.n_wait += 1
            for g, v in self.snap[f][c].items():
                if v > clk.get(g, 0):
                    clk[g] = v
            clk[f] = max(clk.get(f, 0), c)

    def _record(self, who, c, reads, writes):
        for k in reads:
            self.readers.setdefault(k, {})[who] = c
        for k in writes:
            self.lastw[k] = (who, c)
            self.readers[k] = {}

    def I(self, eng, fn, w=(), r=(), inc=True):
        rk = [self._key(x) for x in r]
        wk = [self._key(x) for x in w]
        self._deps(eng, rk, wk)
        ins = fn(self.eng[eng])
        self.n_ins += 1
        if not inc:
            self._record(eng, self.cnt[eng] + 1, rk, wk)
            self.pending[eng] = True
            return ins
        self.pending[eng] = False
        self.cnt[eng] += 1
        c = self.cnt[eng]
        ins.then_inc(self._semv(eng, c)[0], 1)
        s = dict(self.clock[eng])
        s[eng] = c
        self.snap[eng][c] = s
        self._record(eng, c, rk, wk)
        return ins

    def dma(self, out, in_, q="sp", **kw):
        rk = [self._key(in_)]
        wk = [self._key(out)]
        self._deps(q, rk, wk)
        ids, pos = self.dq[q]
        sid = ids[pos % len(ids)]
        self.dq[q][1] = pos + 1
        clk = self.clock[q]
        prev = self.cnt[sid]
        if prev and clk.get(sid, 0) < prev:
            self.eng[q].wait_ge(self.sem[sid], prev)
            for g, v in self.snap[sid][prev].items():
                if v > clk.get(g, 0):
                    clk[g] = v
            clk[sid] = prev
        ins = self.eng[q].dma_start(out=self._ap(out), in_=self._ap(in_), **kw)
        c = prev + 16
        self.cnt[sid] = c
        ins.then_inc(self.sem[sid], 16)
        s = dict(clk)
        s[sid] = c
        self.snap[sid][c] = s
        self._record(sid, c, rk, wk)
        self.n_ins += 1
        return ins

    def finish(self):
        assert not any(self.pending.values()), self.pending
        e = "sp"
        clk = self.clock[e]
        for f in self.sem:
            c = self.cnt[f]
            if c and clk.get(f, 0) < c and f != e:
                sm, val = self._semv(f, c)
                self.eng[e].wait_ge(sm, val)
        self.es.close()

    def mm(self, out, lhsT, rhs, start=True, stop=True, inc=True, **kw):
        a = self._ap
        lt = a(lhsT)
        row = lt.base_partition() if lt.partition_size() < 128 else -1
        okey = self._key(out)
        prev = self.ps_row.get(okey)
        if prev is not None and prev[0] != row and self.clock["pe"].get("pe", 0) < prev[1]:
            if prev[1] > self.cnt["pe"]:
                raise RuntimeError("PE row switch on bank %s needs a materialised count" % (okey,))
            sm, val = self._semv("pe", prev[1])
            self.eng["pe"].wait_ge(sm, val)
            self.clock["pe"]["pe"] = prev[1]
            self.n_wait += 1
        self.ps_row[okey] = (row, self.cnt["pe"] + 1)
        tp = kw.get("tile_position")
        if self.drain_mode and tp is not None and self.cnt["pe"] and self.clock["pe"].get("pe", 0) < self.cnt["pe"]:
            if self.drain_mode == 1 or (self.drain_mode == 2 and tp != self.last_tp):
                sm, val = self._semv("pe", self.cnt["pe"])
                self.eng["pe"].wait_ge(sm, val)
                self.clock["pe"]["pe"] = self.cnt["pe"]
        self.last_tp = tp
        return self.I("pe", lambda e: e.matmul(a(out), a(lhsT), a(rhs), start=start, stop=stop, **kw),
                      w=[out], r=[lhsT, rhs], inc=inc)

    def tr(self, out, in_, ident):
        a = self._ap
        return self.I("pe", lambda e: e.transpose(a(out), a(in_), a(ident)), w=[out], r=[in_, ident])

    def act(self, out, in_, func, bias=None, scale=1.0, accum=None, eng="act"):
        a = self._ap
        r = [in_]
        kw = {}
        if bias is not None:
            if isinstance(bias, (int, float)):
                kw["bias"] = float(bias)
            else:
                kw["bias"] = a(bias)
                r.append(bias)
        if isinstance(scale, (int, float)):
            kw["scale"] = float(scale)
        else:
            kw["scale"] = a(scale)
            r.append(scale)
        w = [out]
        if accum is not None:
            kw["accum_out"] = a(accum)
            w.append(accum)
        return self.I(eng, lambda e: e.activation(a(out), a(in_), func, **kw), w=w, r=r)

    def tt(self, out, x, y, op, eng="dve"):
        a = self._ap
        return self.I(eng, lambda e: e.tensor_tensor(a(out), a(x), a(y), op), w=[out], r=[x, y])

    def ts(self, out, x, s1, op0, s2=None, op1=None, eng="dve", accum=None):
        a = self._ap
        r = [x]
        v1 = s1
        if not isinstance(s1, (int, float)):
            r.append(s1)
            v1 = a(s1)
        v2 = s2
        if s2 is not None and not isinstance(s2, (int, float)):
            r.append(s2)
            v2 = a(s2)
        kw = {}
        w = [out]
        if op1 is not None:
            kw["op1"] = op1
        if accum is not None:
            kw["accum_out"] = a(accum)
            w.append(accum)
        return self.I(eng, lambda e: e.tensor_scalar(a(out), a(x), v1, v2, op0, **kw), w=w, r=r)

    def stt(self, out, x, s, y, op0, op1, eng="dve"):
        a = self._ap
        r = [x, y]
        v = s
        if not isinstance(s, (int, float)):
            r.append(s)
            v = a(s)
        return self.I(eng, lambda e: e.scalar_tensor_tensor(a(out), a(x), v, a(y), op0, op1), w=[out], r=r)

    def cp(self, out, in_, eng="dve"):
        a = self._ap
        if eng == "act":
            return self.I(eng, lambda e: e.copy(a(out), a(in_)), w=[out], r=[in_])
        return self.I(eng, lambda e: e.tensor_copy(a(out), a(in_)), w=[out], r=[in_])

    def memset(self, out, val, eng="pool"):
        a = self._ap
        return self.I(eng, lambda e: e.memset(a(out), val), w=[out])

    def recip(self, out, in_):
        a = self._ap
        return self.I("dve", lambda e: e.reciprocal(a(out), a(in_)), w=[out], r=[in_])

    def rsqrt(self, out, in_, eps):
        self.act(out, in_, AF.Ln, bias=eps)
        self.act(out, out, AF.Exp, scale=-0.5)

    def barrier(self):
        assert not any(self.pending.values()), self.pending
        for e in self.eng:
            clk = self.clock[e]
            for f in self.sem:
                c = self.cnt[f]
                if f == e or not c or clk.get(f, 0) >= c:
                    continue
                sm, val = self._semv(f, c)
                self.eng[e].wait_ge(sm, val)
                clk[f] = c
        full = {f: self.cnt[f] for f in self.sem if self.cnt[f]}
        for e in self.eng:
            for f, c in full.items():
                if f != e:
                    self.clock[e][f] = max(self.clock[e].get(f, 0), c)

    def scope(self):
        kb = self

        class _S:
            def __enter__(s):
                s.old = kb.es
                kb.es = ExitStack()
                return s

            def __exit__(s, *a):
                kb.barrier()
                kb.es.close()
                kb.es = s.old
                return False

        return _S()

T = 3072
NG = 6
D = 1024
NPP = 160
O_BADA, O_LN1G, O_LN1B, O_LN2G, O_LN2B, O_QN, O_KN = 0, 48, 56, 64, 72, 80, 81
O_MU, O_W0, O_A0, O_KK, O_KA, O_RK, O_LXG, O_LXB, O_OMKA, O_OMU = 82, 108, 116, 124, 128, 132, 136, 140, 144, 148
NPP = 176
ALPHA = 4.0 ** 0.25
C_UA, C_VA, C_Q, C_K, C_V, C_R, C_GL = 0, 512, 1024, 1536, 1664, 1792, 3712


class StopBuild(Exception):
    pass


def build(stop_after=None, dbg=()):
    kb = KB()
    kb.stop_after = stop_after
    kb.drain_mode = 1 if "drain1" in dbg else (2 if "drain2" in dbg else 0)
    try:
        _build(kb, stop_after, dbg)
    except StopBuild:
        pass
    kb.finish()
    return kb


def chk(kb, name, l=0):
    if kb.stop_after == (name, l):
        raise StopBuild()


def _build(kb, stop_after, dbg):
    nc = kb.nc
    IN = lambda n, s, dt=F32: kb.dram(n, s, dt, kind="ExternalInput")
    OUT = lambda n, s, dt=F32: kb.dram(n, s, dt, kind="ExternalOutput")
    x_in = IN("x", [T, D])
    ck_in = IN("ck", [2, 512, 128])
    cvv_in = IN("cvv", [2, 512, 128])
    st_in = IN("st0", [2, 2, 128, 256])
    cvec_in = IN("cvec", [128, 8, 2])
    pp_in = IN("pp", [2, 128, NPP])
    w_ada = IN("w_ada", [2, D, 6144])
    w_in = IN("w_in", [2, D, 6784])
    w_br = IN("w_branch", [2, 3, 512, D])
    w_out = IN("w_out", [2, D, D])
    w_up = IN("w_up", [2, D, 4096])
    w_dn = IN("w_down", [2, 4096, D])
    wsT_in = IN("wsT", [2, 128, 4, 128])
    sgb_in = IN("sgb", [2, 1, 512])
    lnA_in = IN("lnA", [2, 2, 128, 512])
    wa2_in = IN("wa2", [2, 2, 128, 512])
    g2_in = IN("g2", [2, 128, 512])
    cst_in = IN("cst", [128, 9, 512])
    rope_in = IN("rope", [64, 2, 2048])
    y_out = OUT("y", [T, D])
    nk_out = OUT("nk", [2, 1024, 128])
    nv_out = OUT("nv", [2, 1024, 128])
    ns_out = OUT("ns", [4, 2, 2, 128, 256])
    XT = [kb.dram("XT%d" % i, [128, 8, T]) for i in range(2)]
    RKV = kb.dram("RKV", [128, 15, T])
    BRA = kb.dram("BRA", [128, 4, T], BF16)
    BRB = kb.dram("BRB", [64, 8, T], BF16)
    BRC = kb.dram("BRC", [128, 4, T], BF16)
    BONS = kb.dram("BONS", [128, 4, T])
    XM = kb.dram("XM", [128, 8, T])
    YS = kb.dram("YS", [128, 4, T])
    dbg_out = {}
    if "dump" in dbg:
        dbg_out["d1"] = OUT("dbg1", [128, 32, 256])
        dbg_out["d2"] = OUT("dbg2", [128, 14, 256])
        dbg_out["d3"] = OUT("dbg3", [128, 6, 1024])

    P = [kb.ps("P%d" % i, [128, 512]) for i in range(8)]
    cst = kb.sb("cst", [128, 9, 512])
    kb.dma(cst[:], cst_in.ap()[:, :, :])
    ident = cst[:, 0, 0:128]
    BDm = cst[:, 0, 128:256]
    BD64 = cst[:, 0, 256:384]
    ones64 = cst[0:64, 0, 384:448]
    maskF4, maskB4, maskFT8, maskBT8, ident8, rst = (cst[:, i, :] for i in range(1, 7))
    rm_f = cst[0:64, 7, 0:64]
    ones_row = cst[:, 7, 64:192]
    onesD = cst[:, 8, 0:128]
    rm_b = kb.sb("rm_b", [64, 64], BF16)
    kb.cp(rm_b[:], rm_f, eng="dve")
    ident_b = kb.sb("ident_b", [128, 128], BF16)
    kb.cp(ident_b[:], ident, eng="dve")
    pp = [kb.sb("pp%d" % l, [128, NPP]) for l in range(2)]
    for l in range(2):
        kb.dma(pp[l][:], pp_in.ap()[l])
    modT = [kb.sb("modT%d" % l, [128, 48, 2]) for l in range(2)]

    with kb.scope():
        cv = kb.sb("cv", [128, 8, 2])
        sl = kb.sb("sl", [128, 8, 2])
        kb.dma(cv[:], cvec_in.ap()[:, :, :])
        kb.act(sl[:], cv[:], AF.Silu)
        wts = [kb.sb("wadat%d" % i, [128, 8, 512]) for i in range(2)]
        for l in range(2):
            wv = w_ada.ap()[l].rearrange("(kc p) n -> p kc n", p=128)
            for nb in range(12):
                wt = wts[nb % 2]
                kb.dma(wt[:], wv[:, :, nb * 512:(nb + 1) * 512])
                ps = P[nb % 2]
                for j in range(4):
                    for kc in range(8):
                        kb.mm(ps[:, 2 * j:2 * j + 2], wt[:, kc, j * 128:(j + 1) * 128], sl[:, kc, :],
                              start=(kc == 0), stop=(kc == 7), inc=(kc == 7))
                for j in range(4):
                    c = nb * 4 + j
                    kb.ts(modT[l][:, c, :], ps[:, 2 * j:2 * j + 2], pp[l][:, O_BADA + c:O_BADA + c + 1], ALU.add)
            for c0 in (8, 32):
                kb.ts(modT[l][:, c0:c0 + 8, :], modT[l][:, c0:c0 + 8, :], 1.0, ALU.add, eng="pool")
            kb.ts(pp[l][:, O_OMU:O_OMU + 26], pp[l][:, O_MU:O_MU + 26], -1.0, ALU.mult, 1.0, ALU.add, eng="pool")
            kb.ts(pp[l][:, O_OMKA:O_OMKA + 4], pp[l][:, O_KA:O_KA + 4], -1.0, ALU.mult, 1.0, ALU.add, eng="pool")

    chk(kb, "mod")
    with kb.scope():
        xts = [kb.sb("xt_t%d" % i, [128, 8, 512]) for i in range(2)]
        xins = [kb.sb("xin%d" % i, [128, 1024]) for i in range(2)]
        k = 0
        for g in range(NG):
            xt = xts[g % 2]
            for tt in range(4):
                xin = xins[tt % 2]
                kb.dma(xin[:], x_in.ap()[(g * 4 + tt) * 128:(g * 4 + tt + 1) * 128, :])
                for half in range(2):
                    ps = P[k % 4]
                    k += 1
                    for c in range(4):
                        kb.tr(ps[:, c * 128:(c + 1) * 128], xin[:, (half * 4 + c) * 128:(half * 4 + c + 1) * 128], ident)
                    kb.cp(xt[:, half * 4:(half + 1) * 4, tt * 128:(tt + 1) * 128],
                          ps[:].rearrange("p (c t) -> p c t", c=4), eng=("dve" if half else "act"))
            kb.dma(XT[0].ap()[:, :, g * 512:(g + 1) * 512], xt[:])

    chk(kb, "t0")
    for l in range(2):
        layer(kb, l, locals())
        chk(kb, "layer", l)

    if stop_after is None:
        with kb.scope():
            xts = [kb.sb("oxt%d" % i, [128, 8, 512]) for i in range(2)]
            yos = [kb.sb("yo%d" % i, [128, 1024]) for i in range(2)]
            k = 0
            for g in range(NG):
                xt = xts[g % 2]
                kb.dma(xt[:], XT[0].ap()[:, :, g * 512:(g + 1) * 512])
                for tt in range(4):
                    yo = yos[tt % 2]
                    for half in range(2):
                        ps = P[k % 4]
                        k += 1
                        for c in range(4):
                            kb.tr(ps[:, c * 128:(c + 1) * 128], xt[:, half * 4 + c, tt * 128:(tt + 1) * 128], ident)
                        kb.cp(yo[:, half * 512:(half + 1) * 512], ps[:], eng=("dve" if half else "act"))
                    kb.dma(y_out.ap()[(g * 4 + tt) * 128:(g * 4 + tt + 1) * 128, :], yo[:])


def mod_cols(modT_l, base, g):
    v = 0 if g < 2 else 1
    return [modT_l[:, base + c, v:v + 1] for c in range(8)]


def layer(kb, l, E):
    P = E["P"]; pp = E["pp"][l]; modT = E["modT"][l]; cst = E["cst"]
    XTi = E["XT"][l % 2]; XTo = E["XT"][(l + 1) % 2]
    w_in = E["w_in"]; RKV = E["RKV"]; BRA = E["BRA"]; BRB = E["BRB"]; BRC = E["BRC"]
    ident = E["ident"]; ident_b = E["ident_b"]
    winv = w_in.ap()[l].rearrange("(kc p) n -> p kc n", p=128)
    pk = [0]

    def PS(lo=0, hi=8):
        i = lo + pk[0] % (hi - lo)
        pk[0] += 1
        return P[i]

    ek = [0]

    def EV():
        ek[0] += 1
        return "dve" if ek[0] % 2 else "act"

    with kb.scope():
        HT = kb.sb("HT", [128, 8, T], BF16)
        QT = kb.sb("QT", [64, 8, T], BF16)
        KT = kb.sb("KT", [64, 2, T + 512], BF16)
        VS = kb.sb("VS", [128, 28, 2, 65], BF16)
        kb.memset(VS[:], 1.0)
        with kb.scope():
            xls = [kb.sb("xl%d" % i, [128, 8, 512]) for i in range(2)]
            for g in range(NG):
                xl = xls[g % 2]
                kb.dma(xl[:], XTi.ap()[:, :, g * 512:(g + 1) * 512])
                sc = mod_cols(modT, 8, g); sh = mod_cols(modT, 0, g)
                for c in range(8):
                    kb.ts(HT[:, c, g * 512:(g + 1) * 512], xl[:, c, :], sc[c], ALU.mult, sh[c], ALU.add,
                          eng=("dve" if c % 2 else "pool"))
        chk(kb, "ht", l)
        with kb.scope():
            UA = kb.sb("UA", [128, 4, T], BF16)
            W = kb.sb("Wua", [128, 8, 512], BF16)
            kb.dma(W[:], winv[:, :, C_UA:C_UA + 512], q="pool")
            for g in range(NG):
                for j in range(4):
                    ps = PS()
                    for kc in range(8):
                        kb.mm(ps[:], W[:, kc, j * 128:(j + 1) * 128], HT[:, kc, g * 512:(g + 1) * 512],
                              start=(kc == 0), stop=(kc == 7), inc=(kc == 7))
                    kb.cp(UA[:, j, g * 512:(g + 1) * 512], ps[:], eng=EV())
            chk(kb, "ua", l)
            W2 = kb.sb("Wva", [128, 8, 512], BF16)
            kb.dma(W2[:], winv[:, :, C_VA:C_VA + 512], q="pool")
            wsT = kb.sb("wsT", [128, 4, 128], BF16)
            kb.dma(wsT[:], E["wsT_in"].ap()[l], q="pool")
            sgb = kb.sb("sgb", [1, 512])
            kb.dma(sgb[:], E["sgb_in"].ap()[l])
            lnG = kb.sb("lnG", [128, 512]); lnB = kb.sb("lnB", [128, 512])
            kb.dma(lnG[:], E["lnA_in"].ap()[l, 0]); kb.dma(lnB[:], E["lnA_in"].ap()[l, 1])
            ones_row = E["ones_row"]
            sqs = [kb.sb("a_sq%d" % i, [128, 512]) for i in range(2)]
            v0s = [kb.sb("a_v0%d" % i, [128, 512]) for i in range(2)]
            vns = [kb.sb("a_vn%d" % i, [128, 512], BF16) for i in range(2)]
            sts = [kb.sb("a_st%d" % i, [128, 8]) for i in range(2)]
            for tt in range(24):
                ps = PS(0, 4)
                for kc in range(8):
                    kb.mm(ps[:], HT[:, kc, tt * 128:(tt + 1) * 128], W2[:, kc, :], start=(kc == 0), stop=(kc == 7), inc=(kc == 7))
                sq = sqs[tt % 2]; v0 = v0s[tt % 2]; vn = vns[tt % 2]; st = sts[tt % 2]
                cut = [int(x[3:]) for x in E["dbg"] if x.startswith("cut")]
                cut = cut[0] if cut else 99
                kb.act(v0[:], ps[:], AF.Copy)
                kb.act(sq[:], v0[:], AF.Square)
                if cut < 2: continue
                kb.I("dve", lambda e: e.reduce_sum(st[:, 0:1], v0[:], AX.X), w=[st[:]], r=[v0[:]])
                kb.I("dve", lambda e: e.reduce_sum(st[:, 1:2], sq[:], AX.X), w=[st[:]], r=[sq[:], st[:]])
                if cut < 3: continue
                kb.ts(st[:, 2:3], st[:, 0:1], 1.0 / 512, ALU.mult)
                kb.tt(st[:, 3:4], st[:, 2:3], st[:, 2:3], ALU.mult)
                kb.stt(st[:, 4:5], st[:, 1:2], 1.0 / 512, st[:, 3:4], ALU.mult, ALU.subtract)
                if cut < 4: continue
                kb.rsqrt(st[:, 5:6], st[:, 4:5], 1e-5)
                kb.stt(st[:, 6:7], st[:, 2:3], -1.0, st[:, 5:6], ALU.mult, ALU.mult)
                if cut < 5: continue
                kb.act(v0[:], v0[:], AF.Identity, bias=st[:, 6:7], scale=st[:, 5:6])
                if cut < 6: continue
                kb.tt(v0[:], v0[:], lnG[:], ALU.mult, eng="pool")
                kb.tt(vn[:], v0[:], lnB[:], ALU.add, eng="pool")
                if cut < 7: continue
                ps2 = PS(4, 8)
                for j in range(4):
                    nob = "nobias" in E["dbg"]
                    kb.mm(ps2[:, j * 128:(j + 1) * 128], vn[:, j * 128:(j + 1) * 128], wsT[:, j, :], start=True, stop=nob)
                    if not nob:
                        kb.mm(ps2[:, j * 128:(j + 1) * 128], ones_row[0:1, :], sgb[0:1, j * 128:(j + 1) * 128], start=False, stop=True)
                if cut < 8: continue
                kb.tt(UA[:, :, tt * 128:(tt + 1) * 128], UA[:, :, tt * 128:(tt + 1) * 128],
                      ps2[:].rearrange("p (j t) -> p j t", j=4), ALU.mult)
            if "nobra" not in E["dbg"]:
                kb.dma(BRA.ap()[:, :, :], UA[:])
        chk(kb, "va", l)
        with kb.scope():
            Wq = kb.sb("Wq", [128, 8, 512], BF16)
            kb.dma(Wq[:], winv[:, :, C_Q:C_Q + 512], q="pool")
            Wkv = kb.sb("Wkv", [128, 8, 256], BF16)
            kb.dma(Wkv[:], winv[:, :, C_K:C_K + 256], q="pool")
            rope = kb.sb("rope", [64, 2, 2048])
            kb.dma(rope[:], E["rope_in"].ap()[:, :, :])
            sqs = [kb.sb("q_sq%d" % i, [64, 512]) for i in range(2)]
            rss = [kb.sb("q_rs%d" % i, [64, 512]) for i in range(2)]
            qns = [kb.sb("q_qn%d" % i, [64, 512]) for i in range(2)]
            qbs = [kb.sb("q_qb%d" % i, [64, 512], BF16) for i in range(2)]
            t1s = [kb.sb("q_t1%d" % i, [64, 512]) for i in range(2)]
            t2s = [kb.sb("q_t2%d" % i, [64, 512]) for i in range(2)]
            qrs = [kb.sb("q_qr%d" % i, [64, 512]) for i in range(2)]
            kos = [kb.sb("q_ko%d" % i, [128, 4, 128]) for i in range(2)]
            vos = [kb.sb("q_vo%d" % i, [128, 128]) for i in range(2)]
            ones64 = E["ones64"]; rm_b = E["rm_b"]
            it = 0
            for g in range(NG):
                latent = g >= 2
                tok = slice(g * 512, (g + 1) * 512)
                for hh in range(10):
                    isk = hh >= 8
                    h = hh - 8 if isk else hh
                    Wt = Wkv if isk else Wq
                    gcol = pp[0:64, (O_KN if isk else O_QN):(O_KN if isk else O_QN) + 1]
                    i2 = it % 2
                    it += 1
                    sq, rs, qn, qb, t1, t2 = sqs[i2], rss[i2], qns[i2], qbs[i2], t1s[i2], t2s[i2]
                    ps = PS(0, 3)
                    for kc in range(8):
                        kb.mm(ps[0:64, :], Wt[:, kc, h * 64:(h + 1) * 64], HT[:, kc, tok], start=(kc == 0), stop=(kc == 7), inc=(kc == 7))
                    qr = qrs[i2]
                    kb.act(qr[:], ps[0:64, :], AF.Copy)
                    kb.act(sq[:], ps[0:64, :], AF.Square)
                    ps2 = PS(3, 6)
                    kb.mm(ps2[0:64, :], ones64, sq[:])
                    kb.rsqrt(rs[:], ps2[0:64, :], 1e-6)
                    dst = (KT[:, h, tok] if isk else QT[:, h, tok])
                    if (not latent) and (not isk):
                        kb.stt(dst, qr[:], gcol, rs[:], ALU.mult, ALU.mult)
                        continue
                    kb.stt(qn[:], qr[:], gcol, rs[:], ALU.mult, ALU.mult)
                    if not latent:
                        kb.cp(dst, qn[:], eng="pool")
                        ko = kos[g % 2]
                        ps3 = PS(6, 8)
                        for tt in range(4):
                            kb.tr(ps3[:, tt * 64:(tt + 1) * 64], qn[:, tt * 128:(tt + 1) * 128], ident[0:64, 0:64])
                        kb.cp(ko[:, :, h * 64:(h + 1) * 64], ps3[:, 0:256].rearrange("p (t d) -> p t d", t=4), eng="act")
                        if h == 1:
                            kb.dma(E["nk_out"].ap()[l, g * 512:(g + 1) * 512, :].rearrange("(t p) c -> p t c", p=128), ko[:])
                        continue
                    pos = slice((g - 2) * 512, (g - 1) * 512)
                    kb.cp(qb[:], qn[:], eng="pool")
                    ps3 = PS(6, 8)
                    kb.mm(ps3[0:64, :], rm_b[:], qb[:])
                    kb.tt(t1[:], qn[:], rope[:, 0, pos], ALU.mult, eng="pool")
                    kb.tt(t2[:], ps3[0:64, :], rope[:, 1, pos], ALU.mult)
                    kb.tt(dst, t1[:], t2[:], ALU.add, eng="pool")
                for t4 in range(4):
                    tt = g * 4 + t4
                    ps = PS(0, 3)
                    for kc in range(8):
                        kb.mm(ps[:, 0:128], HT[:, kc, tt * 128:(tt + 1) * 128], Wkv[:, kc, 128:256], start=(kc == 0), stop=(kc == 7), inc=(kc == 7))
                    vo = vos[t4 % 2]
                    kb.cp(vo[:], ps[:, 0:128], eng="act")
                    kb.cp(VS[:, tt, :, 0:64], vo[:].rearrange("p (k d) -> p k d", k=2), eng="pool")
                    if not latent:
                        kb.dma(E["nv_out"].ap()[l, tt * 128:(tt + 1) * 128, :], vo[:])
            ckt = kb.sb("ckt", [128, 4, 128]); cvt = kb.sb("cvt", [128, 4, 128])
            kb.dma(ckt[:], E["ck_in"].ap()[l].rearrange("(t p) c -> p t c", p=128))
            kb.dma(cvt[:], E["cvv_in"].ap()[l].rearrange("(t p) c -> p t c", p=128))
            for t4 in range(4):
                for kv in range(2):
                    ps = PS(0, 3)
                    kb.tr(ps[0:64, 0:128], ckt[:, t4, kv * 64:(kv + 1) * 64], ident)
                    kb.cp(KT[:, kv, T + t4 * 128:T + (t4 + 1) * 128], ps[0:64, 0:128], eng=EV())
                kb.cp(VS[:, 24 + t4, :, 0:64], cvt[:, t4, :].rearrange("p (k d) -> p k d", k=2), eng="pool")
        chk(kb, "q", l)
        with kb.scope():
            Ws = [kb.sb("Wr%d" % i, [128, 8, 512], BF16) for i in range(2)]
            stg = [kb.sb("rstg%d" % i, [128, 512]) for i in range(3)]
            k = 0
            for wb in range(4):
                ncol = 512 if wb < 3 else 384
                W = Ws[wb % 2]
                kb.dma(W[:, :, 0:ncol], winv[:, :, C_R + wb * 512:C_R + wb * 512 + ncol], q="pool")
                for bb in range(ncol // 128):
                    b = wb * 4 + bb
                    for g in range(NG):
                        ps = PS()
                        for kc in range(8):
                            kb.mm(ps[:], W[:, kc, bb * 128:(bb + 1) * 128], HT[:, kc, g * 512:(g + 1) * 512],
                                  start=(kc == 0), stop=(kc == 7), inc=(kc == 7))
                        s = stg[k % 3]
                        k += 1
                        kb.cp(s[:], ps[:], eng=EV())
                        kb.dma(RKV.ap()[:, b, g * 512:(g + 1) * 512], s[:])
        chk(kb, "rkv", l)
        with kb.scope():
            pts = [kb.sb("pt%d" % i, [128, 512], BF16) for i in range(3)]
            ons = [kb.sb("on%d" % i, [128, 512]) for i in range(2)]
            rcs = [kb.sb("rc%d" % i, [128, 512]) for i in range(2)]
            obs = [kb.sb("ob%d" % i, [64, 512], BF16) for i in range(2)]
            seqs = [(s * 256, 256, [(s * 256 + i * 128, s * 2 + i) for i in range(2)]) for s in range(4)]
            seqs.append((1024, 2048, [(T + i * 128, 24 + i) for i in range(4)] + [(1024 + i * 128, 8 + i) for i in range(16)]))
            n = 0
            apend = [None]
            for (t0, L, keys) in seqs:
                for kv in range(2):
                    for qt in range(L // 128):
                        q0 = t0 + qt * 128
                        po = P[6 + n % 2]
                        nk_ = len(keys)
                        pss = [None] * nk_

                        def score(i):
                            ps = PS(0, 4)
                            kb.mm(ps[:], KT[:, kv, keys[i][0]:keys[i][0] + 128], QT[:, 4 * kv:4 * kv + 4, q0:q0 + 128])
                            pss[i] = ps

                        score(0)
                        for i, (kc0, vt) in enumerate(keys):
                            if i + 1 < nk_:
                                score(i + 1)
                            if i == min(2, nk_ - 1) and apend[0] is not None:
                                apend[0]()
                                apend[0] = None
                            pt = pts[i % 3]
                            kb.act(pt[:], pss[i][:], AF.Exp, scale=0.125)
                            kb.mm(po[0:65, :], VS[:, vt, kv, :], pt[:], start=(i == 0), stop=(i == nk_ - 1))
                        on = ons[n % 2]; rc = rcs[n % 2]; ob = obs[n % 2]
                        kb.cp(on[0:65, :], po[0:65, :], eng="act")
                        kb.recip(rc[64:65, :], on[64:65, :])
                        pb = P[4 + n % 2]

                        def fin(on=on, rc=rc, ob=ob, pb=pb, kv=kv, q0=q0):
                            kb.mm(pb[0:64, :], E["ones_row"][64:65, 0:64], rc[64:65, :], tile_position=(64, 0))
                            kb.tt(ob[:], on[0:64, :], pb[0:64, :], ALU.mult)
                            kb.dma(BRB.ap()[:, 4 * kv:4 * kv + 4, q0:q0 + 128], ob[:].rearrange("p (g t) -> p g t", g=4))

                        apend[0] = fin
                        n += 1
            if apend[0] is not None:
                apend[0]()
                apend[0] = None
    chk(kb, "att", l)
    if "noC" not in E["dbg"]:
        rwkv(kb, l, E)
    chk(kb, "rw", l)
    merge_ffn(kb, l, E)


def mmg(kb, terms):
    order = sorted(range(len(terms)), key=lambda i: (terms[i][4][0], i))
    first, last = {}, {}
    for n, i in enumerate(order):
        r = terms[i][0]
        first.setdefault(r, n)
        last[r] = n
    for n, i in enumerate(order):
        r, out, lhsT, rhs, tp = terms[i]
        kb.mm(out, lhsT, rhs, start=(first[r] == n), stop=(last[r] == n), tile_position=tp)


def rwkv(kb, l, E):
    P = E["P"]; pp = E["pp"][l]; RKV = E["RKV"]; BRC = E["BRC"]; BONS = E["BONS"]; YS = E["YS"]
    ident = E["ident"]; BDm = E["BDm"]; BD64 = E["BD64"]
    NP_ = 256
    fk = [0]

    def FPS():
        fk[0] += 1
        return P[6 + fk[0] % 2]

    def bc4(off, n=NP_):
        return pp[:, off:off + 4].unsqueeze(2).to_broadcast([128, 4, n])

    with kb.scope():
        A = lambda n, s, dt=F32: kb.sb(n, s, dt)
        XR = A("XR", [128, 14, NP_ + 2])
        R = A("rR", [128, 4, NP_]); K = A("rK", [128, 4, NP_]); V = A("rV", [128, 4, NP_]); LO = A("rLO", [128, NP_])
        TW = A("rTW", [128, NP_]); LW = A("rLW", [128, 4, NP_]); AA = A("rAA", [128, 4, NP_])
        KK = A("rKK", [128, 4, NP_])
        CL = A("rCL", [128, 4, NP_]); KM = A("rKM", [128, 4, NP_]); BB = A("rBB", [128, 4, NP_])
        Bh = A("rBh", [128, 4, NP_]); Kh = A("rKh", [128, 4, NP_])
        ARs = [A("rAR%d" % i, [128, 4, 4, 128]) for i in range(2)]
        Bts = [A("rBt%d" % i, [128, 4, NP_]) for i in range(2)]
        Kts = [A("rKt%d" % i, [128, 4, NP_]) for i in range(2)]
        BhTs = [A("rBhT%d" % i, [128, 2, 512]) for i in range(2)]
        KhTs = [A("rKhT%d" % i, [128, 2, 512]) for i in range(2)]
        VTs = [A("rVT%d" % i, [128, 2, 512]) for i in range(2)]
        BONs = [A("rBON%d" % i, [128, 4, NP_]) for i in range(2)]
        GAMs = [A("rGAM%d" % i, [128, 4, 4]) for i in range(2)]
        YPs = [A("rYP%d" % i, [128, 4, NP_]) for i in range(2)]
        F1 = A("fT1", [128, 4, NP_]); F2 = A("fT2", [128, 4, NP_]); FE = A("fEE", [128, 4, NP_])
        WA = [A("rWA%d" % d, [128, 512]) for d in range(2)]
        for d in range(2):
            kb.dma(WA[d][:], E["wa2_in"].ap()[l, d])
        G2 = A("rG2", [128, 512], BF16)
        kb.dma(G2[:], E["g2_in"].ap()[l], q="pool")
        NM = A("cNM", [128, 8, 128]); KMt = A("cKMt", [128, 8, 128])
        Nb = [A("cN%d" % i, [128, 8, 64]) for i in range(2)]
        NTb = [A("cNT%d" % i, [128, 8, 64]) for i in range(2)]
        TTb = [A("cTT%d" % i, [128, 8, 64]) for i in range(2)]
        RHS = A("cRHS", [128, 512]); U = A("cU", [128, 512]); RHS0 = A("cRHS0", [128, 512])
        ST = [A("cST%d" % i, [128, 4, 64]) for i in range(2)]
        SGD = A("rSGD", [128, NP_], BF16); GD = A("rGD", [128, NP_]); OC = A("rOC", [128, 4, NP_], BF16)
        v64 = lambda ap: ap.rearrange("p a (c t) -> p a c t", t=64)

        items = []
        for (t0, L, latent, sidx) in [(s * 256, 256, False, s) for s in range(4)] + [(1024, 2048, True, None)]:
            npc = L // NP_
            for d in range(2):
                order = list(range(npc)) if d == 0 else list(range(npc - 1, -1, -1))
                for j, pc in enumerate(order):
                    items.append(dict(t0=t0, latent=latent, sidx=sidx, d=d, pc=pc, npc=npc,
                                      first=(j == 0), last=(j == npc - 1)))
        sic = [0]

        def front(it, b):
            d, pc, npc = it["d"], it["pc"], it["npc"]
            p0 = it["t0"] + pc * NP_
            AR, Bt, Kt, BhT, KhT, VT, BON, GAM, YP = ARs[b], Bts[b], Kts[b], BhTs[b], KhTs[b], VTs[b], BONs[b], GAMs[b], YPs[b]
            lo_h = 1 if pc == 0 else 0
            hi_h = NP_ + 1 if pc == npc - 1 else NP_ + 2
            if pc == 0:
                kb.memset(XR[:, :, 0:1], 0.0)
            if pc == npc - 1:
                kb.memset(XR[:, :, NP_ + 1:NP_ + 2], 0.0)
            kb.dma(XR[:, :, lo_h:hi_h], RKV.ap()[:, 0:14, p0 - 1 + lo_h:p0 - 1 + hi_h])
            yield
            sh = 0 if d == 0 else 2
            dsts = [R[:, i, :] for i in range(4)] + [K[:, i, :] for i in range(4)] + [V[:, i, :] for i in range(4)] + [LO[:]]
            for blk in range(13):
                sbk = blk if blk < 12 else 12 + d
                mu = pp[:, O_MU + d * 13 + blk:O_MU + d * 13 + blk + 1]
                omu = pp[:, O_OMU + d * 13 + blk:O_OMU + d * 13 + blk + 1]
                tmp = F1[:, blk % 4, :]
                kb.ts(tmp, XR[:, sbk, sh:sh + NP_], mu, ALU.mult, eng="pool")
                kb.stt(dsts[blk], XR[:, sbk, 1:NP_ + 1], omu, tmp, ALU.mult, ALU.add)
                if blk % 3 == 2:
                    yield
            kb.act(TW[0:64, :], LO[0:64, :], AF.Tanh)
            for pr in range(4):
                ps = FPS()
                kb.mm(ps[:, 0:NP_], WA[d][0:64, pr * 128:(pr + 1) * 128], TW[0:64, :])
                kb.act(LW[:, pr, :], ps[:, 0:NP_], AF.Sigmoid, bias=pp[:, O_W0 + d * 4 + pr:O_W0 + d * 4 + pr + 1])
                ps = FPS()
                kb.mm(ps[:, 0:NP_], WA[d][64:128, pr * 128:(pr + 1) * 128], LO[64:128, :], tile_position=(64, 0))
                kb.act(AA[:, pr, :], ps[:, 0:NP_], AF.Sigmoid, bias=pp[:, O_A0 + d * 4 + pr:O_A0 + d * 4 + pr + 1])
                yield
            kb.ts(LW[:], LW[:], -0.6065306597126334, ALU.mult, eng="pool")
            for pr in range(4):
                kb.I("dve", lambda e, pr=pr: e.tensor_tensor_scan(CL[:, pr, :], E["rst"][:, 0:NP_], LW[:, pr, :], 0.0, ALU.mult, ALU.add),
                     w=[CL[:]], r=[LW[:], E["cst"][:]])
            yield
            if d == 0:
                tot = v64(CL[:])[:, :, :, 63:64]
            else:
                totf = v64(CL[:])[:, :, :, 63:64].to_broadcast([128, 4, 4, 64])
                kb.tt(v64(F2[:]), totf, v64(CL[:]), ALU.subtract)
                kb.tt(CL[:], F2[:], LW[:], ALU.add)
                tot = v64(CL[:])[:, :, :, 0:1]
            kb.act(GAM[:].unsqueeze(3), tot, AF.Exp)
            yield
            kb.tt(KK[:], K[:], bc4(O_KK), ALU.mult, eng="pool")
            kb.tt(F1[:], KK[:], KK[:], ALU.mult, eng="pool")
            for hf in range(2):
                ps = FPS()
                kb.mm(ps[:], BDm, F1[:, 2 * hf:2 * hf + 2, :])
                kb.rsqrt(F2[:, 2 * hf:2 * hf + 2, :], ps[:].rearrange("p (a t) -> p a t", a=2), 1e-24)
            yield
            kb.tt(KK[:], KK[:], F2[:], ALU.mult)
            kb.tt(F1[:], AA[:], bc4(O_KA), ALU.mult, eng="pool")
            kb.tt(F1[:], F1[:], bc4(O_OMKA), ALU.add, eng="pool")
            kb.tt(KM[:], K[:], F1[:], ALU.mult)
            kb.tt(BB[:], KK[:], AA[:], ALU.mult, eng="pool")
            yield
            kb.tt(F1[:], R[:], KM[:], ALU.mult, eng="pool")
            kb.tt(F1[:], F1[:], bc4(O_RK), ALU.mult, eng="pool")
            for hf in range(2):
                ps = FPS()
                kb.mm(ps[:], BDm, F1[:, 2 * hf:2 * hf + 2, :])
                kb.tt(BON[:, 2 * hf:2 * hf + 2, :], ps[:].rearrange("p (a t) -> p a t", a=2), V[:, 2 * hf:2 * hf + 2, :], ALU.mult)
            if d == 0:
                kb.dma(BONS.ap()[:, :, p0:p0 + NP_], BON[:])
            else:
                kb.dma(F2[:], BONS.ap()[:, :, p0:p0 + NP_])
                kb.tt(BON[:], BON[:], F2[:], ALU.add)
            yield
            kb.tt(F1[:], CL[:], LW[:], ALU.subtract, eng="pool")
            kb.act(FE[:], F1[:], AF.Exp)
            kb.stt(AR[:, :, :, 0:64], v64(KK[:]), -1.0, v64(FE[:]), ALU.mult, ALU.mult)
            yield
            kb.act(FE[:], CL[:], AF.Exp)
            kb.tt(AR[:, :, :, 64:128], v64(R[:]), v64(FE[:]), ALU.mult)
            yield
            kb.act(FE[:], CL[:], AF.Exp, scale=-1.0)
            kb.tt(Bt[:], BB[:], FE[:], ALU.mult)
            kb.tt(Kt[:], KM[:], FE[:], ALU.mult, eng="pool")
            yield
            kb.tt(v64(F1[:]), tot.to_broadcast([128, 4, 4, 64]), v64(CL[:]), ALU.subtract)
            kb.act(FE[:], F1[:], AF.Exp)
            kb.tt(Bh[:], BB[:], FE[:], ALU.mult)
            kb.tt(Kh[:], KM[:], FE[:], ALU.mult, eng="pool")
            yield
            for (src, dst) in ((Bh, BhT), (Kh, KhT), (V, VT)):
                for tb in range(2):
                    ps = FPS()
                    for pr in range(4):
                        kb.tr(ps[:, pr * 128:(pr + 1) * 128], src[:, pr, tb * 128:(tb + 1) * 128], ident)
                    kb.cp(dst[:, tb, :], ps[:], eng=("act" if tb else "dve"))
                    yield

        def pairs(it, b, step):
            d, pc = it["d"], it["pc"]
            AR, Bt, Kt, BhT, KhT, VT, GAM, YP = ARs[b], Bts[b], Kts[b], BhTs[b], KhTs[b], VTs[b], GAMs[b], YPs[b]
            if d == 1:
                p0_ = it["t0"] + pc * NP_
                kb.dma(YP[:], YS.ap()[:, :, p0_:p0_ + NP_])
            if it["first"]:
                sic[0] = 0
                if it["latent"]:
                    kb.dma(ST[0][:], E["st_in"].ap()[l, d].rearrange("p (a v) -> p a v", a=4))
                else:
                    kb.memset(ST[0][:], 0.0)
            mask4 = E["maskF4"] if d == 0 else E["maskB4"]
            maskT8 = E["maskFT8"] if d == 0 else E["maskBT8"]
            for tb in (range(2) if d == 0 else range(1, -1, -1)):
                PA = [P[0], P[1]]; PB = [P[2], P[3]]; PC = [P[4], P[5]]
                for ci in range(2):
                    c = 2 * tb + ci; cb = ci * 64; cs = slice(c * 64, (c + 1) * 64)
                    for h in range(8):
                        par = h % 2; hb = par * 64; pr = h // 2
                        lastm = (h >= 6)
                        kb.mm(PA[par][cb:cb + 64, pr * 128:(pr + 1) * 128], Bt[hb:hb + 64, pr, cs],
                              AR[hb:hb + 64, pr, c, :], tile_position=(hb, cb), inc=lastm)
                        kb.mm(PB[par][cb:cb + 64, pr * 128:(pr + 1) * 128], Kt[hb:hb + 64, pr, cs],
                              AR[hb:hb + 64, pr, c, :], tile_position=(hb, cb), inc=lastm)
                        kb.mm(PC[par][cb:cb + 64, pr * 64:(pr + 1) * 64], AR[hb:hb + 64, pr, c, 0:64],
                              Bt[hb:hb + 64, pr, cs], tile_position=(hb, cb), inc=lastm)
                m4 = mask4.rearrange("p (h t) -> p h t", h=4)
                for par in range(2):
                    kb.tt(NM[:, par:8:2, :], PA[par][:].rearrange("p (h t) -> p h t", h=4), m4, ALU.mult)
                    kb.tt(KMt[:, par:8:2, :], PB[par][:].rearrange("p (h t) -> p h t", h=4), m4, ALU.mult)
                    kb.tt(NTb[0][:, par:8:2, :], PC[par][:, 0:256].rearrange("p (h t) -> p h t", h=4),
                          maskT8[:, 0:256].rearrange("p (h t) -> p h t", h=4), ALU.mult)
                kb.cp(Nb[0][:], NM[:, :, 0:64], eng="pool")
                kb.tt(TTb[0][:], Nb[0][:], E["ident8"].rearrange("p (h t) -> p h t", h=8), ALU.add, eng="pool")
                step()
                cur = 0
                for lev in range(5):
                    nx = 1 - cur
                    Nc, NTc, TTc = Nb[cur], NTb[cur], TTb[cur]
                    PNT, PN, PTt = P[0], P[1], P[2]
                    for ci in range(2):
                        cb = ci * 64
                        for h in range(8):
                            hs = slice(h * 64, (h + 1) * 64)
                            kb.mm(PNT[cb:cb + 64, hs], Nc[cb:cb + 64, h, :], NTc[cb:cb + 64, h, :], tile_position=(cb, cb), inc=(h == 7))
                            if lev < 4:
                                kb.mm(PN[cb:cb + 64, hs], NTc[cb:cb + 64, h, :], Nc[cb:cb + 64, h, :], tile_position=(cb, cb), inc=(h == 7))
                    kb.cp(NTb[nx][:], PNT[:].rearrange("p (h t) -> p h t", h=8), eng="act")
                    if lev < 4:
                        kb.cp(Nb[nx][:], PN[:].rearrange("p (h t) -> p h t", h=8), eng="dve")
                    for ci in range(2):
                        cb = ci * 64
                        for h in range(8):
                            hs = slice(h * 64, (h + 1) * 64)
                            kb.mm(PTt[cb:cb + 64, hs], NTb[nx][cb:cb + 64, h, :], TTc[cb:cb + 64, h, :], tile_position=(cb, cb), inc=(h == 7))
                    kb.tt(TTb[nx][:], TTc[:], PTt[:].rearrange("p (h t) -> p h t", h=8), ALU.add)
                    cur = nx
                    step()
                TT = TTb[cur]
                PX = P[5]
                for ci in range(2):
                    cb = ci * 64
                    for h in range(8):
                        hs = slice(h * 64, (h + 1) * 64)
                        kb.mm(PX[cb:cb + 64, hs], KMt[cb:cb + 64, h, 0:64], VT[cb:cb + 64, tb, hs], tile_position=(cb, cb), inc=(h == 7))
                kb.cp(RHS0[:], PX[:], eng="act")
                step()
                for ci in ((0, 1) if d == 0 else (1, 0)):
                    c = 2 * tb + ci; cb = ci * 64
                    Sc = ST[sic[0] % 2]; Sn = ST[(sic[0] + 1) % 2]
                    sic[0] += 1
                    PR, PU, PYa, PYb, PSn = P[3], P[4], P[5], P[0], P[1]
                    for h in (0, 2, 4, 6, 1, 3, 5, 7):
                        hb = (h % 2) * 64; pr = h // 2; hs = slice(h * 64, (h + 1) * 64)
                        kb.mm(PR[cb:cb + 64, hs], AR[hb:hb + 64, pr, c, 0:64], Sc[hb:hb + 64, pr, :], tile_position=(hb, cb), inc=(h >= 6))
                    kb.tt(RHS[cb:cb + 64, :], PR[cb:cb + 64, :], RHS0[cb:cb + 64, :], ALU.add)
                    step()
                    for h in range(8):
                        hs = slice(h * 64, (h + 1) * 64)
                        kb.mm(PU[cb:cb + 64, hs], TT[cb:cb + 64, h, :], RHS[cb:cb + 64, hs], tile_position=(cb, cb), inc=(h == 7))
                    kb.cp(U[cb:cb + 64, :], PU[cb:cb + 64, :], eng="act")
                    step()
                    for h in (0, 2, 4, 6, 1, 3, 5, 7):
                        hb = (h % 2) * 64; pr = h // 2
                        ys = slice(pr * 64, (pr + 1) * 64)
                        kb.mm(PYa[hb:hb + 64, ys], Sc[hb:hb + 64, pr, :], AR[hb:hb + 64, pr, c, 64:128], tile_position=(hb, hb), inc=(h >= 6))
                    for h in range(8):
                        hb = (h % 2) * 64; pr = h // 2; hs = slice(h * 64, (h + 1) * 64)
                        ys = slice(pr * 64, (pr + 1) * 64)
                        kb.mm(PSn[hb:hb + 64, ys], BhT[cb:cb + 64, tb, hs], U[cb:cb + 64, hs],
                              start=True, stop=False, tile_position=(cb, hb), inc=False)
                        kb.mm(PSn[hb:hb + 64, ys], KhT[cb:cb + 64, tb, hs], VT[cb:cb + 64, tb, hs],
                              start=False, stop=True, tile_position=(cb, hb), inc=(h >= 6))
                    kb.tt(Sn[:], Sc[:], GAM[:, :, c:c + 1].to_broadcast([128, 4, 64]), ALU.mult, eng="pool")
                    kb.tt(Sn[:], Sn[:], PSn[:, 0:256].rearrange("p (a t) -> p a t", a=4), ALU.add)
                    for h in range(8):
                        hb = (h % 2) * 64; pr = h // 2; hs = slice(h * 64, (h + 1) * 64)
                        ys = slice(pr * 64, (pr + 1) * 64)
                        kb.mm(PYb[hb:hb + 64, ys], U[cb:cb + 64, hs], NM[cb:cb + 64, h, 64:128],
                              start=True, stop=False, tile_position=(cb, hb), inc=False)
                        kb.mm(PYb[hb:hb + 64, ys], VT[cb:cb + 64, tb, hs], KMt[cb:cb + 64, h, 64:128],
                              start=False, stop=True, tile_position=(cb, hb), inc=(h >= 6))
                    ytk = slice(c * 64, (c + 1) * 64)
                    pya = PYa[:, 0:256].rearrange("p (a t) -> p a t", a=4)
                    pyb = PYb[:, 0:256].rearrange("p (a t) -> p a t", a=4)
                    if d == 0:
                        kb.cp(YP[:, :, ytk], pya, eng="act")
                    else:
                        kb.tt(YP[:, :, ytk], YP[:, :, ytk], pya, ALU.add)
                    kb.tt(YP[:, :, ytk], YP[:, :, ytk], pyb, ALU.add)
                    step()
            if it["last"] and not it["latent"]:
                kb.dma(E["ns_out"].ap()[it["sidx"], l, d].rearrange("p (a v) -> p a v", a=4), ST[sic[0] % 2][:])

        def post(it, b):
            d, pc = it["d"], it["pc"]
            p0 = it["t0"] + pc * NP_
            YP, BON = YPs[b], BONs[b]
            if d == 0:
                kb.dma(YS.ap()[:, :, p0:p0 + NP_], YP[:])
                return
            for hf in range(2):
                ps = FPS()
                kb.mm(ps[:], BD64, YP[:, 2 * hf:2 * hf + 2, :])
                kb.tt(F1[:, 2 * hf:2 * hf + 2, :], YP[:, 2 * hf:2 * hf + 2, :], ps[:].rearrange("p (a t) -> p a t", a=2), ALU.subtract)
            kb.tt(F2[:], F1[:], F1[:], ALU.mult, eng="pool")
            for hf in range(2):
                ps = FPS()
                kb.mm(ps[:], BD64, F2[:, 2 * hf:2 * hf + 2, :])
                kb.rsqrt(FE[:, 2 * hf:2 * hf + 2, :], ps[:].rearrange("p (a t) -> p a t", a=2), 64e-5)
            kb.tt(F1[:], F1[:], FE[:], ALU.mult)
            kb.tt(F1[:], F1[:], bc4(O_LXG), ALU.mult, eng="pool")
            kb.tt(F1[:], F1[:], bc4(O_LXB), ALU.add, eng="pool")
            kb.tt(F1[:], F1[:], BON[:], ALU.add)
            kb.dma(GD[:], RKV.ap()[:, 14, p0:p0 + NP_])
            kb.act(SGD[:], GD[:], AF.Sigmoid)
            for pr in range(4):
                ps = FPS()
                kb.mm(ps[:, 0:NP_], G2[:, pr * 128:(pr + 1) * 128], SGD[:])
                kb.tt(OC[:, pr, :], F1[:, pr, :], ps[:, 0:NP_], ALU.mult)
            kb.dma(BRC.ap()[:, :, p0:p0 + NP_], OC[:])

        g0 = front(items[0], 0)
        for _ in g0:
            pass
        for i, it in enumerate(items):
            b = i % 2
            nxt = front(items[i + 1], 1 - b) if i + 1 < len(items) else None

            def step(nxt=nxt):
                if nxt is not None:
                    next(nxt, None)

            pairs(it, b, step)
            if nxt is not None:
                for _ in nxt:
                    pass
            post(it, b)


def merge_ffn(kb, l, E):
    P = E["P"]; pp = E["pp"][l]; modT = E["modT"][l]
    XTi = E["XT"][l % 2]; XTo = E["XT"][(l + 1) % 2]
    XM = E["XM"]
    BRA = E["BRA"]; BRB = E["BRB"]; BRC = E["BRC"]; onesD = E["onesD"]
    winv = E["w_in"].ap()[l].rearrange("(kc p) n -> p kc n", p=128)
    pk = [0]

    def PS(lo=0, hi=8):
        i = lo + pk[0] % (hi - lo)
        pk[0] += 1
        return P[i]

    def lnorm(Y, sqs, sc3, og, ob, n):
        pm = PS(0, 2); pq = PS(2, 4)
        for c in range(8):
            kb.mm(pm[:, 0:n], onesD, Y[:, c, :], start=(c == 0), stop=(c == 7), inc=(c == 7))
        for c in range(8):
            sq = sqs[c % 2]
            kb.act(sq[:, 0:n], Y[:, c, :], AF.Square)
            kb.mm(pq[:, 0:n], onesD, sq[:, 0:n], start=(c == 0), stop=(c == 7))
        mean = sc3[:, 0, 0:n]; rstd = sc3[:, 1, 0:n]; tmp = sc3[:, 2, 0:n]
        kb.cp(mean, pm[:, 0:n], eng="act")
        kb.tt(tmp, mean, mean, ALU.mult, eng="pool")
        kb.tt(tmp, pq[:, 0:n], tmp, ALU.subtract)
        kb.rsqrt(rstd, tmp, 1e-5)
        for c in range(8):
            e = "dve" if c % 2 else "pool"
            kb.tt(Y[:, c, :], Y[:, c, :], mean, ALU.subtract, eng=e)
            kb.tt(Y[:, c, :], Y[:, c, :], rstd, ALU.mult, eng=e)
            kb.ts(Y[:, c, :], Y[:, c, :], pp[:, og + c:og + c + 1], ALU.mult, pp[:, ob + c:ob + c + 1], ALU.add, eng=e)

    noC = "noC" in E["dbg"]
    with kb.scope():
        WG = kb.sb("mWG", [128, 8, 3072], BF16)
        WB = kb.sb("mWB", [128, 8, D], BF16)
        wbrv = E["w_br"].ap()[l]
        WBB = kb.sb("mWBB", [64, 8, D], BF16)
        WO = kb.sb("mWO", [128, 8, D], BF16)
        wov_ = E["w_out"].ap()[l].rearrange("(kc p) n -> p kc n", p=128)
        for hf_ in range(2):
            hsl = slice(hf_ * 512, (hf_ + 1) * 512)
            for j in range(3):
                i = j * 2 + hf_
                kb.dma((WG[:, :, i * 512:(i + 1) * 512], i), winv[:, :, C_GL + i * 512:C_GL + (i + 1) * 512], q="pool")
            kb.dma((WB[:, 0:4, hsl], hf_), wbrv[0].rearrange("(cc p) d -> p cc d", p=128)[:, :, hsl], q="pool")
            kb.dma((WB[:, 4:8, hsl], 2 + hf_), wbrv[2].rearrange("(cc p) d -> p cc d", p=128)[:, :, hsl], q="pool")
            kb.dma((WBB[:, :, hsl], hf_), wbrv[1].rearrange("(h p) d -> p h d", p=64)[:, :, hsl], q="pool")
        for hf_ in range(2):
            hsl = slice(hf_ * 512, (hf_ + 1) * 512)
            kb.dma((WO[:, :, hsl], hf_), wov_[:, :, hsl], q="pool")
        XLs = [kb.sb("mXL%d" % i, [128, 8, 512]) for i in range(2)]
        HT = kb.sb("mHT", [128, 8, 512], BF16)
        OA = kb.sb("mOA", [128, 4, 512], BF16); OB = kb.sb("mOB", [64, 8, 512], BF16); OCt = kb.sb("mOC", [128, 4, 512], BF16)
        MI = kb.sb("mMI", [128, 8, 512], BF16)
        sqs = [kb.sb("msq%d" % i, [128, 512]) for i in range(2)]
        sc3 = kb.sb("msc3", [128, 3, 512])
        gts = [kb.sb("mgt%d" % i, [128, 512]) for i in range(2)]
        acc = kb.sb("macc", [128, 512])

        def prep(g):
            XL = XLs[g % 2]
            tok = slice(g * 512, (g + 1) * 512)
            kb.dma(XL[:], XTi.ap()[:, :, tok])
            kb.dma(OA[:], BRA.ap()[:, :, tok]); kb.dma(OB[:], BRB.ap()[:, :, tok])
            if not noC:
                kb.dma(OCt[:], BRC.ap()[:, :, tok])
            sc = mod_cols(modT, 8, g); sh = mod_cols(modT, 0, g)
            for c in range(8):
                kb.ts(HT[:, c, :], XL[:, c, :], sc[c], ALU.mult, sh[c], ALU.add, eng=("dve" if c % 2 else "pool"))

        def mix(g):
            for dc in range(8):
                dcs = slice(dc * 128, (dc + 1) * 128)
                for j in range(3):
                    if j == 2 and noC:
                        continue
                    pg = PS(0, 3); pb = PS(3, 6)
                    for kc in range(8):
                        kb.mm(pg[:], (WG[:, kc, j * D + dc * 128:j * D + (dc + 1) * 128], j * 2 + dc // 4), HT[:, kc, :], start=(kc == 0), stop=(kc == 7), inc=(kc == 7))
                    if j != 1:
                        src = OA if j == 0 else OCt
                        for cc in range(4):
                            kb.mm(pb[:], (WB[:, (j // 2) * 4 + cc, dcs], (j // 2) * 2 + dc // 4), src[:, cc, :], start=(cc == 0), stop=(cc == 3), inc=(cc == 3))
                    else:
                        for h in range(8):
                            kb.mm(pb[:], (WBB[:, h, dcs], dc // 4), OB[:, h, :], start=(h == 0), stop=(h == 7), inc=(h == 7))
                    gt = gts[j % 2]
                    kb.act(gt[:], pg[:], AF.Sigmoid)
                    if j == 0:
                        kb.tt(acc[:], gt[:], pb[:], ALU.mult)
                    else:
                        kb.tt(gt[:], gt[:], pb[:], ALU.mult)
                        kb.tt((MI[:, dc, :] if (j == 2 or (noC and j == 1)) else acc[:]), acc[:], gt[:], ALU.add, eng="pool")

        def wout_ln(g):
            XL = XLs[g % 2]
            g1 = mod_cols(modT, 16, g)
            for dc in range(8):
                ps = PS(6, 8)
                for kc in range(8):
                    kb.mm(ps[:], (WO[:, kc, dc * 128:(dc + 1) * 128], dc // 4), MI[:, kc, :], start=(kc == 0), stop=(kc == 7), inc=(kc == 7))
                kb.ts(XL[:, dc, :], XL[:, dc, :], ALPHA, ALU.mult, eng="pool")
                kb.stt(XL[:, dc, :], ps[:], g1[dc], XL[:, dc, :], ALU.mult, ALU.add)
            lnorm(XL, sqs, sc3, O_LN1G, O_LN1B, 512)
            kb.dma(XM.ap()[:, :, g * 512:(g + 1) * 512], XL[:])

        prep(0)
        for g in range(NG):
            mix(g)
            if g + 1 < NG:
                prep(g + 1)
            wout_ln(g)
    chk(kb, "mrg", l)
    with kb.scope():
        WU = kb.sb("fWU", [128, 8, 4096], BF16)
        upv = E["w_up"].ap()[l].rearrange("(kc p) n -> p kc n", p=128)
        WD = kb.sb("fWD", [128, 32, D], BF16)
        dnv = E["w_dn"].ap()[l].rearrange("(fc p) n -> p fc n", p=128)
        for i in range(8):
            kb.dma((WU[:, :, i * 512:(i + 1) * 512], i), upv[:, :, i * 512:(i + 1) * 512], q="pool")
        for i in range(8):
            kb.dma((WD[:, i * 4:(i + 1) * 4, :], i), dnv[:, i * 4:(i + 1) * 4, :], q="pool")
        NF = 256
        NGF = T // NF
        X1s = [kb.sb("fX1%d" % i, [128, 8, NF]) for i in range(2)]
        H2 = kb.sb("fH2", [128, 8, NF], BF16)
        ACTt = kb.sb("fACT", [128, 32, NF], BF16)
        sqs = [kb.sb("fsq%d" % i, [128, 512]) for i in range(2)]
        sc3 = kb.sb("fsc3", [128, 3, 512])
        rl = [kb.sb("frl%d" % i, [128, NF]) for i in range(2)]

        def prep_up(gg):
            g = (gg * NF) // 512
            X1 = X1s[gg % 2]
            kb.dma(X1[:], XM.ap()[:, :, gg * NF:(gg + 1) * NF])
            sc2 = mod_cols(modT, 32, g); sh2 = mod_cols(modT, 24, g)
            for c in range(8):
                kb.ts(H2[:, c, :], X1[:, c, :], sc2[c], ALU.mult, sh2[c], ALU.add, eng=("dve" if c % 2 else "pool"))
            for fc in range(32):
                ps = PS(0, 4)
                for kc in range(8):
                    kb.mm(ps[:, 0:NF], (WU[:, kc, fc * 128:(fc + 1) * 128], fc // 4), H2[:, kc, :], start=(kc == 0), stop=(kc == 7), inc=(kc == 7))
                r_ = rl[fc % 2]
                kb.act(r_[:], ps[:, 0:NF], AF.Relu)
                kb.tt(ACTt[:, fc, :], r_[:], r_[:], ALU.mult, eng=("dve" if fc % 2 else "pool"))

        def down(gg):
            g = (gg * NF) // 512
            X1 = X1s[gg % 2]
            g2 = mod_cols(modT, 40, g)
            for dc in range(8):
                ps = PS(4, 8)
                for fc in range(32):
                    kb.mm(ps[:, 0:NF], (WD[:, fc, dc * 128:(dc + 1) * 128], fc // 4), ACTt[:, fc, :], start=(fc == 0), stop=(fc == 31), inc=(fc == 31))
                kb.ts(X1[:, dc, :], X1[:, dc, :], ALPHA, ALU.mult, eng="pool")
                kb.stt(X1[:, dc, :], ps[:, 0:NF], g2[dc], X1[:, dc, :], ALU.mult, ALU.add)

        def ln_out(gg):
            X1 = X1s[gg % 2]
            lnorm(X1, sqs, sc3, O_LN2G, O_LN2B, NF)
            kb.dma(XTo.ap()[:, :, gg * NF:(gg + 1) * NF], X1[:])

        prep_up(0)
        for gg in range(NGF):
            down(gg)
            if gg + 1 < NGF:
                prep_up(gg + 1)
            ln_out(gg)


def _consts():
    c = np.zeros((128, 9, 512), np.float32)
    r = np.arange(128)
    c[:, 0, 0:128] = np.eye(128)
    bd = (r[:, None] // 64 == r[None, :] // 64).astype(np.float32)
    c[:, 0, 128:256] = bd
    c[:, 0, 256:384] = bd / 64.0
    c[0:64, 0, 384:448] = 1.0 / 64.0
    s = (r % 64)[:, None]
    col = np.arange(128)[None, :]
    mf = np.where(col < 64, s < col, s <= col - 64).astype(np.float32)
    mb = np.where(col < 64, s > col, s >= col - 64).astype(np.float32)
    c[:, 1, :] = np.tile(mf, (1, 4))
    c[:, 2, :] = np.tile(mb, (1, 4))
    t = (r % 64)[:, None]
    sc = np.arange(64)[None, :]
    c[:, 3, :] = np.tile((sc < t).astype(np.float32), (1, 8))
    c[:, 4, :] = np.tile((sc > t).astype(np.float32), (1, 8))
    c[:, 5, :] = np.tile((sc == t).astype(np.float32), (1, 8))
    c[:, 6, :] = (np.arange(512) % 64 != 0).astype(np.float32)[None, :]
    rm = np.zeros((64, 64), np.float32)
    for d in range(64):
        i = d % 32
        if i < 16:
            rm[d + 16, d] = -1.0
        else:
            rm[d - 16, d] = 1.0
    c[0:64, 7, 0:64] = rm
    c[:, 7, 64:192] = 1.0
    c[:, 8, 0:128] = 1.0 / 1024.0
    tt = np.arange(2048)
    row = (tt // 64).astype(np.float32)
    colp = (tt % 64).astype(np.float32)
    inv = (10000.0 ** (-np.arange(0, 32, 2, dtype=np.float32) / 32.0)).astype(np.float32)
    rope = np.zeros((64, 2, 2048), np.float32)
    for d in range(64):
        pos = row if d < 32 else colp
        ang = (pos * inv[(d % 32) % 16]).astype(np.float32)
        rope[d, 0] = np.cos(ang)
        rope[d, 1] = np.sin(ang)
    return c, rope


def _colT(v, n):
    return np.ascontiguousarray(np.asarray(v, np.float32).reshape(n, 128).T)


_NC_CACHE = {}


def kernel(**inp):
    f = lambda k: np.asarray(inp[k], np.float32)
    n = inp.get("_ncores", 8)
    cst, rope = _consts()
    pp = np.zeros((2, 128, NPP), np.float32)
    for l in range(2):
        pp[l, :, O_BADA:O_BADA + 48] = _colT(f("b_ada")[l], 48)
        pp[l, :, O_LN1G:O_LN1G + 8] = _colT(f("ln1_g")[l], 8)
        pp[l, :, O_LN1B:O_LN1B + 8] = _colT(f("ln1_b")[l], 8)
        pp[l, :, O_LN2G:O_LN2G + 8] = _colT(f("ln2_g")[l], 8)
        pp[l, :, O_LN2B:O_LN2B + 8] = _colT(f("ln2_b")[l], 8)
        pp[l, 0:64, O_QN] = f("q_norm")[l]
        pp[l, 0:64, O_KN] = f("k_norm")[l]
        for d in range(2):
            pp[l, :, O_MU + d * 13:O_MU + (d + 1) * 13] = _colT(f("rwkv_mu")[l, d], 13)
            pp[l, :, O_W0 + d * 4:O_W0 + (d + 1) * 4] = _colT(f("rwkv_w0")[l, d], 4)
            pp[l, :, O_A0 + d * 4:O_A0 + (d + 1) * 4] = _colT(f("rwkv_a0")[l, d], 4)
        pp[l, :, O_KK:O_KK + 4] = _colT(f("rwkv_k_k")[l], 4)
        pp[l, :, O_KA:O_KA + 4] = _colT(f("rwkv_k_a")[l], 4)
        pp[l, :, O_RK:O_RK + 4] = _colT(f("rwkv_r_k")[l].reshape(512), 4)
        pp[l, :, O_LXG:O_LXG + 4] = _colT(f("rwkv_lnx_g")[l], 4)
        pp[l, :, O_LXB:O_LXB + 4] = _colT(f("rwkv_lnx_b")[l], 4)
    wsT = np.ascontiguousarray(f("sgu_w").transpose(0, 3, 1, 2))
    sgb = np.ascontiguousarray(f("sgu_b").reshape(2, 1, 512))
    lnA = np.ascontiguousarray(np.broadcast_to(
        np.stack([f("sgu_ln_g"), f("sgu_ln_b")], 1)[:, :, None, :], (2, 2, 128, 512)))
    wa2 = np.ascontiguousarray(np.concatenate([f("rwkv_w2"), f("rwkv_a2")], axis=2))
    shared = dict(pp=pp, w_ada=f("w_ada"), w_in=f("w_in"), w_branch=f("w_branch"), w_out=f("w_out"),
                  w_up=f("w_up"), w_down=f("w_down"), wsT=wsT, sgb=sgb, lnA=lnA, wa2=wa2, g2=f("rwkv_g2"),
                  cst=cst, rope=rope)
    xp, xs = f("x_prompt"), f("x_sample")
    ck, cvv, st, c, cctx = f("cache_k"), f("cache_v"), f("state_wkv"), f("c"), f("c_ctx")
    in_maps = []
    for i in range(n):
        m = dict(shared)
        m["x"] = np.ascontiguousarray(np.concatenate([xp[4 * i:4 * i + 4].reshape(1024, D), xs[i]], 0))
        m["ck"] = np.ascontiguousarray(ck[i].reshape(2, 512, 128))
        m["cvv"] = np.ascontiguousarray(cvv[i].reshape(2, 512, 128))
        s_ = st[i].reshape(2, 2, 4, 2, 64, 64)
        m["st0"] = np.ascontiguousarray(s_.transpose(0, 1, 3, 5, 2, 4).reshape(2, 2, 128, 256))
        cvec = np.stack([cctx, c[i]], -1)
        m["cvec"] = np.ascontiguousarray(cvec.reshape(8, 128, 2).transpose(1, 0, 2))
        in_maps.append(m)
    if inp.get("_prep_only"):
        return in_maps
    if "nc" not in _NC_CACHE:
        _NC_CACHE["nc"] = build()
    kb = _NC_CACHE["nc"]
    res = run_bass_kernel_spmd(kb.nc, in_maps, core_ids=list(range(n)))
    yp = np.zeros((32, 256, D), np.float32)
    ys = np.zeros((8, 2048, D), np.float32)
    nk = np.zeros((32, 2, 256, 2, 64), np.float32)
    nv = np.zeros((32, 2, 256, 2, 64), np.float32)
    ns = np.zeros((32, 2, 2, 8, 64, 64), np.float32)
    for i in range(n):
        r = res.results[i]
        yp[4 * i:4 * i + 4] = r["y"][:1024].reshape(4, 256, D)
        ys[i] = r["y"][1024:]
        nk[4 * i:4 * i + 4] = r["nk"].reshape(2, 4, 256, 2, 64).transpose(1, 0, 2, 3, 4)
        nv[4 * i:4 * i + 4] = r["nv"].reshape(2, 4, 256, 2, 64).transpose(1, 0, 2, 3, 4)
        s_ = r["ns"].reshape(4, 2, 2, 2, 64, 4, 64)
        ns[4 * i:4 * i + 4] = s_.transpose(0, 1, 2, 5, 3, 6, 4).reshape(4, 2, 2, 8, 64, 64)
    return (yp, ys, nk, nv, ns)
```

```python
import numpy as np
from contextlib import ExitStack
import concourse.bass as bass
import concourse.mybir as mybir
from concourse.bass_utils import run_bass_kernel_spmd

F32 = mybir.dt.float32
F32R = mybir.dt.float32r
BF16 = mybir.dt.bfloat16
AF = mybir.ActivationFunctionType
ALU = mybir.AluOpType
AX = mybir.AxisListType


class KB:
    def __init__(self, n_dma_sems=12):
        self.nc = bass.Bass("TRN2", target_bir_lowering=False)
        nc = self.nc
        self.es = ExitStack()
        self.es_root = self.es
        self.eng = {"pe": nc.tensor, "dve": nc.vector, "act": nc.scalar, "pool": nc.gpsimd, "sp": nc.sync}
        self.sem = {}
        self.cnt = {}
        self.clock = {}
        self.snap = {}
        for e in self.eng:
            self.sem[e] = self.es.enter_context(nc.semaphore("s_" + e))
            self.cnt[e] = 0
            self.clock[e] = {}
            self.snap[e] = {}
        self.dq = {}
        for q in ("sp", "pool", "act"):
            ids = []
            for i in range(n_dma_sems):
                sid = "d_%s_%d" % (q, i)
                self.sem[sid] = self.es.enter_context(nc.semaphore(sid))
                self.cnt[sid] = 0
                self.snap[sid] = {}
                ids.append(sid)
            self.dq[q] = [ids, 0]
        self.EPOCH = 60000
        self.epsem = {e: {0: self.sem[e]} for e in self.eng}
        self.lastw = {}
        self.readers = {}
        self.n_ins = 0
        self.n_wait = 0
        self.uid = 0
        self.ps_row = {}
        self.pending = {}
        self.drain_mode = 0
        self.last_tp = None

    def sb(self, name, shape, dt=F32):
        self.uid += 1
        name = "%s_%d" % (name, self.uid)
        return self.es.enter_context(self.nc.sbuf_tensor("s_" + name, list(shape), dt))

    def ps(self, name, shape, dt=F32):
        return self.es.enter_context(self.nc.psum_tensor("p_" + name, list(shape), dt))

    def dram(self, name, shape, dt=F32, kind="Internal"):
        return self.nc.dram_tensor(name, list(shape), dt, kind=kind)

    def _semv(self, f, c):
        if f not in self.eng:
            return self.sem[f], c
        ep = (c - 1) // self.EPOCH
        if ep not in self.epsem[f]:
            self.epsem[f][ep] = self.es_root.enter_context(self.nc.semaphore("s_%s_e%d" % (f, ep)))
        return self.epsem[f][ep], c - ep * self.EPOCH

    @staticmethod
    def _key(x):
        if isinstance(x, tuple):
            return (x[0].tensor.name, x[1])
        return (x.tensor.name, None)

    @staticmethod
    def _ap(x):
        return x[0] if isinstance(x, tuple) else x

    def _deps(self, eng, reads, writes):
        need = {}

        def add(f, c, same_ok):
            if f == eng and same_ok:
                return
            if c > need.get(f, 0):
                need[f] = c

        for k in reads:
            lw = self.lastw.get(k)
            if lw:
                add(lw[0], lw[1], False)
            if k[0].startswith("p_"):
                for f, c in self.readers.get(k, {}).items():
                    add(f, c, True)
        for k in writes:
            lw = self.lastw.get(k)
            if lw:
                add(lw[0], lw[1], True)
            for f, c in self.readers.get(k, {}).items():
                add(f, c, True)
        clk = self.clock[eng]
        e = self.eng[eng]
        for f, c in need.items():
            if clk.get(f, 0) >= c:
                continue
            if c > self.cnt[f]:
                raise RuntimeError("wait on unmaterialised count %s %d > %d" % (f, c, self.cnt[f]))
            sm, val = self._semv(f, c)
            e.wait_ge(sm, val)
            self.n_wait += 1
            for g, v in self.snap[f][c].items():
                if v > clk.get(g, 0):
                    clk[g] = v
            clk[f] = max(clk.get(f, 0), c)

    def _record(self, who, c, reads, writes):
        for k in reads:
            self.readers.setdefault(k, {})[who] = c
        for k in writes:
            self.lastw[k] = (who, c)
            self.readers[k] = {}

    def I(self, eng, fn, w=(), r=(), inc=True):
        rk = [self._key(x) for x in r]
        wk = [self._key(x) for x in w]
        self._deps(eng, rk, wk)
        ins = fn(self.eng[eng])
        self.n_ins += 1
        if not inc:
            self._record(eng, self.cnt[eng] + 1, rk, wk)
            self.pending[eng] = True
            return ins
        self.pending[eng] = False
        self.cnt[eng] += 1
        c = self.cnt[eng]
        ins.then_inc(self._semv(eng, c)[0], 1)
        s = dict(self.clock[eng])
        s[eng] = c
        self.snap[eng][c] = s
        self._record(eng, c, rk, wk)
        return ins

    def dma(self, out, in_, q="sp", **kw):
        rk = [self._key(in_)]
        wk = [self._key(out)]
        self._deps(q, rk, wk)
        ids, pos = self.dq[q]
        sid = ids[pos % len(ids)]
        self.dq[q][1] = pos + 1
        clk = self.clock[q]
        prev = self.cnt[sid]
        if prev and clk.get(sid, 0) < prev:
            self.eng[q].wait_ge(self.sem[sid], prev)
            for g, v in self.snap[sid][prev].items():
                if v > clk.get(g, 0):
                    clk[g] = v
            clk[sid] = prev
        ins = self.eng[q].dma_start(out=self._ap(out), in_=self._ap(in_), **kw)
        c = prev + 16
        self.cnt[sid] = c
        ins.then_inc(self.sem[sid], 16)
        s = dict(clk)
        s[sid] = c
        self.snap[sid][c] = s
        self._record(sid, c, rk, wk)
        self.n_ins += 1
        return ins

    def finish(self):
        assert not any(self.pending.values()), self.pending
        e = "sp"
        clk = self.clock[e]
        for f in self.sem:
            c = self.cnt[f]
            if c and clk.get(f, 0) < c and f != e:
                sm, val = self._semv(f, c)
                self.eng[e].wait_ge(sm, val)
        self.es.close()

    def mm(self, out, lhsT, rhs, start=True, stop=True, inc=True, **kw):
        a = self._ap
        lt = a(lhsT)
        row = lt.base_partition() if lt.partition_size() < 128 else -1
        okey = self._key(out)
        prev = self.ps_row.get(okey)
        if prev is not None and prev[0] != row and self.clock["pe"].get("pe", 0) < prev[1]:
            if prev[1] > self.cnt["pe"]:
                raise RuntimeError("PE row switch on bank %s needs a materialised count" % (okey,))
            sm, val = self._semv("pe", prev[1])
            self.eng["pe"].wait_ge(sm, val)
            self.clock["pe"]["pe"] = prev[1]
            self.n_wait += 1
        self.ps_row[okey] = (row, self.cnt["pe"] + 1)
        tp = kw.get("tile_position")
        if self.drain_mode and tp is not None and self.cnt["pe"] and self.clock["pe"].get("pe", 0) < self.cnt["pe"]:
            if self.drain_mode == 1 or (self.drain_mode == 2 and tp != self.last_tp):
                sm, val = self._semv("pe", self.cnt["pe"])
                self.eng["pe"].wait_ge(sm, val)
                self.clock["pe"]["pe"] = self.cnt["pe"]
        self.last_tp = tp
        return self.I("pe", lambda e: e.matmul(a(out), a(lhsT), a(rhs), start=start, stop=stop, **kw),
                      w=[out], r=[lhsT, rhs], inc=inc)

    def tr(self, out, in_, ident):
        a = self._ap
        return self.I("pe", lambda e: e.transpose(a(out), a(in_), a(ident)), w=[out], r=[in_, ident])

    def act(self, out, in_, func, bias=None, scale=1.0, accum=None, eng="act"):
        a = self._ap
        r = [in_]
        kw = {}
        if bias is not None:
            if isinstance(bias, (int, float)):
                kw["bias"] = float(bias)
            else:
                kw["bias"] = a(bias)
                r.append(bias)
        if isinstance(scale, (int, float)):
            kw["scale"] = float(scale)
        else:
            kw["scale"] = a(scale)
            r.append(scale)
        w = [out]
        if accum is not None:
            kw["accum_out"] = a(accum)
            w.append(accum)
        return self.I(eng, lambda e: e.activation(a(out), a(in_), func, **kw), w=w, r=r)

    def tt(self, out, x, y, op, eng="dve"):
        a = self._ap
        return self.I(eng, lambda e: e.tensor_tensor(a(out), a(x), a(y), op), w=[out], r=[x, y])

    def ts(self, out, x, s1, op0, s2=None, op1=None, eng="dve", accum=None):
        a = self._ap
        r = [x]
        v1 = s1
        if not isinstance(s1, (int, float)):
            r.append(s1)
            v1 = a(s1)
        v2 = s2
        if s2 is not None and not isinstance(s2, (int, float)):
            r.append(s2)
            v2 = a(s2)
        kw = {}
        w = [out]
        if op1 is not None:
            kw["op1"] = op1
        if accum is not None:
            kw["accum_out"] = a(accum)
            w.append(accum)
        return self.I(eng, lambda e: e.tensor_scalar(a(out), a(x), v1, v2, op0, **kw), w=w, r=r)

    def stt(self, out, x, s, y, op0, op1, eng="dve"):
        a = self._ap
        r = [x, y]
        v = s
        if not isinstance(s, (int, float)):
            r.append(s)
            v = a(s)
        return self.I(eng, lambda e: e.scalar_tensor_tensor(a(out), a(x), v, a(y), op0, op1), w=[out], r=r)

    def cp(self, out, in_, eng="dve"):
        a = self._ap
        if eng == "act":
            return self.I(eng, lambda e: e.copy(a(out), a(in_)), w=[out], r=[in_])
        return self.I(eng, lambda e: e.tensor_copy(a(out), a(in_)), w=[out], r=[in_])

    def memset(self, out, val, eng="pool"):
        a = self._ap
        return self.I(eng, lambda e: e.memset(a(out), val), w=[out])

    def recip(self, out, in_):
        a = self._ap
        return self.I("dve", lambda e: e.reciprocal(a(out), a(in_)), w=[out], r=[in_])

    def rsqrt(self, out, in_, eps):
        self.act(out, in_, AF.Ln, bias=eps)
        self.act(out, out, AF.Exp, scale=-0.5)

    def barrier(self):
        assert not any(self.pending.values()), self.pending
        for e in self.eng:
            clk = self.clock[e]
            for f in self.sem:
                c = self.cnt[f]
                if f == e or not c or clk.get(f, 0) >= c:
                    continue
                sm, val = self._semv(f, c)
                self.eng[e].wait_ge(sm, val)
                clk[f] = c
        full = {f: self.cnt[f] for f in self.sem if self.cnt[f]}
        for e in self.eng:
            for f, c in full.items():
                if f != e:
                    self.clock[e][f] = max(self.clock[e].get(f, 0), c)

    def scope(self):
        kb = self

        class _S:
            def __enter__(s):
                s.old = kb.es
                kb.es = ExitStack()
                return s

            def __exit__(s, *a):
                kb.barrier()
                kb.es.close()
                kb.es = s.old
                return False

        return _S()

T = 3072
NG = 6
D = 1024
NPP = 160
O_BADA, O_LN1G, O_LN1B, O_LN2G, O_LN2B, O_QN, O_KN = 0, 48, 56, 64, 72, 80, 81
O_MU, O_W0, O_A0, O_KK, O_KA, O_RK, O_LXG, O_LXB, O_OMKA, O_OMU = 82, 108, 116, 124, 128, 132, 136, 140, 144, 148
NPP = 176
ALPHA = 4.0 ** 0.25
C_UA, C_VA, C_Q, C_K, C_V, C_R, C_GL = 0, 512, 1024, 1536, 1664, 1792, 3712


class StopBuild(Exception):
    pass


def build(stop_after=None, dbg=()):
    kb = KB()
    kb.stop_after = stop_after
    kb.drain_mode = 1 if "drain1" in dbg else (2 if "drain2" in dbg else 0)
    try:
        _build(kb, stop_after, dbg)
    except StopBuild:
        pass
    kb.finish()
    return kb


def chk(kb, name, l=0):
    if kb.stop_after == (name, l):
        raise StopBuild()


def _build(kb, stop_after, dbg):
    nc = kb.nc
    IN = lambda n, s, dt=F32: kb.dram(n, s, dt, kind="ExternalInput")
    OUT = lambda n, s, dt=F32: kb.dram(n, s, dt, kind="ExternalOutput")
    x_in = IN("x", [T, D])
    ck_in = IN("ck", [2, 512, 128])
    cvv_in = IN("cvv", [2, 512, 128])
    st_in = IN("st0", [2, 2, 128, 256])
    cvec_in = IN("cvec", [128, 8, 2])
    pp_in = IN("pp", [2, 128, NPP])
    w_ada = IN("w_ada", [2, D, 6144])
    w_in = IN("w_in", [2, D, 6784])
    w_br = IN("w_branch", [2, 3, 512, D])
    w_out = IN("w_out", [2, D, D])
    w_up = IN("w_up", [2, D, 4096])
    w_dn = IN("w_down", [2, 4096, D])
    wsT_in = IN("wsT", [2, 128, 4, 128])
    sgb_in = IN("sgb", [2, 1, 512])
    lnA_in = IN("lnA", [2, 2, 128, 512])
    wa2_in = IN("wa2", [2, 2, 128, 512])
    g2_in = IN("g2", [2, 128, 512])
    cst_in = IN("cst", [128, 9, 512])
    rope_in = IN("rope", [64, 2, 2048])
    y_out = OUT("y", [T, D])
    nk_out = OUT("nk", [2, 1024, 128])
    nv_out = OUT("nv", [2, 1024, 128])
    ns_out = OUT("ns", [4, 2, 2, 128, 256])
    XT = [kb.dram("XT%d" % i, [128, 8, T]) for i in range(2)]
    RKV = kb.dram("RKV", [128, 15, T])
    BRA = kb.dram("BRA", [128, 4, T], BF16)
    BRB = kb.dram("BRB", [64, 8, T], BF16)
    BRC = kb.dram("BRC", [128, 4, T], BF16)
    BONS = kb.dram("BONS", [128, 4, T])
    XM = kb.dram("XM", [128, 8, T])
    YS = kb.dram("YS", [128, 4, T])
    dbg_out = {}
    if "dump" in dbg:
        dbg_out["d1"] = OUT("dbg1", [128, 32, 256])
        dbg_out["d2"] = OUT("dbg2", [128, 14, 256])
        dbg_out["d3"] = OUT("dbg3", [128, 6, 1024])

    P = [kb.ps("P%d" % i, [128, 512]) for i in range(8)]
    cst = kb.sb("cst", [128, 9, 512])
    kb.dma(cst[:], cst_in.ap()[:, :, :])
    ident = cst[:, 0, 0:128]
    BDm = cst[:, 0, 128:256]
    BD64 = cst[:, 0, 256:384]
    ones64 = cst[0:64, 0, 384:448]
    maskF4, maskB4, maskFT8, maskBT8, ident8, rst = (cst[:, i, :] for i in range(1, 7))
    rm_f = cst[0:64, 7, 0:64]
    ones_row = cst[:, 7, 64:192]
    onesD = cst[:, 8, 0:128]
    rm_b = kb.sb("rm_b", [64, 64], BF16)
    kb.cp(rm_b[:], rm_f, eng="dve")
    ident_b = kb.sb("ident_b", [128, 128], BF16)
    kb.cp(ident_b[:], ident, eng="dve")
    pp = [kb.sb("pp%d" % l, [128, NPP]) for l in range(2)]
    for l in range(2):
        kb.dma(pp[l][:], pp_in.ap()[l])
    modT = [kb.sb("modT%d" % l, [128, 48, 2]) for l in range(2)]

    with kb.scope():
        cv = kb.sb("cv", [128, 8, 2])
        sl = kb.sb("sl", [128, 8, 2])
        kb.dma(cv[:], cvec_in.ap()[:, :, :])
        kb.act(sl[:], cv[:], AF.Silu)
        wts = [kb.sb("wadat%d" % i, [128, 8, 512]) for i in range(2)]
        for l in range(2):
            wv = w_ada.ap()[l].rearrange("(kc p) n -> p kc n", p=128)
            for nb in range(12):
                wt = wts[nb % 2]
                kb.dma(wt[:], wv[:, :, nb * 512:(nb + 1) * 512])
                ps = P[nb % 2]
                for j in range(4):
                    for kc in range(8):
                        kb.mm(ps[:, 2 * j:2 * j + 2], wt[:, kc, j * 128:(j + 1) * 128], sl[:, kc, :],
                              start=(kc == 0), stop=(kc == 7), inc=(kc == 7))
                for j in range(4):
                    c = nb * 4 + j
                    kb.ts(modT[l][:, c, :], ps[:, 2 * j:2 * j + 2], pp[l][:, O_BADA + c:O_BADA + c + 1], ALU.add)
            for c0 in (8, 32):
                kb.ts(modT[l][:, c0:c0 + 8, :], modT[l][:, c0:c0 + 8, :], 1.0, ALU.add, eng="pool")
            kb.ts(pp[l][:, O_OMU:O_OMU + 26], pp[l][:, O_MU:O_MU + 26], -1.0, ALU.mult, 1.0, ALU.add, eng="pool")
            kb.ts(pp[l][:, O_OMKA:O_OMKA + 4], pp[l][:, O_KA:O_KA + 4], -1.0, ALU.mult, 1.0, ALU.add, eng="pool")

    chk(kb, "mod")
    with kb.scope():
        xts = [kb.sb("xt_t%d" % i, [128, 8, 512]) for i in range(2)]
        xins = [kb.sb("xin%d" % i, [128, 1024]) for i in range(2)]
        k = 0
        for g in range(NG):
            xt = xts[g % 2]
            for tt in range(4):
                xin = xins[tt % 2]
                kb.dma(xin[:], x_in.ap()[(g * 4 + tt) * 128:(g * 4 + tt + 1) * 128, :])
                for half in range(2):
                    ps = P[k % 4]
                    k += 1
                    for c in range(4):
                        kb.tr(ps[:, c * 128:(c + 1) * 128], xin[:, (half * 4 + c) * 128:(half * 4 + c + 1) * 128], ident)
                    kb.cp(xt[:, half * 4:(half + 1) * 4, tt * 128:(tt + 1) * 128],
                          ps[:].rearrange("p (c t) -> p c t", c=4), eng=("dve" if half else "act"))
            kb.dma(XT[0].ap()[:, :, g * 512:(g + 1) * 512], xt[:])

    chk(kb, "t0")
    for l in range(2):
        layer(kb, l, locals())
        chk(kb, "layer", l)

    if stop_after is None:
        with kb.scope():
            xts = [kb.sb("oxt%d" % i, [128, 8, 512]) for i in range(2)]
            yos = [kb.sb("yo%d" % i, [128, 1024]) for i in range(2)]
            k = 0
            for g in range(NG):
                xt = xts[g % 2]
                kb.dma(xt[:], XT[0].ap()[:, :, g * 512:(g + 1) * 512])
                for tt in range(4):
                    yo = yos[tt % 2]
                    for half in range(2):
                        ps = P[k % 4]
                        k += 1
                        for c in range(4):
                            kb.tr(ps[:, c * 128:(c + 1) * 128], xt[:, half * 4 + c, tt * 128:(tt + 1) * 128], ident)
                        kb.cp(yo[:, half * 512:(half + 1) * 512], ps[:], eng=("dve" if half else "act"))
                    kb.dma(y_out.ap()[(g * 4 + tt) * 128:(g * 4 + tt + 1) * 128, :], yo[:])


def mod_cols(modT_l, base, g):
    v = 0 if g < 2 else 1
    return [modT_l[:, base + c, v:v + 1] for c in range(8)]


def layer(kb, l, E):
    P = E["P"]; pp = E["pp"][l]; modT = E["modT"][l]; cst = E["cst"]
    XTi = E["XT"][l % 2]; XTo = E["XT"][(l + 1) % 2]
    w_in = E["w_in"]; RKV = E["RKV"]; BRA = E["BRA"]; BRB = E["BRB"]; BRC = E["BRC"]
    ident = E["ident"]; ident_b = E["ident_b"]
    winv = w_in.ap()[l].rearrange("(kc p) n -> p kc n", p=128)
    pk = [0]

    def PS(lo=0, hi=8):
        i = lo + pk[0] % (hi - lo)
        pk[0] += 1
        return P[i]

    ek = [0]

    def EV():
        ek[0] += 1
        return "dve" if ek[0] % 2 else "act"

    with kb.scope():
        HT = kb.sb("HT", [128, 8, T], BF16)
        QT = kb.sb("QT", [64, 8, T], BF16)
        KT = kb.sb("KT", [64, 2, T + 512], BF16)
        VS = kb.sb("VS", [128, 28, 2, 65], BF16)
        kb.memset(VS[:], 1.0)
        with kb.scope():
            xls = [kb.sb("xl%d" % i, [128, 8, 512]) for i in range(2)]
            for g in range(NG):
                xl = xls[g % 2]
                kb.dma(xl[:], XTi.ap()[:, :, g * 512:(g + 1) * 512])
                sc = mod_cols(modT, 8, g); sh = mod_cols(modT, 0, g)
                for c in range(8):
                    kb.ts(HT[:, c, g * 512:(g + 1) * 512], xl[:, c, :], sc[c], ALU.mult, sh[c], ALU.add,
                          eng=("dve" if c % 2 else "pool"))
        chk(kb, "ht", l)
        with kb.scope():
            UA = kb.sb("UA", [128, 4, T], BF16)
            W = kb.sb("Wua", [128, 8, 512], BF16)
            kb.dma(W[:], winv[:, :, C_UA:C_UA + 512], q="pool")
            for g in range(NG):
                for j in range(4):
                    ps = PS()
                    for kc in range(8):
                        kb.mm(ps[:], W[:, kc, j * 128:(j + 1) * 128], HT[:, kc, g * 512:(g + 1) * 512],
                              start=(kc == 0), stop=(kc == 7), inc=(kc == 7))
                    kb.cp(UA[:, j, g * 512:(g + 1) * 512], ps[:], eng=EV())
            chk(kb, "ua", l)
            W2 = kb.sb("Wva", [128, 8, 512], BF16)
            kb.dma(W2[:], winv[:, :, C_VA:C_VA + 512], q="pool")
            wsT = kb.sb("wsT", [128, 4, 128], BF16)
            kb.dma(wsT[:], E["wsT_in"].ap()[l], q="pool")
            sgb = kb.sb("sgb", [1, 512])
            kb.dma(sgb[:], E["sgb_in"].ap()[l])
            lnG = kb.sb("lnG", [128, 512]); lnB = kb.sb("lnB", [128, 512])
            kb.dma(lnG[:], E["lnA_in"].ap()[l, 0]); kb.dma(lnB[:], E["lnA_in"].ap()[l, 1])
            ones_row = E["ones_row"]
            sqs = [kb.sb("a_sq%d" % i, [128, 512]) for i in range(2)]
            v0s = [kb.sb("a_v0%d" % i, [128, 512]) for i in range(2)]
            vns = [kb.sb("a_vn%d" % i, [128, 512], BF16) for i in range(2)]
            sts = [kb.sb("a_st%d" % i, [128, 8]) for i in range(2)]
            for tt in range(24):
                ps = PS(0, 4)
                for kc in range(8):
                    kb.mm(ps[:], HT[:, kc, tt * 128:(tt + 1) * 128], W2[:, kc, :], start=(kc == 0), stop=(kc == 7), inc=(kc == 7))
                sq = sqs[tt % 2]; v0 = v0s[tt % 2]; vn = vns[tt % 2]; st = sts[tt % 2]
                cut = [int(x[3:]) for x in E["dbg"] if x.startswith("cut")]
                cut = cut[0] if cut else 99
                kb.act(v0[:], ps[:], AF.Copy)
                kb.act(sq[:], v0[:], AF.Square)
                if cut < 2: continue
                kb.I("dve", lambda e: e.reduce_sum(st[:, 0:1], v0[:], AX.X), w=[st[:]], r=[v0[:]])
                kb.I("dve", lambda e: e.reduce_sum(st[:, 1:2], sq[:], AX.X), w=[st[:]], r=[sq[:], st[:]])
                if cut < 3: continue
                kb.ts(st[:, 2:3], st[:, 0:1], 1.0 / 512, ALU.mult)
                kb.tt(st[:, 3:4], st[:, 2:3], st[:, 2:3], ALU.mult)
                kb.stt(st[:, 4:5], st[:, 1:2], 1.0 / 512, st[:, 3:4], ALU.mult, ALU.subtract)
                if cut < 4: continue
                kb.rsqrt(st[:, 5:6], st[:, 4:5], 1e-5)
                kb.stt(st[:, 6:7], st[:, 2:3], -1.0, st[:, 5:6], ALU.mult, ALU.mult)
                if cut < 5: continue
                kb.act(v0[:], v0[:], AF.Identity, bias=st[:, 6:7], scale=st[:, 5:6])
                if cut < 6: continue
                kb.tt(v0[:], v0[:], lnG[:], ALU.mult, eng="pool")
                kb.tt(vn[:], v0[:], lnB[:], ALU.add, eng="pool")
                if cut < 7: continue
                ps2 = PS(4, 8)
                for j in range(4):
                    nob = "nobias" in E["dbg"]
                    kb.mm(ps2[:, j * 128:(j + 1) * 128], vn[:, j * 128:(j + 1) * 128], wsT[:, j, :], start=True, stop=nob)
                    if not nob:
                        kb.mm(ps2[:, j * 128:(j + 1) * 128], ones_row[0:1, :], sgb[0:1, j * 128:(j + 1) * 128], start=False, stop=True)
                if cut < 8: continue
                kb.tt(UA[:, :, tt * 128:(tt + 1) * 128], UA[:, :, tt * 128:(tt + 1) * 128],
                      ps2[:].rearrange("p (j t) -> p j t", j=4), ALU.mult)
            if "nobra" not in E["dbg"]:
                kb.dma(BRA.ap()[:, :, :], UA[:])
        chk(kb, "va", l)
        with kb.scope():
            Wq = kb.sb("Wq", [128, 8, 512], BF16)
            kb.dma(Wq[:], winv[:, :, C_Q:C_Q + 512], q="pool")
            Wkv = kb.sb("Wkv", [128, 8, 256], BF16)
            kb.dma(Wkv[:], winv[:, :, C_K:C_K + 256], q="pool")
            rope = kb.sb("rope", [64, 2, 2048])
            kb.dma(rope[:], E["rope_in"].ap()[:, :, :])
            sqs = [kb.sb("q_sq%d" % i, [64, 512]) for i in range(2)]
            rss = [kb.sb("q_rs%d" % i, [64, 512]) for i in range(2)]
            qns = [kb.sb("q_qn%d" % i, [64, 512]) for i in range(2)]
            qbs = [kb.sb("q_qb%d" % i, [64, 512], BF16) for i in range(2)]
            t1s = [kb.sb("q_t1%d" % i, [64, 512]) for i in range(2)]
            t2s = [kb.sb("q_t2%d" % i, [64, 512]) for i in range(2)]
            qrs = [kb.sb("q_qr%d" % i, [64, 512]) for i in range(2)]
            kos = [kb.sb("q_ko%d" % i, [128, 4, 128]) for i in range(2)]
            vos = [kb.sb("q_vo%d" % i, [128, 128]) for i in range(2)]
            ones64 = E["ones64"]; rm_b = E["rm_b"]
            iters = [(g, hh) for g in range(NG) for hh in range(10)]
            stt_ = {}

            def stA(n):
                g, hh = iters[n]
                isk = hh >= 8
                h = hh - 8 if isk else hh
                Wt = Wkv if isk else Wq
                tok = slice(g * 512, (g + 1) * 512)
                i2 = n % 2
                ps = PS(0, 3)
                for kc in range(8):
                    kb.mm(ps[0:64, :], Wt[:, kc, h * 64:(h + 1) * 64], HT[:, kc, tok], start=(kc == 0), stop=(kc == 7), inc=(kc == 7))
                kb.act(qrs[i2][:], ps[0:64, :], AF.Copy)
                kb.act(sqs[i2][:], ps[0:64, :], AF.Square)
                stt_[n] = dict(g=g, h=h, isk=isk, tok=tok, i2=i2, latent=(g >= 2), done=False)

            def stB1(n):
                d_ = stt_[n]
                g, h, isk, tok, i2, latent = d_["g"], d_["h"], d_["isk"], d_["tok"], d_["i2"], d_["latent"]
                gcol = pp[0:64, (O_KN if isk else O_QN):(O_KN if isk else O_QN) + 1]
                sq, rs, qn, qb, qr = sqs[i2], rss[i2], qns[i2], qbs[i2], qrs[i2]
                ps2 = PS(3, 6)
                kb.mm(ps2[0:64, :], ones64, sq[:])
                kb.rsqrt(rs[:], ps2[0:64, :], 1e-6)
                dst = (KT[:, h, tok] if isk else QT[:, h, tok])
                if (not latent) and (not isk):
                    kb.stt(dst, qr[:], gcol, rs[:], ALU.mult, ALU.mult)
                    d_["done"] = True
                    return
                kb.stt(qn[:], qr[:], gcol, rs[:], ALU.mult, ALU.mult)
                if not latent:
                    kb.cp(dst, qn[:], eng="pool")
                else:
                    kb.cp(qb[:], qn[:], eng="pool")

            def stB2(n):
                d_ = stt_[n]
                if d_["done"]:
                    return
                g, h, isk, tok, i2, latent = d_["g"], d_["h"], d_["isk"], d_["tok"], d_["i2"], d_["latent"]
                qn, qb, t1, t2 = qns[i2], qbs[i2], t1s[i2], t2s[i2]
                dst = (KT[:, h, tok] if isk else QT[:, h, tok])
                if not latent:
                    ko = kos[g % 2]
                    ps3 = PS(6, 8)
                    for tt in range(4):
                        kb.tr(ps3[:, tt * 64:(tt + 1) * 64], qn[:, tt * 128:(tt + 1) * 128], ident[0:64, 0:64])
                    kb.cp(ko[:, :, h * 64:(h + 1) * 64], ps3[:, 0:256].rearrange("p (t d) -> p t d", t=4), eng="act")
                    if h == 1:
                        kb.dma(E["nk_out"].ap()[l, g * 512:(g + 1) * 512, :].rearrange("(t p) c -> p t c", p=128), ko[:])
                    return
                pos = slice((g - 2) * 512, (g - 1) * 512)
                ps3 = PS(6, 8)
                kb.mm(ps3[0:64, :], rm_b[:], qb[:])
                kb.tt(t1[:], qn[:], rope[:, 0, pos], ALU.mult, eng="pool")
                kb.tt(t2[:], ps3[0:64, :], rope[:, 1, pos], ALU.mult)
                kb.tt(dst, t1[:], t2[:], ALU.add, eng="pool")

            def vproj(g):
                latent = g >= 2
                for t4 in range(4):
                    tt = g * 4 + t4
                    ps = PS(0, 3)
                    for kc in range(8):
                        kb.mm(ps[:, 0:128], HT[:, kc, tt * 128:(tt + 1) * 128], Wkv[:, kc, 128:256], start=(kc == 0), stop=(kc == 7), inc=(kc == 7))
                    vo = vos[t4 % 2]
                    kb.cp(vo[:], ps[:, 0:128], eng="act")
                    kb.cp(VS[:, tt, :, 0:64], vo[:].rearrange("p (k d) -> p k d", k=2), eng="pool")
                    if not latent:
                        kb.dma(E["nv_out"].ap()[l, tt * 128:(tt + 1) * 128, :], vo[:])

            NI = len(iters)
            for step_ in range(NI + 2):
                if step_ < NI:
                    stA(step_)
                if 0 <= step_ - 1 < NI:
                    stB1(step_ - 1)
                if 0 <= step_ - 2 < NI:
                    stB2(step_ - 2)
                    g_, hh_ = iters[step_ - 2]
                    if hh_ == 9:
                        vproj(g_)
            ckt = kb.sb("ckt", [128, 4, 128]); cvt = kb.sb("cvt", [128, 4, 128])
            kb.dma(ckt[:], E["ck_in"].ap()[l].rearrange("(t p) c -> p t c", p=128))
            kb.dma(cvt[:], E["cvv_in"].ap()[l].rearrange("(t p) c -> p t c", p=128))
            for t4 in range(4):
                for kv in range(2):
                    ps = PS(0, 3)
                    kb.tr(ps[0:64, 0:128], ckt[:, t4, kv * 64:(kv + 1) * 64], ident)
                    kb.cp(KT[:, kv, T + t4 * 128:T + (t4 + 1) * 128], ps[0:64, 0:128], eng=EV())
                kb.cp(VS[:, 24 + t4, :, 0:64], cvt[:, t4, :].rearrange("p (k d) -> p k d", k=2), eng="pool")
        chk(kb, "q", l)
        with kb.scope():
            Ws = [kb.sb("Wr%d" % i, [128, 8, 512], BF16) for i in range(2)]
            stg = [kb.sb("rstg%d" % i, [128, 512]) for i in range(3)]
            k = 0
            for wb in range(4):
                ncol = 512 if wb < 3 else 384
                W = Ws[wb % 2]
                kb.dma(W[:, :, 0:ncol], winv[:, :, C_R + wb * 512:C_R + wb * 512 + ncol], q="pool")
                for bb in range(ncol // 128):
                    b = wb * 4 + bb
                    for g in range(NG):
                        ps = PS()
                        for kc in range(8):
                            kb.mm(ps[:], W[:, kc, bb * 128:(bb + 1) * 128], HT[:, kc, g * 512:(g + 1) * 512],
                                  start=(kc == 0), stop=(kc == 7), inc=(kc == 7))
                        s = stg[k % 3]
                        k += 1
                        kb.cp(s[:], ps[:], eng=EV())
                        kb.dma(RKV.ap()[:, b, g * 512:(g + 1) * 512], s[:])
        chk(kb, "rkv", l)
        with kb.scope():
            pts = [kb.sb("pt%d" % i, [128, 512], BF16) for i in range(3)]
            ons = [kb.sb("on%d" % i, [128, 512]) for i in range(2)]
            rcs = [kb.sb("rc%d" % i, [128, 512]) for i in range(2)]
            obs = [kb.sb("ob%d" % i, [64, 512], BF16) for i in range(2)]
            seqs = [(s * 256, 256, [(s * 256 + i * 128, s * 2 + i) for i in range(2)]) for s in range(4)]
            seqs.append((1024, 2048, [(T + i * 128, 24 + i) for i in range(4)] + [(1024 + i * 128, 8 + i) for i in range(16)]))
            n = 0
            apend = [None]
            for (t0, L, keys) in seqs:
                for kv in range(2):
                    for qt in range(L // 128):
                        q0 = t0 + qt * 128
                        po = P[6 + n % 2]
                        nk_ = len(keys)
                        pss = [None] * nk_

                        def score(i):
                            ps = PS(0, 4)
                            kb.mm(ps[:], KT[:, kv, keys[i][0]:keys[i][0] + 128], QT[:, 4 * kv:4 * kv + 4, q0:q0 + 128])
                            pss[i] = ps

                        score(0)
                        for i, (kc0, vt) in enumerate(keys):
                            if i + 1 < nk_:
                                score(i + 1)
                            if i == min(2, nk_ - 1) and apend[0] is not None:
                                apend[0]()
                                apend[0] = None
                            pt = pts[i % 3]
                            kb.act(pt[:], pss[i][:], AF.Exp, scale=0.125)
                            kb.mm(po[0:65, :], VS[:, vt, kv, :], pt[:], start=(i == 0), stop=(i == nk_ - 1))
                        on = ons[n % 2]; rc = rcs[n % 2]; ob = obs[n % 2]
                        kb.cp(on[0:65, :], po[0:65, :], eng="act")
                        kb.recip(rc[64:65, :], on[64:65, :])
                        pb = P[4 + n % 2]

                        def fin(on=on, rc=rc, ob=ob, pb=pb, kv=kv, q0=q0):
                            kb.mm(pb[0:64, :], E["ones_row"][64:65, 0:64], rc[64:65, :], tile_position=(64, 0))
                            kb.tt(ob[:], on[0:64, :], pb[0:64, :], ALU.mult)
                            kb.dma(BRB.ap()[:, 4 * kv:4 * kv + 4, q0:q0 + 128], ob[:].rearrange("p (g t) -> p g t", g=4))

                        apend[0] = fin
                        n += 1
            if apend[0] is not None:
                apend[0]()
                apend[0] = None
    chk(kb, "att", l)
    if "noC" not in E["dbg"]:
        rwkv(kb, l, E)
    chk(kb, "rw", l)
    merge_ffn(kb, l, E)


def mmg(kb, terms):
    order = sorted(range(len(terms)), key=lambda i: (terms[i][4][0], i))
    first, last = {}, {}
    for n, i in enumerate(order):
        r = terms[i][0]
        first.setdefault(r, n)
        last[r] = n
    for n, i in enumerate(order):
        r, out, lhsT, rhs, tp = terms[i]
        kb.mm(out, lhsT, rhs, start=(first[r] == n), stop=(last[r] == n), tile_position=tp)


def rwkv(kb, l, E):
    P = E["P"]; pp = E["pp"][l]; RKV = E["RKV"]; BRC = E["BRC"]; BONS = E["BONS"]; YS = E["YS"]
    ident = E["ident"]; BDm = E["BDm"]; BD64 = E["BD64"]
    NP_ = 256
    fk = [0]

    def FPS():
        fk[0] += 1
        return P[6 + fk[0] % 2]

    def bc4(off, n=NP_):
        return pp[:, off:off + 4].unsqueeze(2).to_broadcast([128, 4, n])

    with kb.scope():
        A = lambda n, s, dt=F32: kb.sb(n, s, dt)
        XR = A("XR", [128, 14, NP_ + 2])
        R = A("rR", [128, 4, NP_]); K = A("rK", [128, 4, NP_]); V = A("rV", [128, 4, NP_]); LO = A("rLO", [128, NP_])
        TW = A("rTW", [128, NP_]); LW = A("rLW", [128, 4, NP_]); AA = A("rAA", [128, 4, NP_])
        KK = A("rKK", [128, 4, NP_])
        CL = A("rCL", [128, 4, NP_]); KM = A("rKM", [128, 4, NP_]); BB = A("rBB", [128, 4, NP_])
        Bh = A("rBh", [128, 4, NP_]); Kh = A("rKh", [128, 4, NP_])
        ARs = [A("rAR%d" % i, [128, 4, 4, 128]) for i in range(2)]
        Bts = [A("rBt%d" % i, [128, 4, NP_]) for i in range(2)]
        Kts = [A("rKt%d" % i, [128, 4, NP_]) for i in range(2)]
        BhTs = [A("rBhT%d" % i, [128, 2, 512]) for i in range(2)]
        KhTs = [A("rKhT%d" % i, [128, 2, 512]) for i in range(2)]
        VTs = [A("rVT%d" % i, [128, 2, 512]) for i in range(2)]
        BONs = [A("rBON%d" % i, [128, 4, NP_]) for i in range(2)]
        GAMs = [A("rGAM%d" % i, [128, 4, 4]) for i in range(2)]
        YPs = [A("rYP%d" % i, [128, 4, NP_]) for i in range(2)]
        F1 = A("fT1", [128, 4, NP_]); F2 = A("fT2", [128, 4, NP_]); FE = A("fEE", [128, 4, NP_])
        WA = [A("rWA%d" % d, [128, 512]) for d in range(2)]
        for d in range(2):
            kb.dma(WA[d][:], E["wa2_in"].ap()[l, d])
        G2 = A("rG2", [128, 512], BF16)
        kb.dma(G2[:], E["g2_in"].ap()[l], q="pool")
        NM = A("cNM", [128, 8, 128]); KMt = A("cKMt", [128, 8, 128])
        Nb = [A("cN%d" % i, [128, 8, 64]) for i in range(2)]
        NTb = [A("cNT%d" % i, [128, 8, 64]) for i in range(2)]
        TTb = [A("cTT%d" % i, [128, 8, 64]) for i in range(2)]
        RHS = A("cRHS", [128, 512]); U = A("cU", [128, 512]); RHS0 = A("cRHS0", [128, 512])
        ST = [A("cST%d" % i, [128, 4, 64]) for i in range(2)]
        SGD = A("rSGD", [128, NP_], BF16); GD = A("rGD", [128, NP_]); OC = A("rOC", [128, 4, NP_], BF16)
        v64 = lambda ap: ap.rearrange("p a (c t) -> p a c t", t=64)

        items = []
        for (t0, L, latent, sidx) in [(s * 256, 256, False, s) for s in range(4)] + [(1024, 2048, True, None)]:
            npc = L // NP_
            for d in range(2):
                order = list(range(npc)) if d == 0 else list(range(npc - 1, -1, -1))
                for j, pc in enumerate(order):
                    items.append(dict(t0=t0, latent=latent, sidx=sidx, d=d, pc=pc, npc=npc,
                                      first=(j == 0), last=(j == npc - 1)))
        sic = [0]

        def front(it, b):
            d, pc, npc = it["d"], it["pc"], it["npc"]
            p0 = it["t0"] + pc * NP_
            AR, Bt, Kt, BhT, KhT, VT, BON, GAM, YP = ARs[b], Bts[b], Kts[b], BhTs[b], KhTs[b], VTs[b], BONs[b], GAMs[b], YPs[b]
            lo_h = 1 if pc == 0 else 0
            hi_h = NP_ + 1 if pc == npc - 1 else NP_ + 2
            if pc == 0:
                kb.memset(XR[:, :, 0:1], 0.0)
            if pc == npc - 1:
                kb.memset(XR[:, :, NP_ + 1:NP_ + 2], 0.0)
            kb.dma(XR[:, :, lo_h:hi_h], RKV.ap()[:, 0:14, p0 - 1 + lo_h:p0 - 1 + hi_h])
            yield
            sh = 0 if d == 0 else 2
            dsts = [R[:, i, :] for i in range(4)] + [K[:, i, :] for i in range(4)] + [V[:, i, :] for i in range(4)] + [LO[:]]
            for blk in range(13):
                sbk = blk if blk < 12 else 12 + d
                mu = pp[:, O_MU + d * 13 + blk:O_MU + d * 13 + blk + 1]
                omu = pp[:, O_OMU + d * 13 + blk:O_OMU + d * 13 + blk + 1]
                tmp = F1[:, blk % 4, :]
                kb.ts(tmp, XR[:, sbk, sh:sh + NP_], mu, ALU.mult, eng="pool")
                kb.stt(dsts[blk], XR[:, sbk, 1:NP_ + 1], omu, tmp, ALU.mult, ALU.add)
                if blk % 3 == 2:
                    yield
            kb.act(TW[0:64, :], LO[0:64, :], AF.Tanh)
            for pr in range(4):
                ps = FPS()
                kb.mm(ps[:, 0:NP_], WA[d][0:64, pr * 128:(pr + 1) * 128], TW[0:64, :])
                kb.act(LW[:, pr, :], ps[:, 0:NP_], AF.Sigmoid, bias=pp[:, O_W0 + d * 4 + pr:O_W0 + d * 4 + pr + 1])
                ps = FPS()
                kb.mm(ps[:, 0:NP_], WA[d][64:128, pr * 128:(pr + 1) * 128], LO[64:128, :], tile_position=(64, 0))
                kb.act(AA[:, pr, :], ps[:, 0:NP_], AF.Sigmoid, bias=pp[:, O_A0 + d * 4 + pr:O_A0 + d * 4 + pr + 1])
                yield
            kb.ts(LW[:], LW[:], -0.6065306597126334, ALU.mult, eng="pool")
            for pr in range(4):
                kb.I("dve", lambda e, pr=pr: e.tensor_tensor_scan(CL[:, pr, :], E["rst"][:, 0:NP_], LW[:, pr, :], 0.0, ALU.mult, ALU.add),
                     w=[CL[:]], r=[LW[:], E["cst"][:]])
            yield
            if d == 0:
                tot = v64(CL[:])[:, :, :, 63:64]
            else:
                totf = v64(CL[:])[:, :, :, 63:64].to_broadcast([128, 4, 4, 64])
                kb.tt(v64(F2[:]), totf, v64(CL[:]), ALU.subtract)
                kb.tt(CL[:], F2[:], LW[:], ALU.add)
                tot = v64(CL[:])[:, :, :, 0:1]
            kb.act(GAM[:].unsqueeze(3), tot, AF.Exp)
            yield
            kb.tt(KK[:], K[:], bc4(O_KK), ALU.mult, eng="pool")
            kb.tt(F1[:], KK[:], KK[:], ALU.mult, eng="pool")
            for hf in range(2):
                ps = FPS()
                kb.mm(ps[:], BDm, F1[:, 2 * hf:2 * hf + 2, :])
                kb.rsqrt(F2[:, 2 * hf:2 * hf + 2, :], ps[:].rearrange("p (a t) -> p a t", a=2), 1e-24)
            yield
            kb.tt(KK[:], KK[:], F2[:], ALU.mult)
            kb.tt(F1[:], AA[:], bc4(O_KA), ALU.mult, eng="pool")
            kb.tt(F1[:], F1[:], bc4(O_OMKA), ALU.add, eng="pool")
            kb.tt(KM[:], K[:], F1[:], ALU.mult)
            kb.tt(BB[:], KK[:], AA[:], ALU.mult, eng="pool")
            yield
            kb.tt(F1[:], R[:], KM[:], ALU.mult, eng="pool")
            kb.tt(F1[:], F1[:], bc4(O_RK), ALU.mult, eng="pool")
            for hf in range(2):
                ps = FPS()
                kb.mm(ps[:], BDm, F1[:, 2 * hf:2 * hf + 2, :])
                kb.tt(BON[:, 2 * hf:2 * hf + 2, :], ps[:].rearrange("p (a t) -> p a t", a=2), V[:, 2 * hf:2 * hf + 2, :], ALU.mult)
            if d == 0:
                kb.dma(BONS.ap()[:, :, p0:p0 + NP_], BON[:])
            else:
                kb.dma(F2[:], BONS.ap()[:, :, p0:p0 + NP_])
                kb.tt(BON[:], BON[:], F2[:], ALU.add)
            yield
            kb.tt(F1[:], CL[:], LW[:], ALU.subtract, eng="pool")
            kb.act(FE[:], F1[:], AF.Exp)
            kb.stt(AR[:, :, :, 0:64], v64(KK[:]), -1.0, v64(FE[:]), ALU.mult, ALU.mult)
            yield
            kb.act(FE[:], CL[:], AF.Exp)
            kb.tt(AR[:, :, :, 64:128], v64(R[:]), v64(FE[:]), ALU.mult)
            yield
            kb.act(FE[:], CL[:], AF.Exp, scale=-1.0)
            kb.tt(Bt[:], BB[:], FE[:], ALU.mult)
            kb.tt(Kt[:], KM[:], FE[:], ALU.mult, eng="pool")
            yield
            kb.tt(v64(F1[:]), tot.to_broadcast([128, 4, 4, 64]), v64(CL[:]), ALU.subtract)
            kb.act(FE[:], F1[:], AF.Exp)
            kb.tt(Bh[:], BB[:], FE[:], ALU.mult)
            kb.tt(Kh[:], KM[:], FE[:], ALU.mult, eng="pool")
            yield
            for (src, dst) in ((Bh, BhT), (Kh, KhT), (V, VT)):
                for tb in range(2):
                    ps = FPS()
                    for pr in range(4):
                        kb.tr(ps[:, pr * 128:(pr + 1) * 128], src[:, pr, tb * 128:(tb + 1) * 128], ident)
                    kb.cp(dst[:, tb, :], ps[:], eng=("act" if tb else "dve"))
                    yield

        def pairs(it, b, step):
            d, pc = it["d"], it["pc"]
            AR, Bt, Kt, BhT, KhT, VT, GAM, YP = ARs[b], Bts[b], Kts[b], BhTs[b], KhTs[b], VTs[b], GAMs[b], YPs[b]
            if d == 1:
                p0_ = it["t0"] + pc * NP_
                kb.dma(YP[:], YS.ap()[:, :, p0_:p0_ + NP_])
            if it["first"]:
                sic[0] = 0
                if it["latent"]:
                    kb.dma(ST[0][:], E["st_in"].ap()[l, d].rearrange("p (a v) -> p a v", a=4))
                else:
                    kb.memset(ST[0][:], 0.0)
            mask4 = E["maskF4"] if d == 0 else E["maskB4"]
            maskT8 = E["maskFT8"] if d == 0 else E["maskBT8"]
            for tb in (range(2) if d == 0 else range(1, -1, -1)):
                PA = [P[0], P[1]]; PB = [P[2], P[3]]; PC = [P[4], P[5]]
                for ci in range(2):
                    c = 2 * tb + ci; cb = ci * 64; cs = slice(c * 64, (c + 1) * 64)
                    for h in range(8):
                        par = h % 2; hb = par * 64; pr = h // 2
                        lastm = (h >= 6)
                        kb.mm(PA[par][cb:cb + 64, pr * 128:(pr + 1) * 128], Bt[hb:hb + 64, pr, cs],
                              AR[hb:hb + 64, pr, c, :], tile_position=(hb, cb), inc=lastm)
                        kb.mm(PB[par][cb:cb + 64, pr * 128:(pr + 1) * 128], Kt[hb:hb + 64, pr, cs],
                              AR[hb:hb + 64, pr, c, :], tile_position=(hb, cb), inc=lastm)
                        kb.mm(PC[par][cb:cb + 64, pr * 64:(pr + 1) * 64], AR[hb:hb + 64, pr, c, 0:64],
                              Bt[hb:hb + 64, pr, cs], tile_position=(hb, cb), inc=lastm)
                m4 = mask4.rearrange("p (h t) -> p h t", h=4)
                for par in range(2):
                    kb.tt(NM[:, par:8:2, :], PA[par][:].rearrange("p (h t) -> p h t", h=4), m4, ALU.mult)
                    kb.tt(KMt[:, par:8:2, :], PB[par][:].rearrange("p (h t) -> p h t", h=4), m4, ALU.mult)
                    kb.tt(NTb[0][:, par:8:2, :], PC[par][:, 0:256].rearrange("p (h t) -> p h t", h=4),
                          maskT8[:, 0:256].rearrange("p (h t) -> p h t", h=4), ALU.mult)
                kb.cp(Nb[0][:], NM[:, :, 0:64], eng="pool")
                kb.tt(TTb[0][:], Nb[0][:], E["ident8"].rearrange("p (h t) -> p h t", h=8), ALU.add, eng="pool")
                step()
                cur = 0
                for lev in range(5):
                    nx = 1 - cur
                    Nc, NTc, TTc = Nb[cur], NTb[cur], TTb[cur]
                    PNT, PN, PTt = P[0], P[1], P[2]
                    for ci in range(2):
                        cb = ci * 64
                        for h in range(8):
                            hs = slice(h * 64, (h + 1) * 64)
                            kb.mm(PNT[cb:cb + 64, hs], Nc[cb:cb + 64, h, :], NTc[cb:cb + 64, h, :], tile_position=(cb, cb), inc=(h == 7))
                            if lev < 4:
                                kb.mm(PN[cb:cb + 64, hs], NTc[cb:cb + 64, h, :], Nc[cb:cb + 64, h, :], tile_position=(cb, cb), inc=(h == 7))
                    kb.cp(NTb[nx][:], PNT[:].rearrange("p (h t) -> p h t", h=8), eng="act")
                    if lev < 4:
                        kb.cp(Nb[nx][:], PN[:].rearrange("p (h t) -> p h t", h=8), eng="dve")
                    for ci in range(2):
                        cb = ci * 64
                        for h in range(8):
                            hs = slice(h * 64, (h + 1) * 64)
                            kb.mm(PTt[cb:cb + 64, hs], NTb[nx][cb:cb + 64, h, :], TTc[cb:cb + 64, h, :], tile_position=(cb, cb), inc=(h == 7))
                    kb.tt(TTb[nx][:], TTc[:], PTt[:].rearrange("p (h t) -> p h t", h=8), ALU.add)
                    cur = nx
                    step()
                TT = TTb[cur]
                PX = P[5]
                for ci in range(2):
                    cb = ci * 64
                    for h in range(8):
                        hs = slice(h * 64, (h + 1) * 64)
                        kb.mm(PX[cb:cb + 64, hs], KMt[cb:cb + 64, h, 0:64], VT[cb:cb + 64, tb, hs], tile_position=(cb, cb), inc=(h == 7))
                kb.cp(RHS0[:], PX[:], eng="act")
                step()
                for ci in ((0, 1) if d == 0 else (1, 0)):
                    c = 2 * tb + ci; cb = ci * 64
                    Sc = ST[sic[0] % 2]; Sn = ST[(sic[0] + 1) % 2]
                    sic[0] += 1
                    PR, PU, PYa, PYb, PSn = P[3], P[4], P[5], P[0], P[1]
                    for h in (0, 2, 4, 6, 1, 3, 5, 7):
                        hb = (h % 2) * 64; pr = h // 2; hs = slice(h * 64, (h + 1) * 64)
                        kb.mm(PR[cb:cb + 64, hs], AR[hb:hb + 64, pr, c, 0:64], Sc[hb:hb + 64, pr, :], tile_position=(hb, cb), inc=(h >= 6))
                    kb.tt(RHS[cb:cb + 64, :], PR[cb:cb + 64, :], RHS0[cb:cb + 64, :], ALU.add)
                    step()
                    for h in range(8):
                        hs = slice(h * 64, (h + 1) * 64)
                        kb.mm(PU[cb:cb + 64, hs], TT[cb:cb + 64, h, :], RHS[cb:cb + 64, hs], tile_position=(cb, cb), inc=(h == 7))
                    kb.cp(U[cb:cb + 64, :], PU[cb:cb + 64, :], eng="act")
                    step()
                    for h in (0, 2, 4, 6, 1, 3, 5, 7):
                        hb = (h % 2) * 64; pr = h // 2
                        ys = slice(pr * 64, (pr + 1) * 64)
                        kb.mm(PYa[hb:hb + 64, ys], Sc[hb:hb + 64, pr, :], AR[hb:hb + 64, pr, c, 64:128], tile_position=(hb, hb), inc=(h >= 6))
                    for h in range(8):
                        hb = (h % 2) * 64; pr = h // 2; hs = slice(h * 64, (h + 1) * 64)
                        ys = slice(pr * 64, (pr + 1) * 64)
                        kb.mm(PSn[hb:hb + 64, ys], BhT[cb:cb + 64, tb, hs], U[cb:cb + 64, hs],
                              start=True, stop=False, tile_position=(cb, hb), inc=False)
                        kb.mm(PSn[hb:hb + 64, ys], KhT[cb:cb + 64, tb, hs], VT[cb:cb + 64, tb, hs],
                              start=False, stop=True, tile_position=(cb, hb), inc=(h >= 6))
                    kb.tt(Sn[:], Sc[:], GAM[:, :, c:c + 1].to_broadcast([128, 4, 64]), ALU.mult, eng="pool")
                    kb.tt(Sn[:], Sn[:], PSn[:, 0:256].rearrange("p (a t) -> p a t", a=4), ALU.add)
                    for h in range(8):
                        hb = (h % 2) * 64; pr = h // 2; hs = slice(h * 64, (h + 1) * 64)
                        ys = slice(pr * 64, (pr + 1) * 64)
                        kb.mm(PYb[hb:hb + 64, ys], U[cb:cb + 64, hs], NM[cb:cb + 64, h, 64:128],
                              start=True, stop=False, tile_position=(cb, hb), inc=False)
                        kb.mm(PYb[hb:hb + 64, ys], VT[cb:cb + 64, tb, hs], KMt[cb:cb + 64, h, 64:128],
                              start=False, stop=True, tile_position=(cb, hb), inc=(h >= 6))
                    ytk = slice(c * 64, (c + 1) * 64)
                    pya = PYa[:, 0:256].rearrange("p (a t) -> p a t", a=4)
                    pyb = PYb[:, 0:256].rearrange("p (a t) -> p a t", a=4)
                    if d == 0:
                        kb.cp(YP[:, :, ytk], pya, eng="act")
                    else:
                        kb.tt(YP[:, :, ytk], YP[:, :, ytk], pya, ALU.add)
                    kb.tt(YP[:, :, ytk], YP[:, :, ytk], pyb, ALU.add)
                    step()
            if it["last"] and not it["latent"]:
                kb.dma(E["ns_out"].ap()[it["sidx"], l, d].rearrange("p (a v) -> p a v", a=4), ST[sic[0] % 2][:])

        def post(it, b):
            d, pc = it["d"], it["pc"]
            p0 = it["t0"] + pc * NP_
            YP, BON = YPs[b], BONs[b]
            if d == 0:
                kb.dma(YS.ap()[:, :, p0:p0 + NP_], YP[:])
                return
            for hf in range(2):
                ps = FPS()
                kb.mm(ps[:], BD64, YP[:, 2 * hf:2 * hf + 2, :])
                kb.tt(F1[:, 2 * hf:2 * hf + 2, :], YP[:, 2 * hf:2 * hf + 2, :], ps[:].rearrange("p (a t) -> p a t", a=2), ALU.subtract)
            kb.tt(F2[:], F1[:], F1[:], ALU.mult, eng="pool")
            for hf in range(2):
                ps = FPS()
                kb.mm(ps[:], BD64, F2[:, 2 * hf:2 * hf + 2, :])
                kb.rsqrt(FE[:, 2 * hf:2 * hf + 2, :], ps[:].rearrange("p (a t) -> p a t", a=2), 64e-5)
            kb.tt(F1[:], F1[:], FE[:], ALU.mult)
            kb.tt(F1[:], F1[:], bc4(O_LXG), ALU.mult, eng="pool")
            kb.tt(F1[:], F1[:], bc4(O_LXB), ALU.add, eng="pool")
            kb.tt(F1[:], F1[:], BON[:], ALU.add)
            kb.dma(GD[:], RKV.ap()[:, 14, p0:p0 + NP_])
            kb.act(SGD[:], GD[:], AF.Sigmoid)
            for pr in range(4):
                ps = FPS()
                kb.mm(ps[:, 0:NP_], G2[:, pr * 128:(pr + 1) * 128], SGD[:])
                kb.tt(OC[:, pr, :], F1[:, pr, :], ps[:, 0:NP_], ALU.mult)
            kb.dma(BRC.ap()[:, :, p0:p0 + NP_], OC[:])

        g0 = front(items[0], 0)
        for _ in g0:
            pass
        for i, it in enumerate(items):
            b = i % 2
            nxt = front(items[i + 1], 1 - b) if i + 1 < len(items) else None

            def step(nxt=nxt):
                if nxt is not None:
                    next(nxt, None)

            pairs(it, b, step)
            if nxt is not None:
                for _ in nxt:
                    pass
            post(it, b)


def merge_ffn(kb, l, E):
    P = E["P"]; pp = E["pp"][l]; modT = E["modT"][l]
    XTi = E["XT"][l % 2]; XTo = E["XT"][(l + 1) % 2]
    XM = E["XM"]
    BRA = E["BRA"]; BRB = E["BRB"]; BRC = E["BRC"]; onesD = E["onesD"]
    winv = E["w_in"].ap()[l].rearrange("(kc p) n -> p kc n", p=128)
    pk = [0]

    def PS(lo=0, hi=8):
        i = lo + pk[0] % (hi - lo)
        pk[0] += 1
        return P[i]

    def lnorm(Y, sqs, sc3, og, ob, n):
        pm = PS(0, 2); pq = PS(2, 4)
        for c in range(8):
            kb.mm(pm[:, 0:n], onesD, Y[:, c, :], start=(c == 0), stop=(c == 7), inc=(c == 7))
        for c in range(8):
            sq = sqs[c % 2]
            kb.act(sq[:, 0:n], Y[:, c, :], AF.Square)
            kb.mm(pq[:, 0:n], onesD, sq[:, 0:n], start=(c == 0), stop=(c == 7))
        mean = sc3[:, 0, 0:n]; rstd = sc3[:, 1, 0:n]; tmp = sc3[:, 2, 0:n]
        kb.cp(mean, pm[:, 0:n], eng="act")
        kb.tt(tmp, mean, mean, ALU.mult, eng="pool")
        kb.tt(tmp, pq[:, 0:n], tmp, ALU.subtract)
        kb.rsqrt(rstd, tmp, 1e-5)
        for c in range(8):
            e = "dve" if c % 2 else "pool"
            kb.tt(Y[:, c, :], Y[:, c, :], mean, ALU.subtract, eng=e)
            kb.tt(Y[:, c, :], Y[:, c, :], rstd, ALU.mult, eng=e)
            kb.ts(Y[:, c, :], Y[:, c, :], pp[:, og + c:og + c + 1], ALU.mult, pp[:, ob + c:ob + c + 1], ALU.add, eng=e)

    noC = "noC" in E["dbg"]
    with kb.scope():
        WG = kb.sb("mWG", [128, 8, 3072], BF16)
        WB = kb.sb("mWB", [128, 8, D], BF16)
        wbrv = E["w_br"].ap()[l]
        WBB = kb.sb("mWBB", [64, 8, D], BF16)
        WO = kb.sb("mWO", [128, 8, D], BF16)
        wov_ = E["w_out"].ap()[l].rearrange("(kc p) n -> p kc n", p=128)
        for hf_ in range(2):
            hsl = slice(hf_ * 512, (hf_ + 1) * 512)
            for j in range(3):
                i = j * 2 + hf_
                kb.dma((WG[:, :, i * 512:(i + 1) * 512], i), winv[:, :, C_GL + i * 512:C_GL + (i + 1) * 512], q="pool")
            kb.dma((WB[:, 0:4, hsl], hf_), wbrv[0].rearrange("(cc p) d -> p cc d", p=128)[:, :, hsl], q="pool")
            kb.dma((WB[:, 4:8, hsl], 2 + hf_), wbrv[2].rearrange("(cc p) d -> p cc d", p=128)[:, :, hsl], q="pool")
            kb.dma((WBB[:, :, hsl], hf_), wbrv[1].rearrange("(h p) d -> p h d", p=64)[:, :, hsl], q="pool")
        for hf_ in range(2):
            hsl = slice(hf_ * 512, (hf_ + 1) * 512)
            kb.dma((WO[:, :, hsl], hf_), wov_[:, :, hsl], q="pool")
        XLs = [kb.sb("mXL%d" % i, [128, 8, 512]) for i in range(2)]
        HT = kb.sb("mHT", [128, 8, 512], BF16)
        OA = kb.sb("mOA", [128, 4, 512], BF16); OB = kb.sb("mOB", [64, 8, 512], BF16); OCt = kb.sb("mOC", [128, 4, 512], BF16)
        MI = kb.sb("mMI", [128, 8, 512], BF16)
        sqs = [kb.sb("msq%d" % i, [128, 512]) for i in range(2)]
        sc3 = kb.sb("msc3", [128, 3, 512])
        gts = [kb.sb("mgt%d" % i, [128, 512]) for i in range(2)]
        acc = kb.sb("macc", [128, 512])

        def prep(g):
            XL = XLs[g % 2]
            tok = slice(g * 512, (g + 1) * 512)
            kb.dma(XL[:], XTi.ap()[:, :, tok])
            kb.dma(OA[:], BRA.ap()[:, :, tok]); kb.dma(OB[:], BRB.ap()[:, :, tok])
            if not noC:
                kb.dma(OCt[:], BRC.ap()[:, :, tok])
            sc = mod_cols(modT, 8, g); sh = mod_cols(modT, 0, g)
            for c in range(8):
                kb.ts(HT[:, c, :], XL[:, c, :], sc[c], ALU.mult, sh[c], ALU.add, eng=("dve" if c % 2 else "pool"))

        def mix(g):
            for dc in range(8):
                dcs = slice(dc * 128, (dc + 1) * 128)
                for j in range(3):
                    if j == 2 and noC:
                        continue
                    pg = PS(0, 3); pb = PS(3, 6)
                    for kc in range(8):
                        kb.mm(pg[:], (WG[:, kc, j * D + dc * 128:j * D + (dc + 1) * 128], j * 2 + dc // 4), HT[:, kc, :], start=(kc == 0), stop=(kc == 7), inc=(kc == 7))
                    if j != 1:
                        src = OA if j == 0 else OCt
                        for cc in range(4):
                            kb.mm(pb[:], (WB[:, (j // 2) * 4 + cc, dcs], (j // 2) * 2 + dc // 4), src[:, cc, :], start=(cc == 0), stop=(cc == 3), inc=(cc == 3))
                    else:
                        for h in range(8):
                            kb.mm(pb[:], (WBB[:, h, dcs], dc // 4), OB[:, h, :], start=(h == 0), stop=(h == 7), inc=(h == 7))
                    gt = gts[j % 2]
                    kb.act(gt[:], pg[:], AF.Sigmoid)
                    if j == 0:
                        kb.tt(acc[:], gt[:], pb[:], ALU.mult)
                    else:
                        kb.tt(gt[:], gt[:], pb[:], ALU.mult)
                        kb.tt((MI[:, dc, :] if (j == 2 or (noC and j == 1)) else acc[:]), acc[:], gt[:], ALU.add, eng="pool")

        def wout_ln(g):
            XL = XLs[g % 2]
            g1 = mod_cols(modT, 16, g)
            for dc in range(8):
                ps = PS(6, 8)
                for kc in range(8):
                    kb.mm(ps[:], (WO[:, kc, dc * 128:(dc + 1) * 128], dc // 4), MI[:, kc, :], start=(kc == 0), stop=(kc == 7), inc=(kc == 7))
                kb.ts(XL[:, dc, :], XL[:, dc, :], ALPHA, ALU.mult, eng="pool")
                kb.stt(XL[:, dc, :], ps[:], g1[dc], XL[:, dc, :], ALU.mult, ALU.add)
            lnorm(XL, sqs, sc3, O_LN1G, O_LN1B, 512)
            kb.dma(XM.ap()[:, :, g * 512:(g + 1) * 512], XL[:])

        prep(0)
        for g in range(NG):
            mix(g)
            if g + 1 < NG:
                prep(g + 1)
            wout_ln(g)
    chk(kb, "mrg", l)
    with kb.scope():
        WU = kb.sb("fWU", [128, 8, 4096], BF16)
        upv = E["w_up"].ap()[l].rearrange("(kc p) n -> p kc n", p=128)
        WD = kb.sb("fWD", [128, 32, D], BF16)
        dnv = E["w_dn"].ap()[l].rearrange("(fc p) n -> p fc n", p=128)
        for i in range(8):
            kb.dma((WU[:, :, i * 512:(i + 1) * 512], i), upv[:, :, i * 512:(i + 1) * 512], q="pool")
        for i in range(8):
            kb.dma((WD[:, i * 4:(i + 1) * 4, :], i), dnv[:, i * 4:(i + 1) * 4, :], q="pool")
        NF = 256
        NGF = T // NF
        X1s = [kb.sb("fX1%d" % i, [128, 8, NF]) for i in range(2)]
        H2 = kb.sb("fH2", [128, 8, NF], BF16)
        ACTt = kb.sb("fACT", [128, 32, NF], BF16)
        sqs = [kb.sb("fsq%d" % i, [128, 512]) for i in range(2)]
        sc3 = kb.sb("fsc3", [128, 3, 512])
        rl = [kb.sb("frl%d" % i, [128, NF]) for i in range(2)]

        def prep_up(gg):
            g = (gg * NF) // 512
            X1 = X1s[gg % 2]
            kb.dma(X1[:], XM.ap()[:, :, gg * NF:(gg + 1) * NF])
            sc2 = mod_cols(modT, 32, g); sh2 = mod_cols(modT, 24, g)
            for c in range(8):
                kb.ts(H2[:, c, :], X1[:, c, :], sc2[c], ALU.mult, sh2[c], ALU.add, eng=("dve" if c % 2 else "pool"))
            for fc in range(32):
                ps = PS(0, 4)
                for kc in range(8):
                    kb.mm(ps[:, 0:NF], (WU[:, kc, fc * 128:(fc + 1) * 128], fc // 4), H2[:, kc, :], start=(kc == 0), stop=(kc == 7), inc=(kc == 7))
                r_ = rl[fc % 2]
                kb.act(r_[:], ps[:, 0:NF], AF.Relu)
                kb.tt(ACTt[:, fc, :], r_[:], r_[:], ALU.mult, eng=("dve" if fc % 2 else "pool"))

        def down(gg):
            g = (gg * NF) // 512
            X1 = X1s[gg % 2]
            g2 = mod_cols(modT, 40, g)
            for dc in range(8):
                ps = PS(4, 8)
                for fc in range(32):
                    kb.mm(ps[:, 0:NF], (WD[:, fc, dc * 128:(dc + 1) * 128], fc // 4), ACTt[:, fc, :], start=(fc == 0), stop=(fc == 31), inc=(fc == 31))
                kb.ts(X1[:, dc, :], X1[:, dc, :], ALPHA, ALU.mult, eng="pool")
                kb.stt(X1[:, dc, :], ps[:, 0:NF], g2[dc], X1[:, dc, :], ALU.mult, ALU.add)

        def ln_out(gg):
            X1 = X1s[gg % 2]
            lnorm(X1, sqs, sc3, O_LN2G, O_LN2B, NF)
            kb.dma(XTo.ap()[:, :, gg * NF:(gg + 1) * NF], X1[:])

        prep_up(0)
        for gg in range(NGF):
            down(gg)
            if gg + 1 < NGF:
                prep_up(gg + 1)
            ln_out(gg)


def _consts():
    c = np.zeros((128, 9, 512), np.float32)
    r = np.arange(128)
    c[:, 0, 0:128] = np.eye(128)
    bd = (r[:, None] // 64 == r[None, :] // 64).astype(np.float32)
    c[:, 0, 128:256] = bd
    c[:, 0, 256:384] = bd / 64.0
    c[0:64, 0, 384:448] = 1.0 / 64.0
    s = (r % 64)[:, None]
    col = np.arange(128)[None, :]
    mf = np.where(col < 64, s < col, s <= col - 64).astype(np.float32)
    mb = np.where(col < 64, s > col, s >= col - 64).astype(np.float32)
    c[:, 1, :] = np.tile(mf, (1, 4))
    c[:, 2, :] = np.tile(mb, (1, 4))
    t = (r % 64)[:, None]
    sc = np.arange(64)[None, :]
    c[:, 3, :] = np.tile((sc < t).astype(np.float32), (1, 8))
    c[:, 4, :] = np.tile((sc > t).astype(np.float32), (1, 8))
    c[:, 5, :] = np.tile((sc == t).astype(np.float32), (1, 8))
    c[:, 6, :] = (np.arange(512) % 64 != 0).astype(np.float32)[None, :]
    rm = np.zeros((64, 64), np.float32)
    for d in range(64):
        i = d % 32
        if i < 16:
            rm[d + 16, d] = -1.0
        else:
            rm[d - 16, d] = 1.0
    c[0:64, 7, 0:64] = rm
    c[:, 7, 64:192] = 1.0
    c[:, 8, 0:128] = 1.0 / 1024.0
    tt = np.arange(2048)
    row = (tt // 64).astype(np.float32)
    colp = (tt % 64).astype(np.float32)
    inv = (10000.0 ** (-np.arange(0, 32, 2, dtype=np.float32) / 32.0)).astype(np.float32)
    rope = np.zeros((64, 2, 2048), np.float32)
    for d in range(64):
        pos = row if d < 32 else colp
        ang = (pos * inv[(d % 32) % 16]).astype(np.float32)
        rope[d, 0] = np.cos(ang)
        rope[d, 1] = np.sin(ang)
    return c, rope


def _colT(v, n):
    return np.ascontiguousarray(np.asarray(v, np.float32).reshape(n, 128).T)


_NC_CACHE = {}


def kernel(**inp):
    f = lambda k: np.asarray(inp[k], np.float32)
    n = inp.get("_ncores", 8)
    cst, rope = _consts()
    pp = np.zeros((2, 128, NPP), np.float32)
    for l in range(2):
        pp[l, :, O_BADA:O_BADA + 48] = _colT(f("b_ada")[l], 48)
        pp[l, :, O_LN1G:O_LN1G + 8] = _colT(f("ln1_g")[l], 8)
        pp[l, :, O_LN1B:O_LN1B + 8] = _colT(f("ln1_b")[l], 8)
        pp[l, :, O_LN2G:O_LN2G + 8] = _colT(f("ln2_g")[l], 8)
        pp[l, :, O_LN2B:O_LN2B + 8] = _colT(f("ln2_b")[l], 8)
        pp[l, 0:64, O_QN] = f("q_norm")[l]
        pp[l, 0:64, O_KN] = f("k_norm")[l]
        for d in range(2):
            pp[l, :, O_MU + d * 13:O_MU + (d + 1) * 13] = _colT(f("rwkv_mu")[l, d], 13)
            pp[l, :, O_W0 + d * 4:O_W0 + (d + 1) * 4] = _colT(f("rwkv_w0")[l, d], 4)
            pp[l, :, O_A0 + d * 4:O_A0 + (d + 1) * 4] = _colT(f("rwkv_a0")[l, d], 4)
        pp[l, :, O_KK:O_KK + 4] = _colT(f("rwkv_k_k")[l], 4)
        pp[l, :, O_KA:O_KA + 4] = _colT(f("rwkv_k_a")[l], 4)
        pp[l, :, O_RK:O_RK + 4] = _colT(f("rwkv_r_k")[l].reshape(512), 4)
        pp[l, :, O_LXG:O_LXG + 4] = _colT(f("rwkv_lnx_g")[l], 4)
        pp[l, :, O_LXB:O_LXB + 4] = _colT(f("rwkv_lnx_b")[l], 4)
    wsT = np.ascontiguousarray(f("sgu_w").transpose(0, 3, 1, 2))
    sgb = np.ascontiguousarray(f("sgu_b").reshape(2, 1, 512))
    lnA = np.ascontiguousarray(np.broadcast_to(
        np.stack([f("sgu_ln_g"), f("sgu_ln_b")], 1)[:, :, None, :], (2, 2, 128, 512)))
    wa2 = np.ascontiguousarray(np.concatenate([f("rwkv_w2"), f("rwkv_a2")], axis=2))
    shared = dict(pp=pp, w_ada=f("w_ada"), w_in=f("w_in"), w_branch=f("w_branch"), w_out=f("w_out"),
                  w_up=f("w_up"), w_down=f("w_down"), wsT=wsT, sgb=sgb, lnA=lnA, wa2=wa2, g2=f("rwkv_g2"),
                  cst=cst, rope=rope)
    xp, xs = f("x_prompt"), f("x_sample")
    ck, cvv, st, c, cctx = f("cache_k"), f("cache_v"), f("state_wkv"), f("c"), f("c_ctx")
    in_maps = []
    for i in range(n):
        m = dict(shared)
        m["x"] = np.ascontiguousarray(np.concatenate([xp[4 * i:4 * i + 4].reshape(1024, D), xs[i]], 0))
        m["ck"] = np.ascontiguousarray(ck[i].reshape(2, 512, 128))
        m["cvv"] = np.ascontiguousarray(cvv[i].reshape(2, 512, 128))
        s_ = st[i].reshape(2, 2, 4, 2, 64, 64)
        m["st0"] = np.ascontiguousarray(s_.transpose(0, 1, 3, 5, 2, 4).reshape(2, 2, 128, 256))
        cvec = np.stack([cctx, c[i]], -1)
        m["cvec"] = np.ascontiguousarray(cvec.reshape(8, 128, 2).transpose(1, 0, 2))
        in_maps.append(m)
    if inp.get("_prep_only"):
        return in_maps
    if "nc" not in _NC_CACHE:
        _NC_CACHE["nc"] = build()
    kb = _NC_CACHE["nc"]
    res = run_bass_kernel_spmd(kb.nc, in_maps, core_ids=list(range(n)))
    yp = np.zeros((32, 256, D), np.float32)
    ys = np.zeros((8, 2048, D), np.float32)
    nk = np.zeros((32, 2, 256, 2, 64), np.float32)
    nv = np.zeros((32, 2, 256, 2, 64), np.float32)
    ns = np.zeros((32, 2, 2, 8, 64, 64), np.float32)
    for i in range(n):
        r = res.results[i]
        yp[4 * i:4 * i + 4] = r["y"][:1024].reshape(4, 256, D)
        ys[i] = r["y"][1024:]
        nk[4 * i:4 * i + 4] = r["nk"].reshape(2, 4, 256, 2, 64).transpose(1, 0, 2, 3, 4)
        nv[4 * i:4 * i + 4] = r["nv"].reshape(2, 4, 256, 2, 64).transpose(1, 0, 2, 3, 4)
        s_ = r["ns"].reshape(4, 2, 2, 2, 64, 4, 64)
        ns[4 * i:4 * i + 4] = s_.transpose(0, 1, 2, 5, 3, 6, 4).reshape(4, 2, 2, 8, 64, 64)
    return (yp, ys, nk, nv, ns)
```

```python
import numpy as np
from contextlib import ExitStack
import concourse.bass as bass
import concourse.mybir as mybir
from concourse.bass_utils import run_bass_kernel_spmd

F32 = mybir.dt.float32
F32R = mybir.dt.float32r
BF16 = mybir.dt.bfloat16
AF = mybir.ActivationFunctionType
ALU = mybir.AluOpType
AX = mybir.AxisListType


class KB:
    def __init__(self, n_dma_sems=12):
        self.nc = bass.Bass("TRN2", target_bir_lowering=False)
        nc = self.nc
        self.es = ExitStack()
        self.es_root = self.es
        self.eng = {"pe": nc.tensor, "dve": nc.vector, "act": nc.scalar, "pool": nc.gpsimd, "sp": nc.sync}
        self.sem = {}
        self.cnt = {}
        self.clock = {}
        self.snap = {}
        for e in self.eng:
            self.sem[e] = self.es.enter_context(nc.semaphore("s_" + e))
            self.cnt[e] = 0
            self.clock[e] = {}
            self.snap[e] = {}
        self.dq = {}
        for q in ("sp", "pool", "act"):
            ids = []
            for i in range(n_dma_sems):
                sid = "d_%s_%d" % (q, i)
                self.sem[sid] = self.es.enter_context(nc.semaphore(sid))
                self.cnt[sid] = 0
                self.snap[sid] = {}
                ids.append(sid)
            self.dq[q] = [ids, 0]
        self.EPOCH = 60000
        self.epsem = {e: {0: self.sem[e]} for e in self.eng}
        self.lastw = {}
        self.readers = {}
        self.n_ins = 0
        self.n_wait = 0
        self.uid = 0
        self.ps_row = {}
        self.pending = {}
        self.drain_mode = 0
        self.last_tp = None

    def sb(self, name, shape, dt=F32):
        self.uid += 1
        name = "%s_%d" % (name, self.uid)
        return self.es.enter_context(self.nc.sbuf_tensor("s_" + name, list(shape), dt))

    def ps(self, name, shape, dt=F32):
        return self.es.enter_context(self.nc.psum_tensor("p_" + name, list(shape), dt))

    def dram(self, name, shape, dt=F32, kind="Internal"):
        return self.nc.dram_tensor(name, list(shape), dt, kind=kind)

    def _semv(self, f, c):
        if f not in self.eng:
            return self.sem[f], c
        ep = (c - 1) // self.EPOCH
        if ep not in self.epsem[f]:
            self.epsem[f][ep] = self.es_root.enter_context(self.nc.semaphore("s_%s_e%d" % (f, ep)))
        return self.epsem[f][ep], c - ep * self.EPOCH

    @staticmethod
    def _key(x):
        if isinstance(x, tuple):
            return (x[0].tensor.name, x[1])
        return (x.tensor.name, None)

    @staticmethod
    def _ap(x):
        return x[0] if isinstance(x, tuple) else x

    def _deps(self, eng, reads, writes):
        need = {}

        def add(f, c, same_ok):
            if f == eng and same_ok:
                return
            if c > need.get(f, 0):
                need[f] = c

        for k in reads:
            lw = self.lastw.get(k)
            if lw:
                add(lw[0], lw[1], False)
            if k[0].startswith("p_"):
                for f, c in self.readers.get(k, {}).items():
                    add(f, c, True)
        for k in writes:
            lw = self.lastw.get(k)
            if lw:
                add(lw[0], lw[1], True)
            for f, c in self.readers.get(k, {}).items():
                add(f, c, True)
        clk = self.clock[eng]
        e = self.eng[eng]
        for f, c in need.items():
            if clk.get(f, 0) >= c:
                continue
            if c > self.cnt[f]:
                raise RuntimeError("wait on unmaterialised count %s %d > %d" % (f, c, self.cnt[f]))
            sm, val = self._semv(f, c)
            e.wait_ge(sm, val)
            self.n_wait += 1
            for g, v in self.snap[f][c].items():
                if v > clk.get(g, 0):
                    clk[g] = v
            clk[f] = max(clk.get(f, 0), c)

    def _record(self, who, c, reads, writes):
        for k in reads:
            self.readers.setdefault(k, {})[who] = c
        for k in writes:
            self.lastw[k] = (who, c)
            self.readers[k] = {}

    def I(self, eng, fn, w=(), r=(), inc=True):
        rk = [self._key(x) for x in r]
        wk = [self._key(x) for x in w]
        self._deps(eng, rk, wk)
        ins = fn(self.eng[eng])
        self.n_ins += 1
        if not inc:
            self._record(eng, self.cnt[eng] + 1, rk, wk)
            self.pending[eng] = True
            return ins
        self.pending[eng] = False
        self.cnt[eng] += 1
        c = self.cnt[eng]
        ins.then_inc(self._semv(eng, c)[0], 1)
        s = dict(self.clock[eng])
        s[eng] = c
        self.snap[eng][c] = s
        self._record(eng, c, rk, wk)
        return ins

    def dma(self, out, in_, q="sp", **kw):
        rk = [self._key(in_)]
        wk = [self._key(out)]
        self._deps(q, rk, wk)
        ids, pos = self.dq[q]
        sid = ids[pos % len(ids)]
        self.dq[q][1] = pos + 1
        clk = self.clock[q]
        prev = self.cnt[sid]
        if prev and clk.get(sid, 0) < prev:
            self.eng[q].wait_ge(self.sem[sid], prev)
            for g, v in self.snap[sid][prev].items():
                if v > clk.get(g, 0):
                    clk[g] = v
            clk[sid] = prev
        ins = self.eng[q].dma_start(out=self._ap(out), in_=self._ap(in_), **kw)
        c = prev + 16
        self.cnt[sid] = c
        ins.then_inc(self.sem[sid], 16)
        s = dict(clk)
        s[sid] = c
        self.snap[sid][c] = s
        self._record(sid, c, rk, wk)
        self.n_ins += 1
        return ins

    def finish(self):
        assert not any(self.pending.values()), self.pending
        e = "sp"
        clk = self.clock[e]
        for f in self.sem:
            c = self.cnt[f]
            if c and clk.get(f, 0) < c and f != e:
                sm, val = self._semv(f, c)
                self.eng[e].wait_ge(sm, val)
        self.es.close()

    def mm(self, out, lhsT, rhs, start=True, stop=True, inc=True, **kw):
        a = self._ap
        lt = a(lhsT)
        row = lt.base_partition() if lt.partition_size() < 128 else -1
        okey = self._key(out)
        prev = self.ps_row.get(okey)
        if prev is not None and prev[0] != row and self.clock["pe"].get("pe", 0) < prev[1]:
            if prev[1] > self.cnt["pe"]:
                raise RuntimeError("PE row switch on bank %s needs a materialised count" % (okey,))
            sm, val = self._semv("pe", prev[1])
            self.eng["pe"].wait_ge(sm, val)
            self.clock["pe"]["pe"] = prev[1]
            self.n_wait += 1
        self.ps_row[okey] = (row, self.cnt["pe"] + 1)
        tp = kw.get("tile_position")
        if self.drain_mode and tp is not None and self.cnt["pe"] and self.clock["pe"].get("pe", 0) < self.cnt["pe"]:
            if self.drain_mode == 1 or (self.drain_mode == 2 and tp != self.last_tp):
                sm, val = self._semv("pe", self.cnt["pe"])
                self.eng["pe"].wait_ge(sm, val)
                self.clock["pe"]["pe"] = self.cnt["pe"]
        self.last_tp = tp
        return self.I("pe", lambda e: e.matmul(a(out), a(lhsT), a(rhs), start=start, stop=stop, **kw),
                      w=[out], r=[lhsT, rhs], inc=inc)

    def tr(self, out, in_, ident):
        a = self._ap
        return self.I("pe", lambda e: e.transpose(a(out), a(in_), a(ident)), w=[out], r=[in_, ident])

    def act(self, out, in_, func, bias=None, scale=1.0, accum=None, eng="act"):
        a = self._ap
        r = [in_]
        kw = {}
        if bias is not None:
            if isinstance(bias, (int, float)):
                kw["bias"] = float(bias)
            else:
                kw["bias"] = a(bias)
                r.append(bias)
        if isinstance(scale, (int, float)):
            kw["scale"] = float(scale)
        else:
            kw["scale"] = a(scale)
            r.append(scale)
        w = [out]
        if accum is not None:
            kw["accum_out"] = a(accum)
            w.append(accum)
        return self.I(eng, lambda e: e.activation(a(out), a(in_), func, **kw), w=w, r=r)

    def tt(self, out, x, y, op, eng="dve"):
        a = self._ap
        return self.I(eng, lambda e: e.tensor_tensor(a(out), a(x), a(y), op), w=[out], r=[x, y])

    def ts(self, out, x, s1, op0, s2=None, op1=None, eng="dve", accum=None):
        a = self._ap
        r = [x]
        v1 = s1
        if not isinstance(s1, (int, float)):
            r.append(s1)
            v1 = a(s1)
        v2 = s2
        if s2 is not None and not isinstance(s2, (int, float)):
            r.append(s2)
            v2 = a(s2)
        kw = {}
        w = [out]
        if op1 is not None:
            kw["op1"] = op1
        if accum is not None:
            kw["accum_out"] = a(accum)
            w.append(accum)
        return self.I(eng, lambda e: e.tensor_scalar(a(out), a(x), v1, v2, op0, **kw), w=w, r=r)

    def stt(self, out, x, s, y, op0, op1, eng="dve"):
        a = self._ap
        r = [x, y]
        v = s
        if not isinstance(s, (int, float)):
            r.append(s)
            v = a(s)
        return self.I(eng, lambda e: e.scalar_tensor_tensor(a(out), a(x), v, a(y), op0, op1), w=[out], r=r)

    def cp(self, out, in_, eng="dve"):
        a = self._ap
        if eng == "act":
            return self.I(eng, lambda e: e.copy(a(out), a(in_)), w=[out], r=[in_])
        return self.I(eng, lambda e: e.tensor_copy(a(out), a(in_)), w=[out], r=[in_])

    def memset(self, out, val, eng="pool"):
        a = self._ap
        return self.I(eng, lambda e: e.memset(a(out), val), w=[out])

    def recip(self, out, in_):
        a = self._ap
        return self.I("dve", lambda e: e.reciprocal(a(out), a(in_)), w=[out], r=[in_])

    def rsqrt(self, out, in_, eps):
        self.act(out, in_, AF.Ln, bias=eps)
        self.act(out, out, AF.Exp, scale=-0.5)

    def barrier(self):
        assert not any(self.pending.values()), self.pending
        for e in self.eng:
            clk = self.clock[e]
            for f in self.sem:
                c = self.cnt[f]
                if f == e or not c or clk.get(f, 0) >= c:
                    continue
                sm, val = self._semv(f, c)
                self.eng[e].wait_ge(sm, val)
                clk[f] = c
        full = {f: self.cnt[f] for f in self.sem if self.cnt[f]}
        for e in self.eng:
            for f, c in full.items():
                if f != e:
                    self.clock[e][f] = max(self.clock[e].get(f, 0), c)

    def scope(self):
        kb = self

        class _S:
            def __enter__(s):
                s.old = kb.es
                kb.es = ExitStack()
                return s

            def __exit__(s, *a):
                kb.barrier()
                kb.es.close()
                kb.es = s.old
                return False

        return _S()

T = 3072
NG = 6
D = 1024
NPP = 160
O_BADA, O_LN1G, O_LN1B, O_LN2G, O_LN2B, O_QN, O_KN = 0, 48, 56, 64, 72, 80, 81
O_MU, O_W0, O_A0, O_KK, O_KA, O_RK, O_LXG, O_LXB, O_OMKA, O_OMU = 82, 108, 116, 124, 128, 132, 136, 140, 144, 148
NPP = 176
ALPHA = 4.0 ** 0.25
C_UA, C_VA, C_Q, C_K, C_V, C_R, C_GL = 0, 512, 1024, 1536, 1664, 1792, 3712


class StopBuild(Exception):
    pass


def build(stop_after=None, dbg=()):
    kb = KB()
    kb.stop_after = stop_after
    kb.drain_mode = 1 if "drain1" in dbg else (2 if "drain2" in dbg else 0)
    try:
        _build(kb, stop_after, dbg)
    except StopBuild:
        pass
    kb.finish()
    return kb


def chk(kb, name, l=0):
    if kb.stop_after == (name, l):
        raise StopBuild()


def _build(kb, stop_after, dbg):
    nc = kb.nc
    IN = lambda n, s, dt=F32: kb.dram(n, s, dt, kind="ExternalInput")
    OUT = lambda n, s, dt=F32: kb.dram(n, s, dt, kind="ExternalOutput")
    x_in = IN("x", [T, D])
    ck_in = IN("ck", [2, 512, 128])
    cvv_in = IN("cvv", [2, 512, 128])
    st_in = IN("st0", [2, 2, 128, 256])
    cvec_in = IN("cvec", [128, 8, 2])
    pp_in = IN("pp", [2, 128, NPP])
    w_ada = IN("w_ada", [2, D, 6144])
    w_in = IN("w_in", [2, D, 6784])
    w_br = IN("w_branch", [2, 3, 512, D])
    w_out = IN("w_out", [2, D, D])
    w_up = IN("w_up", [2, D, 4096])
    w_dn = IN("w_down", [2, 4096, D])
    wsT_in = IN("wsT", [2, 128, 4, 128])
    sgb_in = IN("sgb", [2, 1, 512])
    lnA_in = IN("lnA", [2, 2, 128, 512])
    wa2_in = IN("wa2", [2, 2, 128, 512])
    g2_in = IN("g2", [2, 128, 512])
    cst_in = IN("cst", [128, 9, 512])
    rope_in = IN("rope", [64, 2, 2048])
    y_out = OUT("y", [T, D])
    nk_out = OUT("nk", [2, 1024, 128])
    nv_out = OUT("nv", [2, 1024, 128])
    ns_out = OUT("ns", [4, 2, 2, 128, 256])
    XT = [kb.dram("XT%d" % i, [128, 8, T]) for i in range(2)]
    RKV = kb.dram("RKV", [128, 15, T])
    BRA = kb.dram("BRA", [128, 4, T], BF16)
    BRB = kb.dram("BRB", [64, 8, T], BF16)
    BRC = kb.dram("BRC", [128, 4, T], BF16)
    BONS = kb.dram("BONS", [128, 4, T])
    XM = kb.dram("XM", [128, 8, T])
    YS = kb.dram("YS", [128, 4, T])
    dbg_out = {}
    if "dump" in dbg:
        dbg_out["d1"] = OUT("dbg1", [128, 32, 256])
        dbg_out["d2"] = OUT("dbg2", [128, 14, 256])
        dbg_out["d3"] = OUT("dbg3", [128, 6, 1024])

    P = [kb.ps("P%d" % i, [128, 512]) for i in range(8)]
    cst = kb.sb("cst", [128, 9, 512])
    kb.dma(cst[:], cst_in.ap()[:, :, :])
    ident = cst[:, 0, 0:128]
    BDm = cst[:, 0, 128:256]
    BD64 = cst[:, 0, 256:384]
    ones64 = cst[0:64, 0, 384:448]
    maskF4, maskB4, maskFT8, maskBT8, ident8, rst = (cst[:, i, :] for i in range(1, 7))
    rm_f = cst[0:64, 7, 0:64]
    ones_row = cst[:, 7, 64:192]
    onesD = cst[:, 8, 0:128]
    rm_b = kb.sb("rm_b", [64, 64], BF16)
    kb.cp(rm_b[:], rm_f, eng="dve")
    ident_b = kb.sb("ident_b", [128, 128], BF16)
    kb.cp(ident_b[:], ident, eng="dve")
    pp = [kb.sb("pp%d" % l, [128, NPP]) for l in range(2)]
    for l in range(2):
        kb.dma(pp[l][:], pp_in.ap()[l])
    modT = [kb.sb("modT%d" % l, [128, 48, 2]) for l in range(2)]

    with kb.scope():
        cv = kb.sb("cv", [128, 8, 2])
        sl = kb.sb("sl", [128, 8, 2])
        kb.dma(cv[:], cvec_in.ap()[:, :, :])
        kb.act(sl[:], cv[:], AF.Silu)
        wts = [kb.sb("wadat%d" % i, [128, 8, 512]) for i in range(2)]
        for l in range(2):
            wv = w_ada.ap()[l].rearrange("(kc p) n -> p kc n", p=128)
            for nb in range(12):
                wt = wts[nb % 2]
                kb.dma(wt[:], wv[:, :, nb * 512:(nb + 1) * 512])
                ps = P[nb % 2]
                for j in range(4):
                    for kc in range(8):
                        kb.mm(ps[:, 2 * j:2 * j + 2], wt[:, kc, j * 128:(j + 1) * 128], sl[:, kc, :],
                              start=(kc == 0), stop=(kc == 7), inc=(kc == 7))
                for j in range(4):
                    c = nb * 4 + j
                    kb.ts(modT[l][:, c, :], ps[:, 2 * j:2 * j + 2], pp[l][:, O_BADA + c:O_BADA + c + 1], ALU.add)
            for c0 in (8, 32):
                kb.ts(modT[l][:, c0:c0 + 8, :], modT[l][:, c0:c0 + 8, :], 1.0, ALU.add, eng="pool")
            kb.ts(pp[l][:, O_OMU:O_OMU + 26], pp[l][:, O_MU:O_MU + 26], -1.0, ALU.mult, 1.0, ALU.add, eng="pool")
            kb.ts(pp[l][:, O_OMKA:O_OMKA + 4], pp[l][:, O_KA:O_KA + 4], -1.0, ALU.mult, 1.0, ALU.add, eng="pool")

    chk(kb, "mod")
    with kb.scope():
        xts = [kb.sb("xt_t%d" % i, [128, 8, 512]) for i in range(2)]
        xins = [kb.sb("xin%d" % i, [128, 1024]) for i in range(2)]
        k = 0
        for g in range(NG):
            xt = xts[g % 2]
            for tt in range(4):
                xin = xins[tt % 2]
                kb.dma(xin[:], x_in.ap()[(g * 4 + tt) * 128:(g * 4 + tt + 1) * 128, :])
                for half in range(2):
                    ps = P[k % 4]
                    k += 1
                    for c in range(4):
                        kb.tr(ps[:, c * 128:(c + 1) * 128], xin[:, (half * 4 + c) * 128:(half * 4 + c + 1) * 128], ident)
                    kb.cp(xt[:, half * 4:(half + 1) * 4, tt * 128:(tt + 1) * 128],
                          ps[:].rearrange("p (c t) -> p c t", c=4), eng=("dve" if half else "act"))
            kb.dma(XT[0].ap()[:, :, g * 512:(g + 1) * 512], xt[:])

    chk(kb, "t0")
    for l in range(2):
        layer(kb, l, locals())
        chk(kb, "layer", l)

    if stop_after is None:
        with kb.scope():
            xts = [kb.sb("oxt%d" % i, [128, 8, 512]) for i in range(2)]
            yos = [kb.sb("yo%d" % i, [128, 1024]) for i in range(2)]
            k = 0
            for g in range(NG):
                xt = xts[g % 2]
                kb.dma(xt[:], XT[0].ap()[:, :, g * 512:(g + 1) * 512])
                for tt in range(4):
                    yo = yos[tt % 2]
                    for half in range(2):
                        ps = P[k % 4]
                        k += 1
                        for c in range(4):
                            kb.tr(ps[:, c * 128:(c + 1) * 128], xt[:, half * 4 + c, tt * 128:(tt + 1) * 128], ident)
                        kb.cp(yo[:, half * 512:(half + 1) * 512], ps[:], eng=("dve" if half else "act"))
                    kb.dma(y_out.ap()[(g * 4 + tt) * 128:(g * 4 + tt + 1) * 128, :], yo[:])


def mod_cols(modT_l, base, g):
    v = 0 if g < 2 else 1
    return [modT_l[:, base + c, v:v + 1] for c in range(8)]


def layer(kb, l, E):
    P = E["P"]; pp = E["pp"][l]; modT = E["modT"][l]; cst = E["cst"]
    XTi = E["XT"][l % 2]; XTo = E["XT"][(l + 1) % 2]
    w_in = E["w_in"]; RKV = E["RKV"]; BRA = E["BRA"]; BRB = E["BRB"]; BRC = E["BRC"]
    ident = E["ident"]; ident_b = E["ident_b"]
    winv = w_in.ap()[l].rearrange("(kc p) n -> p kc n", p=128)
    pk = [0]

    def PS(lo=0, hi=8):
        i = lo + pk[0] % (hi - lo)
        pk[0] += 1
        return P[i]

    ek = [0]

    def EV():
        ek[0] += 1
        return "dve" if ek[0] % 2 else "act"

    with kb.scope():
        HT = kb.sb("HT", [128, 8, T], BF16)
        QT = kb.sb("QT", [64, 8, T], BF16)
        KT = kb.sb("KT", [64, 2, T + 512], BF16)
        VS = kb.sb("VS", [128, 28, 2, 65], BF16)
        kb.memset(VS[:], 1.0)
        with kb.scope():
            xls = [kb.sb("xl%d" % i, [128, 8, 512]) for i in range(2)]
            for g in range(NG):
                xl = xls[g % 2]
                kb.dma(xl[:], XTi.ap()[:, :, g * 512:(g + 1) * 512])
                sc = mod_cols(modT, 8, g); sh = mod_cols(modT, 0, g)
                for c in range(8):
                    kb.ts(HT[:, c, g * 512:(g + 1) * 512], xl[:, c, :], sc[c], ALU.mult, sh[c], ALU.add,
                          eng=("dve" if c % 2 else "pool"))
        chk(kb, "ht", l)
        with kb.scope():
            UA = kb.sb("UA", [128, 4, T], BF16)
            W = kb.sb("Wua", [128, 8, 512], BF16)
            kb.dma(W[:], winv[:, :, C_UA:C_UA + 512], q="pool")
            for g in range(NG):
                for j in range(4):
                    ps = PS()
                    for kc in range(8):
                        kb.mm(ps[:], W[:, kc, j * 128:(j + 1) * 128], HT[:, kc, g * 512:(g + 1) * 512],
                              start=(kc == 0), stop=(kc == 7), inc=(kc == 7))
                    kb.cp(UA[:, j, g * 512:(g + 1) * 512], ps[:], eng=EV())
            chk(kb, "ua", l)
            W2 = kb.sb("Wva", [128, 8, 512], BF16)
            kb.dma(W2[:], winv[:, :, C_VA:C_VA + 512], q="pool")
            wsT = kb.sb("wsT", [128, 4, 128], BF16)
            kb.dma(wsT[:], E["wsT_in"].ap()[l], q="pool")
            sgb = kb.sb("sgb", [1, 512])
            kb.dma(sgb[:], E["sgb_in"].ap()[l])
            lnG = kb.sb("lnG", [128, 512]); lnB = kb.sb("lnB", [128, 512])
            kb.dma(lnG[:], E["lnA_in"].ap()[l, 0]); kb.dma(lnB[:], E["lnA_in"].ap()[l, 1])
            ones_row = E["ones_row"]
            sqs = [kb.sb("a_sq%d" % i, [128, 512]) for i in range(2)]
            v0s = [kb.sb("a_v0%d" % i, [128, 512]) for i in range(2)]
            vns = [kb.sb("a_vn%d" % i, [128, 512], BF16) for i in range(2)]
            sts = [kb.sb("a_st%d" % i, [128, 8]) for i in range(2)]
            def vaA(tt):
                ps = PS(0, 4)
                for kc in range(8):
                    kb.mm(ps[:], HT[:, kc, tt * 128:(tt + 1) * 128], W2[:, kc, :], start=(kc == 0), stop=(kc == 7), inc=(kc == 7))
                kb.act(v0s[tt % 2][:], ps[:], AF.Copy)

            def vaB(tt):
                sq = sqs[tt % 2]; v0 = v0s[tt % 2]; vn = vns[tt % 2]; st = sts[tt % 2]
                kb.act(sq[:], v0[:], AF.Square)
                kb.I("dve", lambda e: e.reduce_sum(st[:, 0:1], v0[:], AX.X), w=[st[:]], r=[v0[:]])
                kb.I("dve", lambda e: e.reduce_sum(st[:, 1:2], sq[:], AX.X), w=[st[:]], r=[sq[:], st[:]])
                kb.ts(st[:, 2:3], st[:, 0:1], 1.0 / 512, ALU.mult)
                kb.tt(st[:, 3:4], st[:, 2:3], st[:, 2:3], ALU.mult)
                kb.stt(st[:, 4:5], st[:, 1:2], 1.0 / 512, st[:, 3:4], ALU.mult, ALU.subtract)
                kb.rsqrt(st[:, 5:6], st[:, 4:5], 1e-5)
                kb.stt(st[:, 6:7], st[:, 2:3], -1.0, st[:, 5:6], ALU.mult, ALU.mult)
                kb.act(v0[:], v0[:], AF.Identity, bias=st[:, 6:7], scale=st[:, 5:6])
                kb.tt(v0[:], v0[:], lnG[:], ALU.mult, eng="pool")
                kb.tt(vn[:], v0[:], lnB[:], ALU.add, eng="pool")

            def vaC(tt):
                vn = vns[tt % 2]
                ps2 = PS(4, 8)
                for j in range(4):
                    kb.mm(ps2[:, j * 128:(j + 1) * 128], vn[:, j * 128:(j + 1) * 128], wsT[:, j, :], start=True, stop=False)
                    kb.mm(ps2[:, j * 128:(j + 1) * 128], ones_row[0:1, :], sgb[0:1, j * 128:(j + 1) * 128], start=False, stop=True)
                kb.tt(UA[:, :, tt * 128:(tt + 1) * 128], UA[:, :, tt * 128:(tt + 1) * 128],
                      ps2[:].rearrange("p (j t) -> p j t", j=4), ALU.mult)

            for step_ in range(24 + 2):
                if step_ < 24:
                    vaA(step_)
                if 0 <= step_ - 2 < 24:
                    vaC(step_ - 2)
                if step_ < 24:
                    vaB(step_)
            if "nobra" not in E["dbg"]:
                kb.dma(BRA.ap()[:, :, :], UA[:])
        chk(kb, "va", l)
        with kb.scope():
            Wq = kb.sb("Wq", [128, 8, 512], BF16)
            kb.dma(Wq[:], winv[:, :, C_Q:C_Q + 512], q="pool")
            Wkv = kb.sb("Wkv", [128, 8, 256], BF16)
            kb.dma(Wkv[:], winv[:, :, C_K:C_K + 256], q="pool")
            rope = kb.sb("rope", [64, 2, 2048])
            kb.dma(rope[:], E["rope_in"].ap()[:, :, :])
            sqs = [kb.sb("q_sq%d" % i, [64, 512]) for i in range(2)]
            rss = [kb.sb("q_rs%d" % i, [64, 512]) for i in range(2)]
            qns = [kb.sb("q_qn%d" % i, [64, 512]) for i in range(2)]
            qbs = [kb.sb("q_qb%d" % i, [64, 512], BF16) for i in range(2)]
            t1s = [kb.sb("q_t1%d" % i, [64, 512]) for i in range(2)]
            t2s = [kb.sb("q_t2%d" % i, [64, 512]) for i in range(2)]
            qrs = [kb.sb("q_qr%d" % i, [64, 512]) for i in range(2)]
            kos = [kb.sb("q_ko%d" % i, [128, 4, 128]) for i in range(2)]
            vos = [kb.sb("q_vo%d" % i, [128, 128]) for i in range(2)]
            ones64 = E["ones64"]; rm_b = E["rm_b"]
            iters = [(g, hh) for g in range(NG) for hh in range(10)]
            stt_ = {}

            def stA(n):
                g, hh = iters[n]
                isk = hh >= 8
                h = hh - 8 if isk else hh
                Wt = Wkv if isk else Wq
                tok = slice(g * 512, (g + 1) * 512)
                i2 = n % 2
                ps = PS(0, 3)
                for kc in range(8):
                    kb.mm(ps[0:64, :], Wt[:, kc, h * 64:(h + 1) * 64], HT[:, kc, tok], start=(kc == 0), stop=(kc == 7), inc=(kc == 7))
                kb.act(qrs[i2][:], ps[0:64, :], AF.Copy)
                kb.act(sqs[i2][:], ps[0:64, :], AF.Square)
                stt_[n] = dict(g=g, h=h, isk=isk, tok=tok, i2=i2, latent=(g >= 2), done=False)

            def stB1(n):
                d_ = stt_[n]
                g, h, isk, tok, i2, latent = d_["g"], d_["h"], d_["isk"], d_["tok"], d_["i2"], d_["latent"]
                gcol = pp[0:64, (O_KN if isk else O_QN):(O_KN if isk else O_QN) + 1]
                sq, rs, qn, qb, qr = sqs[i2], rss[i2], qns[i2], qbs[i2], qrs[i2]
                ps2 = PS(3, 6)
                kb.mm(ps2[0:64, :], ones64, sq[:])
                kb.rsqrt(rs[:], ps2[0:64, :], 1e-6)
                dst = (KT[:, h, tok] if isk else QT[:, h, tok])
                if (not latent) and (not isk):
                    kb.stt(dst, qr[:], gcol, rs[:], ALU.mult, ALU.mult)
                    d_["done"] = True
                    return
                kb.stt(qn[:], qr[:], gcol, rs[:], ALU.mult, ALU.mult)
                if not latent:
                    kb.cp(dst, qn[:], eng="pool")
                else:
                    kb.cp(qb[:], qn[:], eng="pool")

            def stB2(n):
                d_ = stt_[n]
                if d_["done"]:
                    return
                g, h, isk, tok, i2, latent = d_["g"], d_["h"], d_["isk"], d_["tok"], d_["i2"], d_["latent"]
                qn, qb, t1, t2 = qns[i2], qbs[i2], t1s[i2], t2s[i2]
                dst = (KT[:, h, tok] if isk else QT[:, h, tok])
                if not latent:
                    ko = kos[g % 2]
                    ps3 = PS(6, 8)
                    for tt in range(4):
                        kb.tr(ps3[:, tt * 64:(tt + 1) * 64], qn[:, tt * 128:(tt + 1) * 128], ident[0:64, 0:64])
                    kb.cp(ko[:, :, h * 64:(h + 1) * 64], ps3[:, 0:256].rearrange("p (t d) -> p t d", t=4), eng="act")
                    if h == 1:
                        kb.dma(E["nk_out"].ap()[l, g * 512:(g + 1) * 512, :].rearrange("(t p) c -> p t c", p=128), ko[:])
                    return
                pos = slice((g - 2) * 512, (g - 1) * 512)
                ps3 = PS(6, 8)
                kb.mm(ps3[0:64, :], rm_b[:], qb[:])
                kb.tt(t1[:], qn[:], rope[:, 0, pos], ALU.mult, eng="pool")
                kb.tt(t2[:], ps3[0:64, :], rope[:, 1, pos], ALU.mult)
                kb.tt(dst, t1[:], t2[:], ALU.add, eng="pool")

            def vproj(g):
                latent = g >= 2
                for t4 in range(4):
                    tt = g * 4 + t4
                    ps = PS(0, 3)
                    for kc in range(8):
                        kb.mm(ps[:, 0:128], HT[:, kc, tt * 128:(tt + 1) * 128], Wkv[:, kc, 128:256], start=(kc == 0), stop=(kc == 7), inc=(kc == 7))
                    vo = vos[t4 % 2]
                    kb.cp(vo[:], ps[:, 0:128], eng="act")
                    kb.cp(VS[:, tt, :, 0:64], vo[:].rearrange("p (k d) -> p k d", k=2), eng="pool")
                    if not latent:
                        kb.dma(E["nv_out"].ap()[l, tt * 128:(tt + 1) * 128, :], vo[:])

            NI = len(iters)
            for step_ in range(NI + 2):
                if step_ < NI:
                    stA(step_)
                if 0 <= step_ - 1 < NI:
                    stB1(step_ - 1)
                if 0 <= step_ - 2 < NI:
                    stB2(step_ - 2)
                    g_, hh_ = iters[step_ - 2]
                    if hh_ == 9:
                        vproj(g_)
            ckt = kb.sb("ckt", [128, 4, 128]); cvt = kb.sb("cvt", [128, 4, 128])
            kb.dma(ckt[:], E["ck_in"].ap()[l].rearrange("(t p) c -> p t c", p=128))
            kb.dma(cvt[:], E["cvv_in"].ap()[l].rearrange("(t p) c -> p t c", p=128))
            for t4 in range(4):
                for kv in range(2):
                    ps = PS(0, 3)
                    kb.tr(ps[0:64, 0:128], ckt[:, t4, kv * 64:(kv + 1) * 64], ident)
                    kb.cp(KT[:, kv, T + t4 * 128:T + (t4 + 1) * 128], ps[0:64, 0:128], eng=EV())
                kb.cp(VS[:, 24 + t4, :, 0:64], cvt[:, t4, :].rearrange("p (k d) -> p k d", k=2), eng="pool")
        chk(kb, "q", l)
        with kb.scope():
            Ws = [kb.sb("Wr%d" % i, [128, 8, 512], BF16) for i in range(2)]
            stg = [kb.sb("rstg%d" % i, [128, 512]) for i in range(3)]
            k = 0
            for wb in range(4):
                ncol = 512 if wb < 3 else 384
                W = Ws[wb % 2]
                kb.dma(W[:, :, 0:ncol], winv[:, :, C_R + wb * 512:C_R + wb * 512 + ncol], q="pool")
                for bb in range(ncol // 128):
                    b = wb * 4 + bb
                    for g in range(NG):
                        ps = PS()
                        for kc in range(8):
                            kb.mm(ps[:], W[:, kc, bb * 128:(bb + 1) * 128], HT[:, kc, g * 512:(g + 1) * 512],
                                  start=(kc == 0), stop=(kc == 7), inc=(kc == 7))
                        s = stg[k % 3]
                        k += 1
                        kb.cp(s[:], ps[:], eng=EV())
                        kb.dma(RKV.ap()[:, b, g * 512:(g + 1) * 512], s[:])
        chk(kb, "rkv", l)
        with kb.scope():
            pts = [kb.sb("pt%d" % i, [128, 512], BF16) for i in range(3)]
            ons = [kb.sb("on%d" % i, [128, 512]) for i in range(2)]
            rcs = [kb.sb("rc%d" % i, [128, 512]) for i in range(2)]
            obs = [kb.sb("ob%d" % i, [64, 512], BF16) for i in range(2)]
            seqs = [(s * 256, 256, [(s * 256 + i * 128, s * 2 + i) for i in range(2)]) for s in range(4)]
            seqs.append((1024, 2048, [(T + i * 128, 24 + i) for i in range(4)] + [(1024 + i * 128, 8 + i) for i in range(16)]))
            n = 0
            apend = [None]
            for (t0, L, keys) in seqs:
                for kv in range(2):
                    for qt in range(L // 128):
                        q0 = t0 + qt * 128
                        po = P[6 + n % 2]
                        nk_ = len(keys)
                        pss = [None] * nk_

                        def score(i):
                            ps = PS(0, 4)
                            kb.mm(ps[:], KT[:, kv, keys[i][0]:keys[i][0] + 128], QT[:, 4 * kv:4 * kv + 4, q0:q0 + 128])
                            pss[i] = ps

                        score(0)
                        for i, (kc0, vt) in enumerate(keys):
                            if i + 1 < nk_:
                                score(i + 1)
                            if i == min(2, nk_ - 1) and apend[0] is not None:
                                apend[0]()
                                apend[0] = None
                            pt = pts[i % 3]
                            kb.act(pt[:], pss[i][:], AF.Exp, scale=0.125)
                            kb.mm(po[0:65, :], VS[:, vt, kv, :], pt[:], start=(i == 0), stop=(i == nk_ - 1))
                        on = ons[n % 2]; rc = rcs[n % 2]; ob = obs[n % 2]
                        kb.cp(on[0:65, :], po[0:65, :], eng="act")
                        kb.recip(rc[64:65, :], on[64:65, :])
                        pb = P[4 + n % 2]

                        def fin(on=on, rc=rc, ob=ob, pb=pb, kv=kv, q0=q0):
                            kb.mm(pb[0:64, :], E["ones_row"][64:65, 0:64], rc[64:65, :], tile_position=(64, 0))
                            kb.tt(ob[:], on[0:64, :], pb[0:64, :], ALU.mult)
                            kb.dma(BRB.ap()[:, 4 * kv:4 * kv + 4, q0:q0 + 128], ob[:].rearrange("p (g t) -> p g t", g=4))

                        apend[0] = fin
                        n += 1
            if apend[0] is not None:
                apend[0]()
                apend[0] = None
    chk(kb, "att", l)
    if "noC" not in E["dbg"]:
        rwkv(kb, l, E)
    chk(kb, "rw", l)
    merge_ffn(kb, l, E)


def mmg(kb, terms):
    order = sorted(range(len(terms)), key=lambda i: (terms[i][4][0], i))
    first, last = {}, {}
    for n, i in enumerate(order):
        r = terms[i][0]
        first.setdefault(r, n)
        last[r] = n
    for n, i in enumerate(order):
        r, out, lhsT, rhs, tp = terms[i]
        kb.mm(out, lhsT, rhs, start=(first[r] == n), stop=(last[r] == n), tile_position=tp)


def rwkv(kb, l, E):
    P = E["P"]; pp = E["pp"][l]; RKV = E["RKV"]; BRC = E["BRC"]; BONS = E["BONS"]; YS = E["YS"]
    ident = E["ident"]; BDm = E["BDm"]; BD64 = E["BD64"]
    NP_ = 256
    fk = [0]

    def FPS():
        fk[0] += 1
        return P[6 + fk[0] % 2]

    def bc4(off, n=NP_):
        return pp[:, off:off + 4].unsqueeze(2).to_broadcast([128, 4, n])

    with kb.scope():
        A = lambda n, s, dt=F32: kb.sb(n, s, dt)
        XR = A("XR", [128, 14, NP_ + 2])
        R = A("rR", [128, 4, NP_]); K = A("rK", [128, 4, NP_]); V = A("rV", [128, 4, NP_]); LO = A("rLO", [128, NP_])
        TW = A("rTW", [128, NP_]); LW = A("rLW", [128, 4, NP_]); AA = A("rAA", [128, 4, NP_])
        KK = A("rKK", [128, 4, NP_])
        CL = A("rCL", [128, 4, NP_]); KM = A("rKM", [128, 4, NP_]); BB = A("rBB", [128, 4, NP_])
        Bh = A("rBh", [128, 4, NP_]); Kh = A("rKh", [128, 4, NP_])
        ARs = [A("rAR%d" % i, [128, 4, 4, 128]) for i in range(2)]
        Bts = [A("rBt%d" % i, [128, 4, NP_]) for i in range(2)]
        Kts = [A("rKt%d" % i, [128, 4, NP_]) for i in range(2)]
        BhTs = [A("rBhT%d" % i, [128, 2, 512]) for i in range(2)]
        KhTs = [A("rKhT%d" % i, [128, 2, 512]) for i in range(2)]
        VTs = [A("rVT%d" % i, [128, 2, 512]) for i in range(2)]
        BONs = [A("rBON%d" % i, [128, 4, NP_]) for i in range(2)]
        GAMs = [A("rGAM%d" % i, [128, 4, 4]) for i in range(2)]
        YPs = [A("rYP%d" % i, [128, 4, NP_]) for i in range(2)]
        F1 = A("fT1", [128, 4, NP_]); F2 = A("fT2", [128, 4, NP_]); FE = A("fEE", [128, 4, NP_])
        WA = [A("rWA%d" % d, [128, 512]) for d in range(2)]
        for d in range(2):
            kb.dma(WA[d][:], E["wa2_in"].ap()[l, d])
        G2 = A("rG2", [128, 512], BF16)
        kb.dma(G2[:], E["g2_in"].ap()[l], q="pool")
        NM = A("cNM", [128, 8, 128]); KMt = A("cKMt", [128, 8, 128])
        Nb = [A("cN%d" % i, [128, 8, 64]) for i in range(2)]
        NTb = [A("cNT%d" % i, [128, 8, 64]) for i in range(2)]
        TTb = [A("cTT%d" % i, [128, 8, 64]) for i in range(2)]
        RHS = A("cRHS", [128, 512]); U = A("cU", [128, 512]); RHS0 = A("cRHS0", [128, 512])
        ST = [A("cST%d" % i, [128, 4, 64]) for i in range(2)]
        SGD = A("rSGD", [128, NP_], BF16); GD = A("rGD", [128, NP_]); OC = A("rOC", [128, 4, NP_], BF16)
        v64 = lambda ap: ap.rearrange("p a (c t) -> p a c t", t=64)

        items = []
        for (t0, L, latent, sidx) in [(s * 256, 256, False, s) for s in range(4)] + [(1024, 2048, True, None)]:
            npc = L // NP_
            for d in range(2):
                order = list(range(npc)) if d == 0 else list(range(npc - 1, -1, -1))
                for j, pc in enumerate(order):
                    items.append(dict(t0=t0, latent=latent, sidx=sidx, d=d, pc=pc, npc=npc,
                                      first=(j == 0), last=(j == npc - 1)))
        sic = [0]

        def front(it, b):
            d, pc, npc = it["d"], it["pc"], it["npc"]
            p0 = it["t0"] + pc * NP_
            AR, Bt, Kt, BhT, KhT, VT, BON, GAM, YP = ARs[b], Bts[b], Kts[b], BhTs[b], KhTs[b], VTs[b], BONs[b], GAMs[b], YPs[b]
            lo_h = 1 if pc == 0 else 0
            hi_h = NP_ + 1 if pc == npc - 1 else NP_ + 2
            if pc == 0:
                kb.memset(XR[:, :, 0:1], 0.0)
            if pc == npc - 1:
                kb.memset(XR[:, :, NP_ + 1:NP_ + 2], 0.0)
            kb.dma(XR[:, :, lo_h:hi_h], RKV.ap()[:, 0:14, p0 - 1 + lo_h:p0 - 1 + hi_h])
            yield
            sh = 0 if d == 0 else 2
            dsts = [R[:, i, :] for i in range(4)] + [K[:, i, :] for i in range(4)] + [V[:, i, :] for i in range(4)] + [LO[:]]
            for blk in range(13):
                sbk = blk if blk < 12 else 12 + d
                mu = pp[:, O_MU + d * 13 + blk:O_MU + d * 13 + blk + 1]
                omu = pp[:, O_OMU + d * 13 + blk:O_OMU + d * 13 + blk + 1]
                tmp = F1[:, blk % 4, :]
                kb.ts(tmp, XR[:, sbk, sh:sh + NP_], mu, ALU.mult, eng="pool")
                kb.stt(dsts[blk], XR[:, sbk, 1:NP_ + 1], omu, tmp, ALU.mult, ALU.add)
                if blk % 3 == 2:
                    yield
            kb.act(TW[0:64, :], LO[0:64, :], AF.Tanh)
            for pr in range(4):
                ps = FPS()
                kb.mm(ps[:, 0:NP_], WA[d][0:64, pr * 128:(pr + 1) * 128], TW[0:64, :])
                kb.act(LW[:, pr, :], ps[:, 0:NP_], AF.Sigmoid, bias=pp[:, O_W0 + d * 4 + pr:O_W0 + d * 4 + pr + 1])
                ps = FPS()
                kb.mm(ps[:, 0:NP_], WA[d][64:128, pr * 128:(pr + 1) * 128], LO[64:128, :], tile_position=(64, 0))
                kb.act(AA[:, pr, :], ps[:, 0:NP_], AF.Sigmoid, bias=pp[:, O_A0 + d * 4 + pr:O_A0 + d * 4 + pr + 1])
                yield
            kb.ts(LW[:], LW[:], -0.6065306597126334, ALU.mult, eng="pool")
            for pr in range(4):
                kb.I("dve", lambda e, pr=pr: e.tensor_tensor_scan(CL[:, pr, :], E["rst"][:, 0:NP_], LW[:, pr, :], 0.0, ALU.mult, ALU.add),
                     w=[CL[:]], r=[LW[:], E["cst"][:]])
            yield
            if d == 0:
                tot = v64(CL[:])[:, :, :, 63:64]
            else:
                totf = v64(CL[:])[:, :, :, 63:64].to_broadcast([128, 4, 4, 64])
                kb.tt(v64(F2[:]), totf, v64(CL[:]), ALU.subtract)
                kb.tt(CL[:], F2[:], LW[:], ALU.add)
                tot = v64(CL[:])[:, :, :, 0:1]
            kb.act(GAM[:].unsqueeze(3), tot, AF.Exp)
            yield
            kb.tt(KK[:], K[:], bc4(O_KK), ALU.mult, eng="pool")
            kb.tt(F1[:], KK[:], KK[:], ALU.mult, eng="pool")
            for hf in range(2):
                ps = FPS()
                kb.mm(ps[:], BDm, F1[:, 2 * hf:2 * hf + 2, :])
                kb.rsqrt(F2[:, 2 * hf:2 * hf + 2, :], ps[:].rearrange("p (a t) -> p a t", a=2), 1e-24)
            yield
            kb.tt(KK[:], KK[:], F2[:], ALU.mult)
            kb.tt(F1[:], AA[:], bc4(O_KA), ALU.mult, eng="pool")
            kb.tt(F1[:], F1[:], bc4(O_OMKA), ALU.add, eng="pool")
            kb.tt(KM[:], K[:], F1[:], ALU.mult)
            kb.tt(BB[:], KK[:], AA[:], ALU.mult, eng="pool")
            yield
            kb.tt(F1[:], R[:], KM[:], ALU.mult, eng="pool")
            kb.tt(F1[:], F1[:], bc4(O_RK), ALU.mult, eng="pool")
            for hf in range(2):
                ps = FPS()
                kb.mm(ps[:], BDm, F1[:, 2 * hf:2 * hf + 2, :])
                kb.tt(BON[:, 2 * hf:2 * hf + 2, :], ps[:].rearrange("p (a t) -> p a t", a=2), V[:, 2 * hf:2 * hf + 2, :], ALU.mult)
            if d == 0:
                kb.dma(BONS.ap()[:, :, p0:p0 + NP_], BON[:])
            else:
                kb.dma(F2[:], BONS.ap()[:, :, p0:p0 + NP_])
                kb.tt(BON[:], BON[:], F2[:], ALU.add)
            yield
            kb.tt(F1[:], CL[:], LW[:], ALU.subtract, eng="pool")
            kb.act(FE[:], F1[:], AF.Exp)
            kb.stt(AR[:, :, :, 0:64], v64(KK[:]), -1.0, v64(FE[:]), ALU.mult, ALU.mult)
            yield
            kb.act(FE[:], CL[:], AF.Exp)
            kb.tt(AR[:, :, :, 64:128], v64(R[:]), v64(FE[:]), ALU.mult)
            yield
            kb.act(FE[:], CL[:], AF.Exp, scale=-1.0)
            kb.tt(Bt[:], BB[:], FE[:], ALU.mult)
            kb.tt(Kt[:], KM[:], FE[:], ALU.mult, eng="pool")
            yield
            kb.tt(v64(F1[:]), tot.to_broadcast([128, 4, 4, 64]), v64(CL[:]), ALU.subtract)
            kb.act(FE[:], F1[:], AF.Exp)
            kb.tt(Bh[:], BB[:], FE[:], ALU.mult)
            kb.tt(Kh[:], KM[:], FE[:], ALU.mult, eng="pool")
            yield
            for (src, dst) in ((Bh, BhT), (Kh, KhT), (V, VT)):
                for tb in range(2):
                    ps = FPS()
                    for pr in range(4):
                        kb.tr(ps[:, pr * 128:(pr + 1) * 128], src[:, pr, tb * 128:(tb + 1) * 128], ident)
                    kb.cp(dst[:, tb, :], ps[:], eng=("act" if tb else "dve"))
                    yield

        def pairs(it, b, step):
            d, pc = it["d"], it["pc"]
            AR, Bt, Kt, BhT, KhT, VT, GAM, YP = ARs[b], Bts[b], Kts[b], BhTs[b], KhTs[b], VTs[b], GAMs[b], YPs[b]
            if d == 1:
                p0_ = it["t0"] + pc * NP_
                kb.dma(YP[:], YS.ap()[:, :, p0_:p0_ + NP_])
            if it["first"]:
                sic[0] = 0
                if it["latent"]:
                    kb.dma(ST[0][:], E["st_in"].ap()[l, d].rearrange("p (a v) -> p a v", a=4))
                else:
                    kb.memset(ST[0][:], 0.0)
            mask4 = E["maskF4"] if d == 0 else E["maskB4"]
            maskT8 = E["maskFT8"] if d == 0 else E["maskBT8"]
            for tb in (range(2) if d == 0 else range(1, -1, -1)):
                PA = [P[0], P[1]]; PB = [P[2], P[3]]; PC = [P[4], P[5]]
                for ci in range(2):
                    c = 2 * tb + ci; cb = ci * 64; cs = slice(c * 64, (c + 1) * 64)
                    for h in range(8):
                        par = h % 2; hb = par * 64; pr = h // 2
                        lastm = (h >= 6)
                        kb.mm(PA[par][cb:cb + 64, pr * 128:(pr + 1) * 128], Bt[hb:hb + 64, pr, cs],
                              AR[hb:hb + 64, pr, c, :], tile_position=(hb, cb), inc=lastm)
                        kb.mm(PB[par][cb:cb + 64, pr * 128:(pr + 1) * 128], Kt[hb:hb + 64, pr, cs],
                              AR[hb:hb + 64, pr, c, :], tile_position=(hb, cb), inc=lastm)
                        kb.mm(PC[par][cb:cb + 64, pr * 64:(pr + 1) * 64], AR[hb:hb + 64, pr, c, 0:64],
                              Bt[hb:hb + 64, pr, cs], tile_position=(hb, cb), inc=lastm)
                m4 = mask4.rearrange("p (h t) -> p h t", h=4)
                for par in range(2):
                    kb.tt(NM[:, par:8:2, :], PA[par][:].rearrange("p (h t) -> p h t", h=4), m4, ALU.mult)
                    kb.tt(KMt[:, par:8:2, :], PB[par][:].rearrange("p (h t) -> p h t", h=4), m4, ALU.mult)
                    kb.tt(NTb[0][:, par:8:2, :], PC[par][:, 0:256].rearrange("p (h t) -> p h t", h=4),
                          maskT8[:, 0:256].rearrange("p (h t) -> p h t", h=4), ALU.mult)
                kb.cp(Nb[0][:], NM[:, :, 0:64], eng="pool")
                kb.tt(TTb[0][:], Nb[0][:], E["ident8"].rearrange("p (h t) -> p h t", h=8), ALU.add, eng="pool")
                step()
                cur = 0
                for lev in range(5):
                    nx = 1 - cur
                    Nc, NTc, TTc = Nb[cur], NTb[cur], TTb[cur]
                    PNT, PN, PTt = P[0], P[1], P[2]
                    for ci in range(2):
                        cb = ci * 64
                        for h in range(8):
                            hs = slice(h * 64, (h + 1) * 64)
                            kb.mm(PNT[cb:cb + 64, hs], Nc[cb:cb + 64, h, :], NTc[cb:cb + 64, h, :], tile_position=(cb, cb), inc=(h == 7))
                            if lev < 4:
                                kb.mm(PN[cb:cb + 64, hs], NTc[cb:cb + 64, h, :], Nc[cb:cb + 64, h, :], tile_position=(cb, cb), inc=(h == 7))
                    kb.cp(NTb[nx][:], PNT[:].rearrange("p (h t) -> p h t", h=8), eng="act")
                    if lev < 4:
                        kb.cp(Nb[nx][:], PN[:].rearrange("p (h t) -> p h t", h=8), eng="dve")
                    for ci in range(2):
                        cb = ci * 64
                        for h in range(8):
                            hs = slice(h * 64, (h + 1) * 64)
                            kb.mm(PTt[cb:cb + 64, hs], NTb[nx][cb:cb + 64, h, :], TTc[cb:cb + 64, h, :], tile_position=(cb, cb), inc=(h == 7))
                    kb.tt(TTb[nx][:], TTc[:], PTt[:].rearrange("p (h t) -> p h t", h=8), ALU.add)
                    cur = nx
                    step()
                TT = TTb[cur]
                PX = P[5]
                for ci in range(2):
                    cb = ci * 64
                    for h in range(8):
                        hs = slice(h * 64, (h + 1) * 64)
                        kb.mm(PX[cb:cb + 64, hs], KMt[cb:cb + 64, h, 0:64], VT[cb:cb + 64, tb, hs], tile_position=(cb, cb), inc=(h == 7))
                kb.cp(RHS0[:], PX[:], eng="act")
                step()
                for ci in ((0, 1) if d == 0 else (1, 0)):
                    c = 2 * tb + ci; cb = ci * 64
                    Sc = ST[sic[0] % 2]; Sn = ST[(sic[0] + 1) % 2]
                    sic[0] += 1
                    PR, PU, PYa, PYb, PSn = P[3], P[4], P[5], P[0], P[1]
                    for h in (0, 2, 4, 6, 1, 3, 5, 7):
                        hb = (h % 2) * 64; pr = h // 2; hs = slice(h * 64, (h + 1) * 64)
                        kb.mm(PR[cb:cb + 64, hs], AR[hb:hb + 64, pr, c, 0:64], Sc[hb:hb + 64, pr, :], tile_position=(hb, cb), inc=(h >= 6))
                    kb.tt(RHS[cb:cb + 64, :], PR[cb:cb + 64, :], RHS0[cb:cb + 64, :], ALU.add)
                    step()
                    for h in range(8):
                        hs = slice(h * 64, (h + 1) * 64)
                        kb.mm(PU[cb:cb + 64, hs], TT[cb:cb + 64, h, :], RHS[cb:cb + 64, hs], tile_position=(cb, cb), inc=(h == 7))
                    kb.cp(U[cb:cb + 64, :], PU[cb:cb + 64, :], eng="act")
                    step()
                    for h in (0, 2, 4, 6, 1, 3, 5, 7):
                        hb = (h % 2) * 64; pr = h // 2
                        ys = slice(pr * 64, (pr + 1) * 64)
                        kb.mm(PYa[hb:hb + 64, ys], Sc[hb:hb + 64, pr, :], AR[hb:hb + 64, pr, c, 64:128], tile_position=(hb, hb), inc=(h >= 6))
                    for h in range(8):
                        hb = (h % 2) * 64; pr = h // 2; hs = slice(h * 64, (h + 1) * 64)
                        ys = slice(pr * 64, (pr + 1) * 64)
                        kb.mm(PSn[hb:hb + 64, ys], BhT[cb:cb + 64, tb, hs], U[cb:cb + 64, hs],
                              start=True, stop=False, tile_position=(cb, hb), inc=False)
                        kb.mm(PSn[hb:hb + 64, ys], KhT[cb:cb + 64, tb, hs], VT[cb:cb + 64, tb, hs],
                              start=False, stop=True, tile_position=(cb, hb), inc=(h >= 6))
                    kb.tt(Sn[:], Sc[:], GAM[:, :, c:c + 1].to_broadcast([128, 4, 64]), ALU.mult, eng="pool")
                    kb.tt(Sn[:], Sn[:], PSn[:, 0:256].rearrange("p (a t) -> p a t", a=4), ALU.add)
                    for h in range(8):
                        hb = (h % 2) * 64; pr = h // 2; hs = slice(h * 64, (h + 1) * 64)
                        ys = slice(pr * 64, (pr + 1) * 64)
                        kb.mm(PYb[hb:hb + 64, ys], U[cb:cb + 64, hs], NM[cb:cb + 64, h, 64:128],
                              start=True, stop=False, tile_position=(cb, hb), inc=False)
                        kb.mm(PYb[hb:hb + 64, ys], VT[cb:cb + 64, tb, hs], KMt[cb:cb + 64, h, 64:128],
                              start=False, stop=True, tile_position=(cb, hb), inc=(h >= 6))
                    ytk = slice(c * 64, (c + 1) * 64)
                    pya = PYa[:, 0:256].rearrange("p (a t) -> p a t", a=4)
                    pyb = PYb[:, 0:256].rearrange("p (a t) -> p a t", a=4)
                    if d == 0:
                        kb.cp(YP[:, :, ytk], pya, eng="act")
                    else:
                        kb.tt(YP[:, :, ytk], YP[:, :, ytk], pya, ALU.add)
                    kb.tt(YP[:, :, ytk], YP[:, :, ytk], pyb, ALU.add)
                    step()
            if it["last"] and not it["latent"]:
                kb.dma(E["ns_out"].ap()[it["sidx"], l, d].rearrange("p (a v) -> p a v", a=4), ST[sic[0] % 2][:])

        def post(it, b):
            d, pc = it["d"], it["pc"]
            p0 = it["t0"] + pc * NP_
            YP, BON = YPs[b], BONs[b]
            if d == 0:
                kb.dma(YS.ap()[:, :, p0:p0 + NP_], YP[:])
                return
            for hf in range(2):
                ps = FPS()
                kb.mm(ps[:], BD64, YP[:, 2 * hf:2 * hf + 2, :])
                kb.tt(F1[:, 2 * hf:2 * hf + 2, :], YP[:, 2 * hf:2 * hf + 2, :], ps[:].rearrange("p (a t) -> p a t", a=2), ALU.subtract)
            kb.tt(F2[:], F1[:], F1[:], ALU.mult, eng="pool")
            for hf in range(2):
                ps = FPS()
                kb.mm(ps[:], BD64, F2[:, 2 * hf:2 * hf + 2, :])
                kb.rsqrt(FE[:, 2 * hf:2 * hf + 2, :], ps[:].rearrange("p (a t) -> p a t", a=2), 64e-5)
            kb.tt(F1[:], F1[:], FE[:], ALU.mult)
            kb.tt(F1[:], F1[:], bc4(O_LXG), ALU.mult, eng="pool")
            kb.tt(F1[:], F1[:], bc4(O_LXB), ALU.add, eng="pool")
            kb.tt(F1[:], F1[:], BON[:], ALU.add)
            kb.dma(GD[:], RKV.ap()[:, 14, p0:p0 + NP_])
            kb.act(SGD[:], GD[:], AF.Sigmoid)
            for pr in range(4):
                ps = FPS()
                kb.mm(ps[:, 0:NP_], G2[:, pr * 128:(pr + 1) * 128], SGD[:])
                kb.tt(OC[:, pr, :], F1[:, pr, :], ps[:, 0:NP_], ALU.mult)
            kb.dma(BRC.ap()[:, :, p0:p0 + NP_], OC[:])

        g0 = front(items[0], 0)
        for _ in g0:
            pass
        for i, it in enumerate(items):
            b = i % 2
            nxt = front(items[i + 1], 1 - b) if i + 1 < len(items) else None

            def step(nxt=nxt):
                if nxt is not None:
                    next(nxt, None)

            pairs(it, b, step)
            if nxt is not None:
                for _ in nxt:
                    pass
            post(it, b)


def merge_ffn(kb, l, E):
    P = E["P"]; pp = E["pp"][l]; modT = E["modT"][l]
    XTi = E["XT"][l % 2]; XTo = E["XT"][(l + 1) % 2]
    XM = E["XM"]
    BRA = E["BRA"]; BRB = E["BRB"]; BRC = E["BRC"]; onesD = E["onesD"]
    winv = E["w_in"].ap()[l].rearrange("(kc p) n -> p kc n", p=128)
    pk = [0]

    def PS(lo=0, hi=8):
        i = lo + pk[0] % (hi - lo)
        pk[0] += 1
        return P[i]

    def lnorm(Y, sqs, sc3, og, ob, n):
        pm = PS(0, 2); pq = PS(2, 4)
        for c in range(8):
            kb.mm(pm[:, 0:n], onesD, Y[:, c, :], start=(c == 0), stop=(c == 7), inc=(c == 7))
        for c in range(8):
            sq = sqs[c % 2]
            kb.act(sq[:, 0:n], Y[:, c, :], AF.Square)
            kb.mm(pq[:, 0:n], onesD, sq[:, 0:n], start=(c == 0), stop=(c == 7))
        mean = sc3[:, 0, 0:n]; rstd = sc3[:, 1, 0:n]; tmp = sc3[:, 2, 0:n]
        kb.cp(mean, pm[:, 0:n], eng="act")
        kb.tt(tmp, mean, mean, ALU.mult, eng="pool")
        kb.tt(tmp, pq[:, 0:n], tmp, ALU.subtract)
        kb.rsqrt(rstd, tmp, 1e-5)
        for c in range(8):
            e = "dve" if c % 2 else "pool"
            kb.tt(Y[:, c, :], Y[:, c, :], mean, ALU.subtract, eng=e)
            kb.tt(Y[:, c, :], Y[:, c, :], rstd, ALU.mult, eng=e)
            kb.ts(Y[:, c, :], Y[:, c, :], pp[:, og + c:og + c + 1], ALU.mult, pp[:, ob + c:ob + c + 1], ALU.add, eng=e)

    noC = "noC" in E["dbg"]
    with kb.scope():
        WG = kb.sb("mWG", [128, 8, 3072], BF16)
        WB = kb.sb("mWB", [128, 8, D], BF16)
        wbrv = E["w_br"].ap()[l]
        WBB = kb.sb("mWBB", [64, 8, D], BF16)
        WO = kb.sb("mWO", [128, 8, D], BF16)
        wov_ = E["w_out"].ap()[l].rearrange("(kc p) n -> p kc n", p=128)
        for hf_ in range(2):
            hsl = slice(hf_ * 512, (hf_ + 1) * 512)
            for j in range(3):
                i = j * 2 + hf_
                kb.dma((WG[:, :, i * 512:(i + 1) * 512], i), winv[:, :, C_GL + i * 512:C_GL + (i + 1) * 512], q="pool")
            kb.dma((WB[:, 0:4, hsl], hf_), wbrv[0].rearrange("(cc p) d -> p cc d", p=128)[:, :, hsl], q="pool")
            kb.dma((WB[:, 4:8, hsl], 2 + hf_), wbrv[2].rearrange("(cc p) d -> p cc d", p=128)[:, :, hsl], q="pool")
            kb.dma((WBB[:, :, hsl], hf_), wbrv[1].rearrange("(h p) d -> p h d", p=64)[:, :, hsl], q="pool")
        for hf_ in range(2):
            hsl = slice(hf_ * 512, (hf_ + 1) * 512)
            kb.dma((WO[:, :, hsl], hf_), wov_[:, :, hsl], q="pool")
        XLs = [kb.sb("mXL%d" % i, [128, 8, 512]) for i in range(2)]
        HT = kb.sb("mHT", [128, 8, 512], BF16)
        OA = kb.sb("mOA", [128, 4, 512], BF16); OB = kb.sb("mOB", [64, 8, 512], BF16); OCt = kb.sb("mOC", [128, 4, 512], BF16)
        MI = kb.sb("mMI", [128, 8, 512], BF16)
        sqs = [kb.sb("msq%d" % i, [128, 512]) for i in range(2)]
        sc3 = kb.sb("msc3", [128, 3, 512])
        gts = [kb.sb("mgt%d" % i, [128, 512]) for i in range(2)]
        acc = kb.sb("macc", [128, 512])

        def prep(g):
            XL = XLs[g % 2]
            tok = slice(g * 512, (g + 1) * 512)
            kb.dma(XL[:], XTi.ap()[:, :, tok])
            kb.dma(OA[:], BRA.ap()[:, :, tok]); kb.dma(OB[:], BRB.ap()[:, :, tok])
            if not noC:
                kb.dma(OCt[:], BRC.ap()[:, :, tok])
            sc = mod_cols(modT, 8, g); sh = mod_cols(modT, 0, g)
            for c in range(8):
                kb.ts(HT[:, c, :], XL[:, c, :], sc[c], ALU.mult, sh[c], ALU.add, eng=("dve" if c % 2 else "pool"))

        def mix(g):
            for dc in range(8):
                dcs = slice(dc * 128, (dc + 1) * 128)
                for j in range(3):
                    if j == 2 and noC:
                        continue
                    pg = PS(0, 3); pb = PS(3, 6)
                    for kc in range(8):
                        kb.mm(pg[:], (WG[:, kc, j * D + dc * 128:j * D + (dc + 1) * 128], j * 2 + dc // 4), HT[:, kc, :], start=(kc == 0), stop=(kc == 7), inc=(kc == 7))
                    if j != 1:
                        src = OA if j == 0 else OCt
                        for cc in range(4):
                            kb.mm(pb[:], (WB[:, (j // 2) * 4 + cc, dcs], (j // 2) * 2 + dc // 4), src[:, cc, :], start=(cc == 0), stop=(cc == 3), inc=(cc == 3))
                    else:
                        for h in range(8):
                            kb.mm(pb[:], (WBB[:, h, dcs], dc // 4), OB[:, h, :], start=(h == 0), stop=(h == 7), inc=(h == 7))
                    gt = gts[j % 2]
                    kb.act(gt[:], pg[:], AF.Sigmoid)
                    if j == 0:
                        kb.tt(acc[:], gt[:], pb[:], ALU.mult)
                    else:
                        kb.tt(gt[:], gt[:], pb[:], ALU.mult)
                        kb.tt((MI[:, dc, :] if (j == 2 or (noC and j == 1)) else acc[:]), acc[:], gt[:], ALU.add, eng="pool")

        def wout_ln(g):
            XL = XLs[g % 2]
            g1 = mod_cols(modT, 16, g)
            for dc in range(8):
                ps = PS(6, 8)
                for kc in range(8):
                    kb.mm(ps[:], (WO[:, kc, dc * 128:(dc + 1) * 128], dc // 4), MI[:, kc, :], start=(kc == 0), stop=(kc == 7), inc=(kc == 7))
                kb.ts(XL[:, dc, :], XL[:, dc, :], ALPHA, ALU.mult, eng="pool")
                kb.stt(XL[:, dc, :], ps[:], g1[dc], XL[:, dc, :], ALU.mult, ALU.add)
            lnorm(XL, sqs, sc3, O_LN1G, O_LN1B, 512)
            kb.dma(XM.ap()[:, :, g * 512:(g + 1) * 512], XL[:])

        prep(0)
        for g in range(NG):
            mix(g)
            if g + 1 < NG:
                prep(g + 1)
            wout_ln(g)
    chk(kb, "mrg", l)
    with kb.scope():
        WU = kb.sb("fWU", [128, 8, 4096], BF16)
        upv = E["w_up"].ap()[l].rearrange("(kc p) n -> p kc n", p=128)
        WD = kb.sb("fWD", [128, 32, D], BF16)
        dnv = E["w_dn"].ap()[l].rearrange("(fc p) n -> p fc n", p=128)
        for i in range(8):
            kb.dma((WU[:, :, i * 512:(i + 1) * 512], i), upv[:, :, i * 512:(i + 1) * 512], q="pool")
        for i in range(8):
            kb.dma((WD[:, i * 4:(i + 1) * 4, :], i), dnv[:, i * 4:(i + 1) * 4, :], q="pool")
        NF = 256
        NGF = T // NF
        X1s = [kb.sb("fX1%d" % i, [128, 8, NF]) for i in range(2)]
        H2 = kb.sb("fH2", [128, 8, NF], BF16)
        ACTt = kb.sb("fACT", [128, 32, NF], BF16)
        sqs = [kb.sb("fsq%d" % i, [128, 512]) for i in range(2)]
        sc3 = kb.sb("fsc3", [128, 3, 512])
        rl = [kb.sb("frl%d" % i, [128, NF]) for i in range(2)]

        def prep_up(gg):
            g = (gg * NF) // 512
            X1 = X1s[gg % 2]
            kb.dma(X1[:], XM.ap()[:, :, gg * NF:(gg + 1) * NF])
            sc2 = mod_cols(modT, 32, g); sh2 = mod_cols(modT, 24, g)
            for c in range(8):
                kb.ts(H2[:, c, :], X1[:, c, :], sc2[c], ALU.mult, sh2[c], ALU.add, eng=("dve" if c % 2 else "pool"))
            for fc in range(32):
                ps = PS(0, 4)
                for kc in range(8):
                    kb.mm(ps[:, 0:NF], (WU[:, kc, fc * 128:(fc + 1) * 128], fc // 4), H2[:, kc, :], start=(kc == 0), stop=(kc == 7), inc=(kc == 7))
                r_ = rl[fc % 2]
                kb.act(r_[:], ps[:, 0:NF], AF.Relu)
                kb.tt(ACTt[:, fc, :], r_[:], r_[:], ALU.mult, eng=("dve" if fc % 2 else "pool"))

        def down(gg):
            g = (gg * NF) // 512
            X1 = X1s[gg % 2]
            g2 = mod_cols(modT, 40, g)
            for dc in range(8):
                ps = PS(4, 8)
                for fc in range(32):
                    kb.mm(ps[:, 0:NF], (WD[:, fc, dc * 128:(dc + 1) * 128], fc // 4), ACTt[:, fc, :], start=(fc == 0), stop=(fc == 31), inc=(fc == 31))
                kb.ts(X1[:, dc, :], X1[:, dc, :], ALPHA, ALU.mult, eng="pool")
                kb.stt(X1[:, dc, :], ps[:, 0:NF], g2[dc], X1[:, dc, :], ALU.mult, ALU.add)

        def ln_out(gg):
            X1 = X1s[gg % 2]
            lnorm(X1, sqs, sc3, O_LN2G, O_LN2B, NF)
            kb.dma(XTo.ap()[:, :, gg * NF:(gg + 1) * NF], X1[:])

        prep_up(0)
        for gg in range(NGF):
            down(gg)
            if gg + 1 < NGF:
                prep_up(gg + 1)
            ln_out(gg)


def _consts():
    c = np.zeros((128, 9, 512), np.float32)
    r = np.arange(128)
    c[:, 0, 0:128] = np.eye(128)
    bd = (r[:, None] // 64 == r[None, :] // 64).astype(np.float32)
    c[:, 0, 128:256] = bd
    c[:, 0, 256:384] = bd / 64.0
    c[0:64, 0, 384:448] = 1.0 / 64.0
    s = (r % 64)[:, None]
    col = np.arange(128)[None, :]
    mf = np.where(col < 64, s < col, s <= col - 64).astype(np.float32)
    mb = np.where(col < 64, s > col, s >= col - 64).astype(np.float32)
    c[:, 1, :] = np.tile(mf, (1, 4))
    c[:, 2, :] = np.tile(mb, (1, 4))
    t = (r % 64)[:, None]
    sc = np.arange(64)[None, :]
    c[:, 3, :] = np.tile((sc < t).astype(np.float32), (1, 8))
    c[:, 4, :] = np.tile((sc > t).astype(np.float32), (1, 8))
    c[:, 5, :] = np.tile((sc == t).astype(np.float32), (1, 8))
    c[:, 6, :] = (np.arange(512) % 64 != 0).astype(np.float32)[None, :]
    rm = np.zeros((64, 64), np.float32)
    for d in range(64):
        i = d % 32
        if i < 16:
            rm[d + 16, d] = -1.0
        else:
            rm[d - 16, d] = 1.0
    c[0:64, 7, 0:64] = rm
    c[:, 7, 64:192] = 1.0
    c[:, 8, 0:128] = 1.0 / 1024.0
    tt = np.arange(2048)
    row = (tt // 64).astype(np.float32)
    colp = (tt % 64).astype(np.float32)
    inv = (10000.0 ** (-np.arange(0, 32, 2, dtype=np.float32) / 32.0)).astype(np.float32)
    rope = np.zeros((64, 2, 2048), np.float32)
    for d in range(64):
        pos = row if d < 32 else colp
        ang = (pos * inv[(d % 32) % 16]).astype(np.float32)
        rope[d, 0] = np.cos(ang)
        rope[d, 1] = np.sin(ang)
    return c, rope


def _colT(v, n):
    return np.ascontiguousarray(np.asarray(v, np.float32).reshape(n, 128).T)


_NC_CACHE = {}


def kernel(**inp):
    f = lambda k: np.asarray(inp[k], np.float32)
    n = inp.get("_ncores", 8)
    cst, rope = _consts()
    pp = np.zeros((2, 128, NPP), np.float32)
    for l in range(2):
        pp[l, :, O_BADA:O_BADA + 48] = _colT(f("b_ada")[l], 48)
        pp[l, :, O_LN1G:O_LN1G + 8] = _colT(f("ln1_g")[l], 8)
        pp[l, :, O_LN1B:O_LN1B + 8] = _colT(f("ln1_b")[l], 8)
        pp[l, :, O_LN2G:O_LN2G + 8] = _colT(f("ln2_g")[l], 8)
        pp[l, :, O_LN2B:O_LN2B + 8] = _colT(f("ln2_b")[l], 8)
        pp[l, 0:64, O_QN] = f("q_norm")[l]
        pp[l, 0:64, O_KN] = f("k_norm")[l]
        for d in range(2):
            pp[l, :, O_MU + d * 13:O_MU + (d + 1) * 13] = _colT(f("rwkv_mu")[l, d], 13)
            pp[l, :, O_W0 + d * 4:O_W0 + (d + 1) * 4] = _colT(f("rwkv_w0")[l, d], 4)
            pp[l, :, O_A0 + d * 4:O_A0 + (d + 1) * 4] = _colT(f("rwkv_a0")[l, d], 4)
        pp[l, :, O_KK:O_KK + 4] = _colT(f("rwkv_k_k")[l], 4)
        pp[l, :, O_KA:O_KA + 4] = _colT(f("rwkv_k_a")[l], 4)
        pp[l, :, O_RK:O_RK + 4] = _colT(f("rwkv_r_k")[l].reshape(512), 4)
        pp[l, :, O_LXG:O_LXG + 4] = _colT(f("rwkv_lnx_g")[l], 4)
        pp[l, :, O_LXB:O_LXB + 4] = _colT(f("rwkv_lnx_b")[l], 4)
    wsT = np.ascontiguousarray(f("sgu_w").transpose(0, 3, 1, 2))
    sgb = np.ascontiguousarray(f("sgu_b").reshape(2, 1, 512))
    lnA = np.ascontiguousarray(np.broadcast_to(
        np.stack([f("sgu_ln_g"), f("sgu_ln_b")], 1)[:, :, None, :], (2, 2, 128, 512)))
    wa2 = np.ascontiguousarray(np.concatenate([f("rwkv_w2"), f("rwkv_a2")], axis=2))
    shared = dict(pp=pp, w_ada=f("w_ada"), w_in=f("w_in"), w_branch=f("w_branch"), w_out=f("w_out"),
                  w_up=f("w_up"), w_down=f("w_down"), wsT=wsT, sgb=sgb, lnA=lnA, wa2=wa2, g2=f("rwkv_g2"),
                  cst=cst, rope=rope)
    xp, xs = f("x_prompt"), f("x_sample")
    ck, cvv, st, c, cctx = f("cache_k"), f("cache_v"), f("state_wkv"), f("c"), f("c_ctx")
    in_maps = []
    for i in range(n):
        m = dict(shared)
        m["x"] = np.ascontiguousarray(np.concatenate([xp[4 * i:4 * i + 4].reshape(1024, D), xs[i]], 0))
        m["ck"] = np.ascontiguousarray(ck[i].reshape(2, 512, 128))
        m["cvv"] = np.ascontiguousarray(cvv[i].reshape(2, 512, 128))
        s_ = st[i].reshape(2, 2, 4, 2, 64, 64)
        m["st0"] = np.ascontiguousarray(s_.transpose(0, 1, 3, 5, 2, 4).reshape(2, 2, 128, 256))
        cvec = np.stack([cctx, c[i]], -1)
        m["cvec"] = np.ascontiguousarray(cvec.reshape(8, 128, 2).transpose(1, 0, 2))
        in_maps.append(m)
    if inp.get("_prep_only"):
        return in_maps
    if "nc" not in _NC_CACHE:
        _NC_CACHE["nc"] = build()
    kb = _NC_CACHE["nc"]
    res = run_bass_kernel_spmd(kb.nc, in_maps, core_ids=list(range(n)))
    yp = np.zeros((32, 256, D), np.float32)
    ys = np.zeros((8, 2048, D), np.float32)
    nk = np.zeros((32, 2, 256, 2, 64), np.float32)
    nv = np.zeros((32, 2, 256, 2, 64), np.float32)
    ns = np.zeros((32, 2, 2, 8, 64, 64), np.float32)
    for i in range(n):
        r = res.results[i]
        yp[4 * i:4 * i + 4] = r["y"][:1024].reshape(4, 256, D)
        ys[i] = r["y"][1024:]
        nk[4 * i:4 * i + 4] = r["nk"].reshape(2, 4, 256, 2, 64).transpose(1, 0, 2, 3, 4)
        nv[4 * i:4 * i + 4] = r["nv"].reshape(2, 4, 256, 2, 64).transpose(1, 0, 2, 3, 4)
        s_ = r["ns"].reshape(4, 2, 2, 2, 64, 4, 64)
        ns[4 * i:4 * i + 4] = s_.transpose(0, 1, 2, 5, 3, 6, 4).reshape(4, 2, 2, 8, 64, 64)
    return (yp, ys, nk, nv, ns)
```
